# Optimizing a Trainium2 kernel written in Bass

```python
import math
import jax, jax.numpy as jnp
from jax import lax
import numpy as np

D_MODEL = 2048
BATCH = 8
SEQ = 2048
DEPTH = 1
DEC_BATCH = 16
DEC_SEQ = 64
PAST_LEN = 1024

CHUNK = 64
HEAD_DIM = 64
N_Q = 3 * D_MODEL // (8 * HEAD_DIM)
N_KV = 4
GRP = N_Q // N_KV
WINDOW = 128
WIN_CHUNKS = WINDOW // CHUNK
NUM_BUCKETS = 32
MAX_DISTANCE = 128
SSM_W = 3 * D_MODEL // 8
SSM_GROUP_CH = 16
SSM_GROUPS = SSM_W // SSM_GROUP_CH
SSM_STATE = 64
N_MEM = 256
MEM_HEADS = 4
MEM_W = D_MODEL // 4
MEM_HEAD_DIM = MEM_W // MEM_HEADS
N_BRANCH = 3
Q_W = N_Q * HEAD_DIM
KV_W = N_KV * HEAD_DIM
MIX_W = Q_W + SSM_W + MEM_W
IN_COLS = Q_W + 2 * KV_W + SSM_W + MEM_W + N_BRANCH * D_MODEL
IN_SPLITS = [Q_W, Q_W + KV_W, Q_W + 2 * KV_W, Q_W + 2 * KV_W + SSM_W, Q_W + 2 * KV_W + SSM_W + MEM_W]
D_FF = ((8 * D_MODEL // 3 + 255) // 256) * 256
CONV_W = 3
EPS = 1e-6
NEG_INF = -1e30

kernel_name = 'hybrid_swa_s5_memxattn_convffn_step'


def rms_norm(x, g):
    xf = x.astype(jnp.float32)
    y = xf * lax.rsqrt(jnp.mean(xf * xf, axis=-1, keepdims=True) + EPS)
    return (y * g.astype(jnp.float32)).astype(x.dtype)


def t5_bucket(rel):
    half = NUM_BUCKETS // 2
    max_exact = half // 2
    n = jnp.abs(rel)
    large = max_exact + (jnp.log(jnp.maximum(n, 1).astype(jnp.float32) / max_exact)
                         / math.log(MAX_DISTANCE / max_exact) * (half - max_exact)).astype(jnp.int32)
    large = jnp.minimum(large, half - 1)
    return jnp.where(rel > 0, half, 0) + jnp.where(n < max_exact, n, large)


def rel_bias(n_q, n_k, k_offset, table):
    rel = jnp.arange(n_k)[None, :] - k_offset - jnp.arange(n_q)[:, None]
    return jnp.transpose(table[t5_bucket(rel)], (2, 0, 1)).astype(jnp.float32)


def band_attention(q, k, v, bias, mask, sinks):
    b, nb, lq, _, hd = q.shape
    lk = k.shape[2]
    qg = q.reshape(b, nb, lq, N_KV, GRP, hd)
    s = jnp.einsum('bnqhgd,bnkhd->bnhgqk', qg, k).astype(jnp.float32) * (hd ** -0.5)
    s = s + bias.reshape(N_KV, GRP, lq, lk)
    s = jnp.where(mask[None, :, None, None, None, :], s, NEG_INF)
    sink = sinks.astype(jnp.float32).reshape(N_KV, GRP, 1, 1)
    m = jnp.maximum(jnp.max(s, axis=-1, keepdims=True), sink)
    p = jnp.exp(s - m)
    p = p / (jnp.sum(p, axis=-1, keepdims=True) + jnp.exp(sink - m))
    o = jnp.einsum('bnhgqk,bnkhd->bnqhgd', p.astype(v.dtype), v)
    return o.reshape(b, nb * lq, N_Q * hd)


def local_attn_prompt(q, k, v, table, sinks):
    b, t = q.shape[:2]
    nc = t // CHUNK
    pad = ((0, 0), (WINDOW, 0), (0, 0), (0, 0))
    kp = jnp.pad(k, pad).reshape(b, nc + WIN_CHUNKS, CHUNK, N_KV, HEAD_DIM)
    vp = jnp.pad(v, pad).reshape(b, nc + WIN_CHUNKS, CHUNK, N_KV, HEAD_DIM)
    kb = jnp.concatenate([kp[:, i:i + nc] for i in range(WIN_CHUNKS + 1)], axis=2)
    vb = jnp.concatenate([vp[:, i:i + nc] for i in range(WIN_CHUNKS + 1)], axis=2)
    lk = WINDOW + CHUNK
    key_pos = jnp.arange(nc)[:, None] * CHUNK - WINDOW + jnp.arange(lk)[None, :]
    mask = key_pos >= 0
    bias = rel_bias(CHUNK, lk, WINDOW, table)
    o = band_attention(q.reshape(b, nc, CHUNK, N_Q, HEAD_DIM), kb, vb, bias, mask, sinks)
    return o, k[:, -WINDOW:], v[:, -WINDOW:]


def local_attn_sample(q, k, v, k_cache, v_cache, table, sinks):
    lc = k_cache.shape[1]
    tn = q.shape[1]
    kf = jnp.concatenate([k_cache.astype(k.dtype), k], axis=1)
    vf = jnp.concatenate([v_cache.astype(v.dtype), v], axis=1)
    bias = rel_bias(tn, lc + tn, lc, table)
    mask = jnp.ones((1, lc + tn), dtype=bool)
    o = band_attention(q[:, None], kf[:, None], vf[:, None], bias, mask, sinks)
    return o, kf[:, -lc:], vf[:, -lc:]


def _lin_combine(e1, e2):
    a1, b1 = e1
    a2, b2 = e2
    return a1 * a2, a2 * b1 + b2


def s5_branch(u, s0, lp):
    f32 = jnp.float32
    b, t, _ = u.shape
    lam = lax.complex(lp['ssm_a_re'].astype(f32), lp['ssm_a_im'].astype(f32))
    dt = jnp.exp(lp['ssm_log_dt'].astype(f32))[:, None]
    lam_dt = lam * dt
    a_bar = jnp.exp(lam_dt)
    b_mat = lax.complex(lp['ssm_b_re'].astype(f32), lp['ssm_b_im'].astype(f32))
    b_bar = ((a_bar - 1.0) / lam)[..., None] * b_mat
    c_mat = lax.complex(lp['ssm_c_re'].astype(f32), lp['ssm_c_im'].astype(f32))
    ug = u.astype(f32).reshape(b, t, SSM_GROUPS, SSM_GROUP_CH)
    bu = jnp.einsum('btgc,gpc->btgp', ug.astype(jnp.complex64), b_bar)
    a_seq = jnp.broadcast_to(a_bar, bu.shape)
    _, s = lax.associative_scan(_lin_combine, (a_seq, bu), axis=1)
    steps = jnp.arange(1, t + 1, dtype=f32)
    s = s + jnp.exp(steps[:, None, None] * lam_dt)[None] * s0[:, None]
    y = jnp.einsum('btgp,gcp->btgc', s, c_mat).real
    y = y + lp['ssm_d'].astype(f32).reshape(SSM_GROUPS, SSM_GROUP_CH) * ug
    y = jax.nn.gelu(y.reshape(b, t, SSM_W)).astype(u.dtype)
    out = y * jax.nn.sigmoid(y @ lp['w_glu'])
    return out, s[:, -1]


def memory_kv(mem, g_mem, w_mem_kv):
    b = mem.shape[0]
    kv = rms_norm(mem, g_mem) @ w_mem_kv
    mk, mv = jnp.split(kv, 2, axis=-1)
    return (mk.reshape(b, N_MEM, MEM_HEADS, MEM_HEAD_DIM),
            mv.reshape(b, N_MEM, MEM_HEADS, MEM_HEAD_DIM))


def memory_attention(qm, mem_k, mem_v):
    b, t, _ = qm.shape
    q = qm.reshape(b, t, MEM_HEADS, MEM_HEAD_DIM)
    s = jnp.einsum('bthd,bmhd->bhtm', q, mem_k.astype(q.dtype)).astype(jnp.float32) * (MEM_HEAD_DIM ** -0.5)
    p = jax.nn.softmax(s, axis=-1).astype(q.dtype)
    o = jnp.einsum('bhtm,bmhd->bthd', p, mem_v.astype(q.dtype))
    return o.reshape(b, t, MEM_W)


def token_mixers(h, attn_past, s0, mem_k, mem_v, table, lp):
    b, t, _ = h.shape
    proj = h @ lp['w_in']
    q, k, v, u, qm, gates = jnp.split(proj, IN_SPLITS, axis=-1)
    q = q.reshape(b, t, N_Q, HEAD_DIM)
    k = k.reshape(b, t, N_KV, HEAD_DIM)
    v = v.reshape(b, t, N_KV, HEAD_DIM)
    if attn_past is None:
        o_a, k_new, v_new = local_attn_prompt(q, k, v, table, lp['attn_sinks'])
    else:
        o_a, k_new, v_new = local_attn_sample(q, k, v, attn_past[0], attn_past[1], table, lp['attn_sinks'])
    o_s, s_new = s5_branch(u, s0, lp)
    o_m = memory_attention(qm, mem_k, mem_v)
    w_out = lp['w_out']
    g = jax.nn.sigmoid(gates.reshape(b, t, N_BRANCH, D_MODEL))
    merged = (g[:, :, 0] * (o_a @ w_out[:Q_W])
              + g[:, :, 1] * (o_s @ w_out[Q_W:Q_W + SSM_W])
              + g[:, :, 2] * (o_m @ w_out[Q_W + SSM_W:]))
    return merged, k_new, v_new, s_new


def conv_ffn(h, conv_prev, lp):
    t = h.shape[1]
    up = h @ lp['w_up']
    a, bv = jnp.split(up, 2, axis=-1)
    a_ext = jnp.concatenate([conv_prev.astype(a.dtype), a], axis=1)
    conv_w = lp['conv_w']
    a_conv = sum(a_ext[:, i:i + t] * conv_w[i] for i in range(CONV_W)) + lp['conv_b']
    out = (jax.nn.gelu(a_conv) * bv) @ lp['w_down']
    return out, a_ext[:, -(CONV_W - 1):]


def layer(x, attn_past, s0, conv_prev, mem_k, mem_v, table, lp):
    h = rms_norm(x, lp['norm_pre_mix'])
    mix, k_new, v_new, s_new = token_mixers(h, attn_past, s0, mem_k, mem_v, table, lp)
    x = x + rms_norm(mix, lp['norm_post_mix'])
    h2 = rms_norm(x, lp['norm_pre_ffn'])
    f, conv_new = conv_ffn(h2, conv_prev, lp)
    x = x + rms_norm(f, lp['norm_post_ffn'])
    return x, k_new, v_new, s_new, conv_new


def setup_inputs(seed: int = 0) -> dict:
    key = jax.random.key(seed)
    ks = iter(jax.random.split(key, 48))
    f32 = jnp.float32

    def nrm(shape, scale):
        return scale * jax.random.normal(next(ks), shape, f32)

    win_cache = min(WINDOW, PAST_LEN)
    a_im = math.pi * jnp.broadcast_to(jnp.arange(SSM_STATE, dtype=f32), (DEPTH, SSM_GROUPS, SSM_STATE))
    return {
        'x_prompt': nrm((BATCH, SEQ, D_MODEL), 1.0),
        'x_sample': nrm((DEC_BATCH, DEC_SEQ, D_MODEL), 1.0),
        'cache_attn_k': nrm((DEPTH, DEC_BATCH, win_cache, N_KV, HEAD_DIM), 1.0),
        'cache_attn_v': nrm((DEPTH, DEC_BATCH, win_cache, N_KV, HEAD_DIM), 1.0),
        'cache_mem_k': nrm((DEPTH, DEC_BATCH, N_MEM, MEM_HEADS, MEM_HEAD_DIM), 1.0),
        'cache_mem_v': nrm((DEPTH, DEC_BATCH, N_MEM, MEM_HEADS, MEM_HEAD_DIM), 1.0),
        'state_ssm_re': nrm((DEPTH, DEC_BATCH, SSM_GROUPS, SSM_STATE), 0.1),
        'state_ssm_im': nrm((DEPTH, DEC_BATCH, SSM_GROUPS, SSM_STATE), 0.1),
        'state_conv': nrm((DEPTH, DEC_BATCH, CONV_W - 1, D_FF), 1.0),
        'mem_prompt': nrm((BATCH, N_MEM, D_MODEL), 1.0),
        'rel_bias_table': nrm((NUM_BUCKETS, N_Q), 0.5),
        'norm_pre_mix': 1.0 + nrm((DEPTH, D_MODEL), 0.02),
        'norm_post_mix': 1.0 + nrm((DEPTH, D_MODEL), 0.02),
        'norm_pre_ffn': 1.0 + nrm((DEPTH, D_MODEL), 0.02),
        'norm_post_ffn': 1.0 + nrm((DEPTH, D_MODEL), 0.02),
        'norm_mem': 1.0 + nrm((DEPTH, D_MODEL), 0.02),
        'w_in': nrm((DEPTH, D_MODEL, IN_COLS), D_MODEL ** -0.5),
        'attn_sinks': nrm((DEPTH, N_Q), 0.5),
        'ssm_a_re': -0.5 + nrm((DEPTH, SSM_GROUPS, SSM_STATE), 0.01),
        'ssm_a_im': a_im + nrm((DEPTH, SSM_GROUPS, SSM_STATE), 0.01),
        'ssm_log_dt': jax.random.uniform(next(ks), (DEPTH, SSM_GROUPS), f32, math.log(1e-3), math.log(1e-1)),
        'ssm_b_re': nrm((DEPTH, SSM_GROUPS, SSM_STATE, SSM_GROUP_CH), (2 * SSM_GROUP_CH) ** -0.5),
        'ssm_b_im': nrm((DEPTH, SSM_GROUPS, SSM_STATE, SSM_GROUP_CH), (2 * SSM_GROUP_CH) ** -0.5),
        'ssm_c_re': nrm((DEPTH, SSM_GROUPS, SSM_GROUP_CH, SSM_STATE), (2 * SSM_STATE) ** -0.5),
        'ssm_c_im': nrm((DEPTH, SSM_GROUPS, SSM_GROUP_CH, SSM_STATE), (2 * SSM_STATE) ** -0.5),
        'ssm_d': nrm((DEPTH, SSM_W), 1.0),
        'w_glu': nrm((DEPTH, SSM_W, SSM_W), SSM_W ** -0.5),
        'w_mem_kv': nrm((DEPTH, D_MODEL, 2 * MEM_W), D_MODEL ** -0.5),
        'w_out': nrm((DEPTH, MIX_W, D_MODEL), MIX_W ** -0.5),
        'w_up': nrm((DEPTH, D_MODEL, 2 * D_FF), D_MODEL ** -0.5),
        'conv_w': nrm((DEPTH, CONV_W, D_FF), CONV_W ** -0.5),
        'conv_b': nrm((DEPTH, D_FF), 0.01),
        'w_down': nrm((DEPTH, D_FF, D_MODEL), D_FF ** -0.5),
    }


def reference(x_prompt, x_sample, cache_attn_k, cache_attn_v, cache_mem_k, cache_mem_v,
              state_ssm_re, state_ssm_im, state_conv, mem_prompt, rel_bias_table,
              norm_pre_mix, norm_post_mix, norm_pre_ffn, norm_post_ffn, norm_mem,
              w_in, attn_sinks, ssm_a_re, ssm_a_im, ssm_log_dt, ssm_b_re, ssm_b_im,
              ssm_c_re, ssm_c_im, ssm_d, w_glu, w_mem_kv, w_out, w_up, conv_w, conv_b, w_down):
    f32 = jnp.float32
    xp, xs = x_prompt, x_sample
    bp = x_prompt.shape[0]
    pk, pv, psr, psi, pc, pmk, pmv = [], [], [], [], [], [], []
    sk, sv, ssr, ssi, sc = [], [], [], [], []
    for l in range(DEPTH):
        lp = dict(norm_pre_mix=norm_pre_mix[l], norm_post_mix=norm_post_mix[l],
                  norm_pre_ffn=norm_pre_ffn[l], norm_post_ffn=norm_post_ffn[l],
                  w_in=w_in[l], attn_sinks=attn_sinks[l],
                  ssm_a_re=ssm_a_re[l], ssm_a_im=ssm_a_im[l], ssm_log_dt=ssm_log_dt[l],
                  ssm_b_re=ssm_b_re[l], ssm_b_im=ssm_b_im[l], ssm_c_re=ssm_c_re[l], ssm_c_im=ssm_c_im[l],
                  ssm_d=ssm_d[l], w_glu=w_glu[l], w_out=w_out[l],
                  w_up=w_up[l], conv_w=conv_w[l], conv_b=conv_b[l], w_down=w_down[l])
        mk_p, mv_p = memory_kv(mem_prompt, norm_mem[l], w_mem_kv[l])
        s0_p = jnp.zeros((bp, SSM_GROUPS, SSM_STATE), jnp.complex64)
        conv0_p = jnp.zeros((bp, CONV_W - 1, D_FF), xp.dtype)
        xp, k_p, v_p, s_p, c_p = layer(xp, None, s0_p, conv0_p, mk_p, mv_p, rel_bias_table, lp)
        s0_s = lax.complex(state_ssm_re[l].astype(f32), state_ssm_im[l].astype(f32))
        xs, k_s, v_s, s_s, c_s = layer(xs, (cache_attn_k[l], cache_attn_v[l]), s0_s, state_conv[l],
                                       cache_mem_k[l], cache_mem_v[l], rel_bias_table, lp)
        pk.append(k_p); pv.append(v_p); psr.append(s_p.real); psi.append(s_p.imag); pc.append(c_p)
        pmk.append(mk_p); pmv.append(mv_p)
        sk.append(k_s); sv.append(v_s); ssr.append(s_s.real); ssi.append(s_s.imag); sc.append(c_s)
    return (xp, xs,
            jnp.stack(pk), jnp.stack(pv), jnp.stack(psr), jnp.stack(psi), jnp.stack(pc),
            jnp.stack(pmk), jnp.stack(pmv),
            jnp.stack(sk), jnp.stack(sv), jnp.stack(ssr), jnp.stack(ssi), jnp.stack(sc))
```

```python
import math
from contextlib import ExitStack
import numpy as np
import concourse.bass as bass
import concourse.mybir as mybir
from concourse.bass_utils import run_bass_kernel_spmd

F32 = mybir.dt.float32
BF16 = mybir.dt.bfloat16
I32 = mybir.dt.int32
AF = mybir.ActivationFunctionType
ALU = mybir.AluOpType

D = 2048
KT = 16
NQ, NKV, HD = 12, 4, 64
DFF = 5632
FT = 44
INC = 8704
L = 64
NGP = 24
QPERM = [0, 3, 1, 4, 2, 5, 6, 9, 7, 10, 8, 11]
EPS = 1e-6
CW = 256


import types


def _snap(fn):
    if getattr(fn, "__closure__", None) is None:
        return fn
    cells = []
    for c in fn.__closure__:
        try:
            cells.append(types.CellType(c.cell_contents))
        except ValueError:
            cells.append(c)
    g = types.FunctionType(fn.__code__, fn.__globals__, fn.__name__, fn.__defaults__, tuple(cells))
    g.__kwdefaults__ = fn.__kwdefaults__
    return g


class Buf:
    def __init__(self, name):
        self.name = name
        self.lastw = None
        self.readers = []
        self.sem = None
        self.cnt = 0
        self.pw = []
        self.nobar = name in ("win", "wglu", "wmkv", "wout", "wup", "wdn", "outs")
        self.excl = name.startswith("ps")


class Prog:
    def __init__(self, nc, es):
        self.nc = nc
        self.es = es
        self.ops = {e: [] for e in ("pe", "act", "dve", "pool", "sp")}
        self.count = {e: 0 for e in ("pe", "act", "dve", "pool", "sp")}
        self.esem = {e: es.enter_context(nc.semaphore("s_" + e)) for e in ("pe", "act", "dve", "pool", "sp")}
        self.dsems = []
        self.bufs = {}
        self.stopped = False

    def buf(self, name):
        if name not in self.bufs:
            self.bufs[name] = Buf(name)
        return self.bufs[name]

    def _deps(self, eng, reads, writes):
        deps = set()
        for b in reads:
            if b.lastw is not None:
                deps.add(b.lastw)
            deps.update(b.pw)
            if b.excl:
                for r in b.readers:
                    if not (r[0] == "eng" and r[1] == eng):
                        deps.add(r)
        for b in writes:
            if b.lastw is not None:
                deps.add(b.lastw)
            deps.update(b.pw)
            for r in b.readers:
                deps.add(r)
        if eng == "pe":
            deps = {d for d in deps if not (d[0] == "eng" and d[1] == "pe")}
        return deps

    def _commit(self, me, reads, writes, par=False):
        for b in reads:
            b.readers.append(me)
        for b in writes:
            if par:
                b.pw.append(me)
            else:
                b.lastw = me
                b.pw = []
            b.readers = []

    def op(self, eng, fn, reads=(), writes=()):
        if self.stopped:
            return
        deps = self._deps(eng, reads, writes)
        self.count[eng] += 1
        me = ("eng", eng, self.count[eng])
        self.ops[eng].append((deps, _snap(fn), None))
        self._commit(me, reads, writes)

    def dma(self, out, in_, reads, writes, sembuf, q="sp", par=False):
        if self.stopped:
            return
        deps = self._deps(q, reads, writes)
        if par:
            drop = set()
            for b in writes:
                drop.update(b.pw)
                if b.lastw is not None and b.lastw[0] == "dma" and b.lastw[1] is sembuf and not b.readers:
                    drop.add(b.lastw)
            deps = {d for d in deps if d not in drop}
        if sembuf.sem is None:
            sembuf.sem = self.es.enter_context(self.nc.semaphore("d_" + sembuf.name))
            self.dsems.append(sembuf)
        sembuf.cnt += 16
        me = ("dma", sembuf, sembuf.cnt)
        self.ops[q].append((deps, (lambda e, o=out, i=in_: e.dma_start(out=o, in_=i)), sembuf))
        self._commit(me, reads, writes, par)

    def barrier(self):
        if self.stopped:
            return
        nop = lambda en: en.nop()
        deps = {("eng", e, self.count[e]) for e in ("pe", "act", "dve", "pool") if self.count[e]}
        deps |= {("dma", b, b.cnt) for b in self.dsems if not b.nobar}
        self.count["sp"] += 1
        self.ops["sp"].append((deps, nop, None))
        me = ("eng", "sp", self.count["sp"])
        for e in ("pe", "act", "dve", "pool"):
            self.count[e] += 1
            self.ops[e].append(({me}, nop, None))

    def emit(self):
        nc = self.nc
        with nc.Block() as block:
            def run(engname, e):
                known = {}
                for deps, fn, sembuf in self.ops[engname]:
                    need = {}
                    for d in deps:
                        key = (d[0], d[1] if d[0] == "eng" else id(d[1]))
                        sem = self.esem[d[1]] if d[0] == "eng" else d[1].sem
                        if known.get(key, 0) >= d[2]:
                            continue
                        if key not in need or need[key][1] < d[2]:
                            need[key] = (sem, d[2])
                    for key, (sem, v) in need.items():
                        e.wait_ge(sem, v)
                        known[key] = v
                    ins = fn(e)
                    if sembuf is not None:
                        ins.then_inc(sembuf.sem, 16)
                    else:
                        ins.then_inc(self.esem[engname], 1)
                if engname == "sp":
                    for en in ("pe", "act", "dve", "pool"):
                        if self.count[en]:
                            e.wait_ge(self.esem[en], self.count[en])
                    for b in self.dsems:
                        e.wait_ge(b.sem, b.cnt)

            @block.tensor
            def _(e):
                run("pe", e)

            @block.scalar
            def _(e):
                run("act", e)

            @block.vector
            def _(e):
                run("dve", e)

            @block.gpsimd
            def _(e):
                run("pool", e)

            @block.sync
            def _(e):
                run("sp", e)


STOP = None


def build_program():
    nc = bass.Bass("TRN2", target_bir_lowering=False)
    es = ExitStack()
    P = Prog(nc, es)

    def ck(name):
        if STOP == name:
            P.stopped = True

    def din(name, shape, dt=F32):
        return nc.dram_tensor(name, list(shape), dt, kind="ExternalInput").ap()

    def dout(name, shape):
        return nc.dram_tensor(name, list(shape), F32, kind="ExternalOutput").ap()

    def dscr(name, shape, dt):
        return nc.dram_tensor(name, list(shape), dt).ap()

    x_p = din("x_p", [2048, D]); x_s = din("x_s", [128, D])
    c_ak = din("c_ak", [2, 128, 256]); c_av = din("c_av", [2, 128, 256])
    c_mk = din("c_mk", [2, 256, 512]); c_mv = din("c_mv", [2, 256, 512])
    st_re = din("st_re", [2, 48, 64]); st_im = din("st_im", [2, 48, 64])
    st_cv = din("st_cv", [2, 2, DFF])
    memp = din("memp", [256, D])
    rbt = din("rbt", [32, 12]); oh = din("oh", [32, 64 * 3 * 64])
    g_pm = din("g_pm", [1, D]); g_qm = din("g_qm", [1, D]); g_pf = din("g_pf", [1, D]); g_qf = din("g_qf", [1, D])
    g_mem = din("g_mem", [1, D])
    w_in = din("w_in", [D, INC]); sinks = din("sinks", [1, 12])
    a_re = din("a_re", [48, 64]); a_im = din("a_im", [48, 64]); log_dt = din("log_dt", [1, 48])
    b_re = din("b_re", [48, 64, 16]); b_im = din("b_im", [48, 64, 16])
    cc_re = din("cc_re", [48, 16, 64]); cc_im = din("cc_im", [48, 16, 64])
    ssm_d = din("ssm_d", [6, 128]); w_glu = din("w_glu", [768, 768]); w_mkv = din("w_mkv", [D, 1024])
    w_out = din("w_out", [D, D]); w_up = din("w_up", [D, 2 * DFF]); conv_w = din("conv_w", [3, DFF])
    conv_b = din("conv_b", [1, DFF]); w_down = din("w_down", [DFF, D])
    y_p = dout("y_p", [2048, D]); y_s = dout("y_s", [128, D])
    o_pk = dout("o_pk", [128, 256]); o_pv = dout("o_pv", [128, 256])
    o_pre = dout("o_pre", [48, 64]); o_pim = dout("o_pim", [48, 64]); o_pcv = dout("o_pcv", [2, DFF])
    o_pmk = dout("o_pmk", [256, 512]); o_pmv = dout("o_pmv", [256, 512])
    o_sk = dout("o_sk", [2, 128, 256]); o_sv = dout("o_sv", [2, 128, 256])
    o_sre = dout("o_sre", [2, 48, 64]); o_sim = dout("o_sim", [2, 48, 64]); o_scv = dout("o_scv", [2, 2, DFF])
    win_b = dscr("win_b", [D, INC], BF16); wglu_b = dscr("wglu_b", [768, 768], BF16)
    wmkv_b = dscr("wmkv_b", [D, 1024], BF16); wout_b = dscr("wout_b", [D, D], BF16)
    wup_b = dscr("wup_b", [D, 2 * DFF], BF16); wdn_b = dscr("wdn_b", [DFF, D], BF16)
    mixs = dscr("mixs", [512, D], F32)
    rot_tab = dscr("rot_tab", [NGP, 2, 128, 512], F32)
    sgscr = dscr("sgscr", [D // CW, 4, 128, 3, CW], F32)
    B_win, B_wglu, B_wmkv, B_wout, B_wup, B_wdn, B_mixs = [P.buf(n) for n in
        ("win", "wglu", "wmkv", "wout", "wup", "wdn", "mixs")]
    origmap = {"win": w_in, "wglu": w_glu, "wmkv": w_mkv, "wout": w_out, "wup": w_up, "wdn": w_down}
    B_yp, B_ys = P.buf("yp"), P.buf("ys")
    B_rot = P.buf("rot"); B_sgscr = P.buf("sgscr")
    B_out = P.buf("outs")

    def sb(name, shape, dt=F32):
        return es.enter_context(nc.sbuf_tensor(name, list(shape), dt))

    def ps(name, shape, dt=F32):
        return es.enter_context(nc.psum_tensor(name, list(shape), dt))

    psf = [(ps(f"psf{i}", [128, 512]), P.buf(f"psf{i}")) for i in range(5)]
    psy = (ps("psy", [128, 512]), P.buf("psy"))
    psb = [(ps(f"psb{i}", [128, 1024], BF16), P.buf(f"psb{i}")) for i in range(2)]
    rr = {"f": 0, "b": 0, "w": 0, "ev": 0}

    def nps():
        rr["f"] = (rr["f"] + 1) % 5
        return psf[rr["f"]]

    def npsb():
        rr["b"] = (rr["b"] + 1) % 2
        return psb[rr["b"]]

    NW = 4
    wbufs = [(sb(f"wb{i}", [128, KT, CW], BF16), P.buf(f"wb{i}")) for i in range(NW)]

    pass0 = {"on": True}

    def wload(src, bsrc, k0, nk, c0, ncol):
        rr["w"] = (rr["w"] + 1) % NW
        t, b = wbufs[rr["w"]]
        v = src.rearrange("(kt p) n -> p kt n", p=128)
        if not pass0["on"]:
            P.dma(t[:, 0:nk, 0:ncol], v[:, k0:k0 + nk, c0:c0 + ncol], [bsrc], [b], b)
            return t, b
        orig = origmap[bsrc.name]
        ov = orig.rearrange("(kt p) n -> p kt n", p=128)
        first = [True]

        bsw = P.buf(f"wbsw{rr['w']}"); bst = P.buf(f"wbst{rr['w']}")

        def ld(dst, srcap):
            P.dma(dst, srcap, [], [b], bsw, q="pool", par=not first[0])
            first[0] = False
        if bsrc.name == "win" and c0 < 768:
            for j in range(ncol // 64):
                h = QPERM[(c0 // 64) + j]
                ld(t[:, 0:nk, j * 64:(j + 1) * 64], ov[:, k0:k0 + nk, h * 64:(h + 1) * 64])
        elif bsrc.name == "wout":
            for k in range(k0, k0 + nk):
                if k < 6:
                    for hf in range(2):
                        h = QPERM[2 * k + hf]
                        ld(t[hf * 64:(hf + 1) * 64, k - k0, 0:ncol], orig[h * 64:(h + 1) * 64, c0:c0 + ncol])
            ka = max(k0, 6)
            if ka < k0 + nk:
                ld(t[:, ka - k0:nk, 0:ncol], ov[:, ka:k0 + nk, c0:c0 + ncol])
        else:
            ld(t[:, 0:nk, 0:ncol], ov[:, k0:k0 + nk, c0:c0 + ncol])
        if bsrc.name != "wmkv":
            P.dma(v[:, k0:k0 + nk, c0:c0 + ncol], t[:, 0:nk, 0:ncol], [b], [bsrc], bst, par=True)
        return t, b

    def evac(out, in_, reads, writes, scale=None):
        rr["ev"] ^= 1
        if rr["ev"]:
            if scale is None:
                P.op("act", lambda e: e.activation(out=out, in_=in_, func=AF.Copy), reads, writes)
            else:
                P.op("act", lambda e: e.activation(out=out, in_=in_, func=AF.Copy, scale=scale), reads, writes)
        else:
            if scale is None:
                P.op("dve", lambda e: e.tensor_copy(out=out, in_=in_), reads, writes)
            else:
                P.op("dve", lambda e: e.tensor_scalar(out=out, in0=in_, scalar1=scale, scalar2=None, op0=ALU.mult), reads, writes)

    def mm(out, pairs, reads, writes):
        def fn(e, out=out, pairs=pairs):
            n = len(pairs)
            ins = None
            for i, (l, r) in enumerate(pairs):
                ins = e.matmul(out, lhsT=l, rhs=r, start=(i == 0), stop=(i == n - 1))
            return ins
        P.op("pe", fn, reads, writes)

    ident = sb("ident", [128, 128]); identb = sb("identb", [128, 128], BF16)
    B_const = P.buf("const")
    ones_b = sb("ones_b", [128, 128], BF16)
    gT = sb("gT", [128, 3, KT])
    gbc = sb("gbc", [128, 2, D], BF16)
    cw = sb("cw", [128, 4, FT])
    dT = sb("dT", [128, 6])
    Ddiag = sb("Ddiag", [128, 6, 128], BF16)
    epsb = sb("epsb", [128, 1])
    biasT = sb("biasT", [64, 3, 64, 12])
    esk = sb("esk", [128, 12])
    rS = sb("rS", [128, NGP])
    tabs = [sb(f"tab{i}", [128, 2, 512]) for i in range(2)]; B_tabs = [P.buf("tab0"), P.buf("tab1")]
    Bre = sb("Bre", [128, NGP, 128], BF16); Bim = sb("Bim", [128, NGP, 128], BF16)
    Cre = sb("Cre", [128, NGP, 128], BF16); Cim = sb("Cim", [128, NGP, 128], BF16)
    sst = sb("sst", [128, 2, NGP])
    cvst = sb("cvst", [128, FT, 2])
    kTt = sb("kTt", [128, 2, 128 + 512], BF16)
    v64 = sb("v64", [64, 10, NKV, 128], BF16)
    B_kT, B_v64, B_sst, B_cvst = P.buf("kT"), P.buf("v64"), P.buf("sst"), P.buf("cvst")
    xt = sb("xt", [128, D]); mt = sb("mt", [128, D]); B_xt, B_mt = P.buf("xt"), P.buf("mt")
    hb = sb("hb", [128, D], BF16); B_hb = P.buf("hb")
    stat = sb("stat", [128, 8]); B_stat = P.buf("stat")
    stg = sb("stg", [128, 512]); B_stg = P.buf("stg")
    ARENA = 82 * 1024
    arena = sb("arena", [128, ARENA // 4])
    B_arena = P.buf("arena")
    off = {"v": 0}

    def carve(shape, dt):
        n = 1
        for s in shape[1:]:
            n *= s
        nbytes = n * (2 if dt == BF16 else 4)
        nbytes = (nbytes + 31) // 32 * 32
        o = off["v"]
        assert o + nbytes <= ARENA, (o, nbytes)
        off["v"] = o + nbytes
        flat = arena[0:shape[0], o // 4:(o + nbytes) // 4]
        if dt != F32:
            flat = flat.bitcast(dt)
        flat = flat[:, 0:n]
        if len(shape) == 2:
            return flat
        names = " ".join(f"d{i}" for i in range(1, len(shape)))
        return flat.rearrange(f"p ({names}) -> p {names}", **{f"d{i}": shape[i] for i in range(2, len(shape))})

    def barrier():
        P.barrier()
        off["v"] = 0

    def rewind(mark):
        P.barrier()
        off["v"] = mark

    def cast(dst, src, bdst, rows=None):
        P.dma(dst, src, [], [bdst], bdst, q="pool")

    def casts_a():
        cast(wmkv_b[:, :], w_mkv[:, :], B_wmkv)
        for s_ in range(12):
            h = QPERM[s_]
            cast(win_b[:, s_ * 64:(s_ + 1) * 64], w_in[:, h * 64:(h + 1) * 64], B_win)
        for r in range(4):
            cast(win_b[r * 512:(r + 1) * 512, 768:INC], w_in[r * 512:(r + 1) * 512, 768:INC], B_win)

    def casts_b():
        cast(wglu_b[:, :], w_glu[:, :], B_wglu)
        for s_ in range(12):
            h = QPERM[s_]
            cast(wout_b[s_ * 64:(s_ + 1) * 64, :], w_out[h * 64:(h + 1) * 64, :], B_wout)
        cast(wout_b[768:D, :], w_out[768:D, :], B_wout)
        for r in range(4):
            cast(wup_b[r * 512:(r + 1) * 512, :], w_up[r * 512:(r + 1) * 512, :], B_wup)
        for r in range(4):
            cast(wdn_b[r * 1408:(r + 1) * 1408, :], w_down[r * 1408:(r + 1) * 1408, :], B_wdn)

    ck("cast")
    P.op("dve", lambda e: e.memset(epsb[:], EPS), [], [B_const])
    idc = carve([128, 128], F32); idr = carve([128, 128], F32); idx = None
    nat = sb("nat_p", [128, 128]); B_nat = P.buf("nat")
    P.op("pool", lambda e: e.iota(idc[:], pattern=[[1, 128]], base=0, channel_multiplier=0,
                                  allow_small_or_imprecise_dtypes=True), [], [B_const])
    P.op("pool", lambda e: e.iota(idr[:], pattern=[[0, 128]], base=0, channel_multiplier=1,
                                  allow_small_or_imprecise_dtypes=True), [], [B_const])
    for i, g in enumerate((g_qm, g_qf)):
        P.dma(gbc[:, i, :], g.broadcast_to([128, D]), [], [B_const], B_const, q="pool")
    P.op("dve", lambda e: e.tensor_tensor(out=ident[:], in0=idc[:], in1=idr[:], op=ALU.is_equal), [B_const], [B_const])
    P.op("dve", lambda e: e.tensor_copy(out=identb[:], in_=ident[:]), [B_const], [B_const])
    P.op("dve", lambda e: e.memset(ones_b[:], 1.0), [], [B_const])
    P.op("dve", lambda e: e.memset(sst[:], 0.0), [], [B_sst])
    P.op("dve", lambda e: e.memset(cvst[:], 0.0), [], [B_cvst])


    def load_T(dst, src_rows_ap, nrows, ncols_in=128):
        P.dma(nat[0:nrows, 0:128], src_rows_ap, [], [B_nat], B_nat)
        pt, pb = nps()
        P.op("pe", lambda e: e.transpose(pt[:, 0:nrows], nat[0:nrows, 0:128], ident[0:nrows, 0:nrows]),
             [B_nat, B_const], [pb])
        P.op("dve", lambda e: e.tensor_copy(out=dst, in_=pt[:, 0:nrows]), [pb], [B_const])

    for i, g in enumerate((g_pm, g_pf, g_mem)):
        load_T(gT[:, i, :], g.rearrange("o (k p) -> (o k) p", p=128), 16)
    for i in range(3):
        load_T(cw[:, i, :], conv_w[i:i + 1, :].rearrange("o (k p) -> (o k) p", p=128), FT)
    load_T(cw[:, 3, :], conv_b.rearrange("o (k p) -> (o k) p", p=128), FT)
    load_T(dT[:, :], ssm_d[:, :], 6)
    for t in range(6):
        P.op("dve", lambda e, t=t: e.tensor_scalar(out=Ddiag[:, t, :], in0=ident[:], scalar1=dT[:, t:t + 1],
                                                   scalar2=None, op0=ALU.mult), [B_const], [B_const])
    P.dma(nat[:, 0:12], sinks.broadcast_to([128, 12]), [], [B_nat], B_nat)
    P.op("act", lambda e: e.activation(out=esk[:], in_=nat[:, 0:12], func=AF.Exp), [B_nat], [B_const])

    oht = carve([32, 64 * 3 * 64], F32); B_oh = P.buf("oh")
    tb = carve([32, 12], F32)
    P.dma(oht[:, :], oh[:, :], [], [B_oh], B_oh)
    P.dma(tb[:, :], rbt[:, :], [], [B_oh], B_oh)
    ohv = oht.rearrange("b (i k j) -> b i k j", i=64, k=3)
    for kc in range(3):
        for ih in range(2):
            pt, pb = nps()
            def fn(e, pt=pt, kc=kc, ih=ih):
                ins = None
                for ii in range(32):
                    i = ih * 32 + ii
                    ins = e.matmul(pt[0:64, ii * 12:(ii + 1) * 12], lhsT=ohv[:, i, kc, :], rhs=tb[:, :],
                                   start=True, stop=True)
                return ins
            P.op("pe", fn, [B_oh], [pb])
            P.op("dve", lambda e, pt=pt, kc=kc, ih=ih: e.tensor_copy(
                out=biasT[:, kc, ih * 32:(ih + 1) * 32, :].rearrange("p i h -> p (i h)"), in_=pt[0:64, 0:384]),
                [pb], [B_const])

    ck("const")
    barrier()

    def to_state(dst, src48x64):
        P.dma(nat[0:48, 0:64], src48x64, [], [B_nat], B_nat)
        P.dma(nat[0:48, 64:128], src48x64, [], [B_nat], B_nat, par=True)
        pt, pb = nps()
        P.op("pe", lambda e: e.transpose(pt[:, 0:48], nat[0:48, 0:128], ident[0:48, 0:48]), [B_nat, B_const], [pb])
        pv = pt[:, 0:48].rearrange("p (g two) -> p g two", two=2)
        P.op("dve", lambda e: e.tensor_copy(out=dst[0:64, :], in_=pv[0:64, :, 0]), [pb], [B_const])
        P.op("dve", lambda e: e.tensor_copy(out=dst[64:128, :], in_=pv[64:128, :, 1]), [pb], [B_const])

    lre = carve([128, NGP], F32); lim = carve([128, NGP], F32); dts = carve([128, NGP], F32)
    th = carve([128, NGP], F32)
    to_state(lre, a_re[:, :]); to_state(lim, a_im[:, :])
    P.dma(nat[:, 0:48], log_dt.broadcast_to([128, 48]), [], [B_nat], B_nat)
    nv = nat[:, 0:48].rearrange("p (g two) -> p g two", two=2)
    P.op("act", lambda e: e.activation(out=dts[0:64, :], in_=nv[0:64, :, 0], func=AF.Exp), [B_nat], [B_const])
    P.op("act", lambda e: e.activation(out=dts[64:128, :], in_=nv[64:128, :, 1], func=AF.Exp), [B_nat], [B_const])
    P.op("dve", lambda e: e.tensor_tensor(out=th[:, :], in0=lim, in1=dts, op=ALU.mult), [B_const], [B_const])
    tmpa = carve([128, NGP], F32)
    P.op("dve", lambda e: e.tensor_tensor(out=tmpa, in0=lre, in1=dts, op=ALU.mult), [B_const], [B_const])
    P.op("act", lambda e: e.activation(out=rS[:, :], in_=tmpa, func=AF.Exp), [B_const], [B_const])
    idx = carve([128, 512], F32)
    P.op("pool", lambda e: e.iota(idx, pattern=[[1, 512]], base=1, channel_multiplier=0,
                                  allow_small_or_imprecise_dtypes=True), [], [B_const])
    LIM = 3.14159
    c0t = carve([128, NGP], F32); s0t = carve([128, NGP], F32)
    tmps = [[carve([128, 512], F32), carve([128, 512], I32), carve([128, 512], F32), carve([128, 512], F32)] for _ in range(3)]
    B_tt = [P.buf("tt0"), P.buf("tt1"), P.buf("tt2")]
    it = 0
    for gp in range(NGP):
        for fn_i, shift in ((0, math.pi / 2), (1, 0.0)):
            ang, ki, kf, sn = tmps[it % 3]; Bt = B_tt[it % 3]; it += 1
            P.op("dve", lambda e, ang=ang, gp=gp, shift=shift: e.tensor_scalar(out=ang, in0=idx, scalar1=th[:, gp:gp + 1], scalar2=shift,
                                                                              op0=ALU.mult, op1=ALU.add), [B_const, Bt], [Bt])
            P.op("dve", lambda e, ang=ang, ki=ki: e.tensor_scalar(out=ki, in0=ang, scalar1=1.0 / (2 * math.pi), scalar2=None, op0=ALU.mult), [Bt], [Bt])
            P.op("dve", lambda e, kf=kf, ki=ki: e.tensor_copy(out=kf, in_=ki), [Bt], [Bt])
            P.op("dve", lambda e, ang=ang, kf=kf: e.scalar_tensor_tensor(out=ang, in0=kf, scalar=-2 * math.pi, in1=ang, op0=ALU.mult, op1=ALU.add), [Bt], [Bt])
            P.op("dve", lambda e, ang=ang: e.tensor_scalar(out=ang, in0=ang, scalar1=-LIM, scalar2=LIM, op0=ALU.max, op1=ALU.min), [Bt], [Bt])
            P.op("act", lambda e, ang=ang, sn=sn: e.activation(out=sn, in_=ang, func=AF.Sin), [Bt], [Bt])
            dstc = c0t if fn_i == 0 else s0t
            P.op("act", lambda e, sn=sn, dstc=dstc, gp=gp: e.activation(out=dstc[:, gp:gp + 1], in_=sn[:, 0:1], func=AF.Copy), [Bt], [B_const])
            P.dma(rot_tab[gp, fn_i], sn, [Bt], [B_rot], Bt)
    nre = carve([128, NGP], F32); nim = carve([128, NGP], F32); den = carve([128, NGP], F32)
    cre = carve([128, NGP], F32); cim = carve([128, NGP], F32); t1 = carve([128, NGP], F32); t2 = carve([128, NGP], F32)
    V = lambda fn, r=(B_const,), w=(B_const,): P.op("dve", fn, list(r), list(w))
    V(lambda e: e.tensor_tensor(out=nre, in0=rS[:, :], in1=c0t, op=ALU.mult))
    V(lambda e: e.tensor_scalar(out=nre, in0=nre, scalar1=-1.0, scalar2=None, op0=ALU.add))
    V(lambda e: e.tensor_tensor(out=nim, in0=rS[:, :], in1=s0t, op=ALU.mult))
    V(lambda e: e.tensor_tensor(out=den, in0=lre, in1=lre, op=ALU.mult))
    V(lambda e: e.tensor_tensor(out=t1, in0=lim, in1=lim, op=ALU.mult))
    V(lambda e: e.tensor_tensor(out=den, in0=den, in1=t1, op=ALU.add))
    V(lambda e: e.reciprocal(out=den, in_=den))
    V(lambda e: e.tensor_tensor(out=t1, in0=nre, in1=lre, op=ALU.mult))
    V(lambda e: e.tensor_tensor(out=t2, in0=nim, in1=lim, op=ALU.mult))
    V(lambda e: e.tensor_tensor(out=t1, in0=t1, in1=t2, op=ALU.add))
    V(lambda e: e.tensor_tensor(out=cre, in0=t1, in1=den, op=ALU.mult))
    V(lambda e: e.tensor_tensor(out=t1, in0=nim, in1=lre, op=ALU.mult))
    V(lambda e: e.tensor_tensor(out=t2, in0=nre, in1=lim, op=ALU.mult))
    V(lambda e: e.tensor_tensor(out=t1, in0=t1, in1=t2, op=ALU.subtract))
    V(lambda e: e.tensor_tensor(out=cim, in0=t1, in1=den, op=ALU.mult))
    Zre = carve([128, NGP, 128], F32); Zim = carve([128, NGP, 128], F32); Zt = carve([128, NGP, 128], F32)
    B_Z = P.buf("Z")
    P.op("dve", lambda e: e.memset(Zre, 0.0), [], [B_Z])
    P.op("dve", lambda e: e.memset(Zim, 0.0), [], [B_Z])
    for (Z, src) in ((Zre, b_re), (Zim, b_im)):
        sv = src.rearrange("(t j two) p c -> two j p t c", j=4, two=2)
        Zv = Z.rearrange("p (t j) m -> p j t m", j=4)
        for gpar in range(2):
            for j in range(4):
                c0 = 32 * j + 16 * gpar
                P.dma(Zv[64 * gpar:64 * gpar + 64, j, :, c0:c0 + 16], sv[gpar, j], [], [B_Z], B_Z, par=True)
    bc = lambda t: t.unsqueeze(2).to_broadcast([128, NGP, 128])
    VZ = lambda fn: P.op("dve", fn, [B_Z, B_const], [B_Z])
    VZ(lambda e: e.tensor_tensor(out=Zt, in0=Zim, in1=bc(cim), op=ALU.mult))
    VZ(lambda e: e.tensor_tensor(out=Zim, in0=Zim, in1=bc(cre), op=ALU.mult))
    Zt2 = carve([128, NGP, 128], F32)
    VZ(lambda e: e.tensor_tensor(out=Zt2, in0=Zre, in1=bc(cim), op=ALU.mult))
    VZ(lambda e: e.tensor_tensor(out=Zim, in0=Zim, in1=Zt2, op=ALU.add))
    VZ(lambda e: e.tensor_tensor(out=Zre, in0=Zre, in1=bc(cre), op=ALU.mult))
    VZ(lambda e: e.tensor_tensor(out=Zre, in0=Zre, in1=Zt, op=ALU.subtract))
    for (Z, dst) in ((Zre, Bre), (Zim, Bim)):
        for gp in range(NGP):
            pt, pb = nps()
            P.op("pe", lambda e, pt=pt, Z=Z, gp=gp: e.transpose(pt[:, 0:128], Z[:, gp, :], ident[:, :]), [B_Z, B_const], [pb])
            evac(dst[:, gp, :], pt[:, 0:128], [pb], [B_const])
    B_W = P.buf("W")
    for (src, dst, sc) in ((cc_re, Cre, None), (cc_im, Cim, -1.0)):
        Wt = Zt if src is cc_re else Zt2
        P.op("dve", lambda e, Wt=Wt: e.memset(Wt, 0.0), [B_Z], [B_W, B_Z])
        sv = src.rearrange("(t j two) c p -> two j c t p", j=4, two=2)
        Wv = Wt.rearrange("p (t j) m -> p j t m", j=4)
        for gpar in range(2):
            for j in range(4):
                r0 = 32 * j + 16 * gpar
                P.dma(Wv[r0:r0 + 16, j, :, 64 * gpar:64 * gpar + 64], sv[gpar, j], [], [B_W], B_W, par=True)
        for gp in range(NGP):
            pt, pb = nps()
            P.op("pe", lambda e, pt=pt, Wt=Wt, gp=gp: e.transpose(pt[:, 0:128], Wt[:, gp, :], ident[:, :]), [B_W, B_const], [pb])
            evac(dst[:, gp, :], pt[:, 0:128], [pb], [B_const], scale=sc)

    ck("ssmsetup")
    def rms_and_T(src_rows, bsrc, ntile, hT, B_hT, gidx, ntok_tile=128, extra=None):
        xb_, Bx_ = (xt, B_xt) if ntile % 2 == 0 else (mt, B_mt)
        P.dma(xb_[0:ntok_tile, :], src_rows, [bsrc], [Bx_], Bx_)
        norm_T(xb_, Bx_, ntile, hT, B_hT, gidx, ntok_tile)

    def rstd_from(col_in, col_out, n):
        P.op("act", lambda e: e.activation(out=stat[0:n, col_out:col_out + 1], in_=stat[0:n, col_in:col_in + 1], func=AF.Ln,
                                           scale=1.0 / D, bias=epsb[0:n, :]), [B_stat, B_const], [B_stat])
        P.op("act", lambda e: e.activation(out=stat[0:n, col_out:col_out + 1], in_=stat[0:n, col_out:col_out + 1], func=AF.Exp,
                                           scale=-0.5), [B_stat], [B_stat])

    ncall = {"n": 0}

    def norm_T(src, bsrc, ntile, hT, B_hT, gidx, n=128):
        ncall["n"] += 1
        ck(f"c{ncall['n']}_n0")
        P.op("dve", lambda e: e.memset(stat[:, 0:2], 0.0), [], [B_stat])
        P.op("act", lambda e: e.activation(out=hb[0:n, :], in_=src[0:n, :], func=AF.Square, accum_out=stat[0:n, 0:1]),
             [bsrc, B_stat], [B_hb, B_stat])
        rstd_from(0, 1, n)
        P.op("dve", lambda e: e.tensor_scalar(out=hb[0:n, :], in0=src[0:n, :], scalar1=stat[0:n, 1:2], scalar2=None,
                                              op0=ALU.mult), [bsrc, B_stat], [B_hb])
        ck(f"c{ncall['n']}_n1")
        for q4 in range(2):
            pt, pb = npsb()
            def fn(e, pt=pt, q4=q4):
                ins = None
                for j in range(8):
                    kt = q4 * 8 + j
                    ins = e.transpose(pt[:, j * 128:j * 128 + n], hb[0:n, kt * 128:(kt + 1) * 128], identb[0:n, 0:n])
                return ins
            P.op("pe", fn, [B_hb, B_const], [pb])
            ck(f"c{ncall['n']}_n2")
            for j in range(8):
                kt = q4 * 8 + j
                evac(hT[:, kt, ntile * 128:ntile * 128 + n], pt[:, j * 128:j * 128 + n], [pb, B_const], [B_hT],
                     scale=gT[:, gidx, kt:kt + 1])
                ck(f"c{ncall['n']}_n3_{q4}_{j}")

    def proj_fm(dst_fn, src, bsrc, c0, ncols, hT, B_hT, ntok, nk=KT, k0=0, tok0=0):
        for cb in range(0, ncols, CW):
            ncb = min(CW, ncols - cb)
            wt, wb = wload(src, bsrc, k0, nk, c0 + cb, ncb)
            for ct in range(ncb // 128):
                pt, pb = nps()
                mm(pt[:, 0:ntok], [(wt[:, k, ct * 128:(ct + 1) * 128], hT[:, k, tok0:tok0 + ntok]) for k in range(nk)],
                   [wb, B_hT], [pb])
                dst, bd = dst_fn((cb // 128) + ct)
                evac(dst, pt[:, 0:ntok], [pb], [bd])

    def mem_kv_prompt(mkT, mvb, B_mkv):
        hmT = carve([128, KT, 256], BF16); B_hm = P.buf("hmT")
        for t in range(2):
            rms_and_T(memp[t * 128:(t + 1) * 128, :], B_const, t, hmT, B_hm, 2)
        ck("mk1")
        for cb in range(0, 1024, CW):
            wt, wb = wload(wmkv_b, B_wmkv, 0, KT, cb, CW)
            ck("mk1a")
            for t in range(2):
                pt, pb = nps()
                mm(pt[:, 0:CW], [(hmT[:, k, t * 128:(t + 1) * 128], wt[:, k, 0:CW]) for k in range(KT)], [wb, B_hm], [pb])
                ck("mk1b")
                P.op("act", lambda e, pt=pt: e.activation(out=stg[:, 0:CW], in_=pt[:, 0:CW], func=AF.Copy), [pb], [B_stg])
                ck("mk1c")
                dst = o_pmk if cb < 512 else o_pmv
                P.dma(dst[t * 128:(t + 1) * 128, (cb % 512):(cb % 512) + CW], stg[:, 0:CW], [B_stg], [], B_stg)
                ck("mk1d")
                if cb >= 512:
                    P.op("dve", lambda e, pt=pt, t=t, cb=cb: e.tensor_copy(out=mvb[:, t, cb - 512:cb - 512 + CW], in_=pt[:, 0:CW]),
                         [pb], [B_mkv])
                ck(f"mk_{cb}_{t}")
        ck("mk2")
        proj_fm(lambda ct: (mkT[:, ct, :], B_mkv), wmkv_b, B_wmkv, 0, 512, hmT, B_hm, 256)

    def mem_kv_sample(mkT, mvb, B_mkv, s):
        f = carve([128, 2, 512], F32); fb = carve([128, 2, 512], BF16); B_f = P.buf("mkf")
        P.dma(f, c_mk[s].rearrange("(t p) n -> p t n", p=128), [], [B_f], B_f)
        P.op("dve", lambda e: e.tensor_copy(out=fb, in_=f), [B_f], [B_f])
        for hh in range(4):
            pt, pb = npsb()
            def fn(e, pt=pt, hh=hh):
                ins = None
                for t in range(2):
                    ins = e.transpose(pt[:, t * 128:(t + 1) * 128], fb[:, t, hh * 128:(hh + 1) * 128], identb[:, :])
                return ins
            P.op("pe", fn, [B_f, B_const], [pb])
            evac(mkT[:, hh, :], pt[:, 0:256], [pb], [B_mkv])
        f2 = carve([128, 2, 512], F32); B_f2 = P.buf("mvf")
        P.dma(f2, c_mv[s].rearrange("(t p) n -> p t n", p=128), [], [B_f2], B_f2)
        P.op("act", lambda e: e.activation(out=mvb, in_=f2, func=AF.Copy), [B_f2], [B_mkv])

    def block(x_src, B_xsrc, y_dst, B_y, tok0, NT, seqs, first, last, sample, skip_s1=False, next_x=None, pre11=None, defer11=False):
        nt = NT // 128
        barrier()
        hT = carve([128, KT, NT], BF16); B_hT = P.buf("hT")
        uT = carve([128, 6, NT], BF16)
        B_q, B_u, B_qm = P.buf("qT"), P.buf("uT"), P.buf("qmT")
        oaT = carve([128, 6, NT], BF16); osT = carve([128, 6, NT], BF16); omT = carve([128, 4, NT], BF16)
        B_oa, B_os, B_om = P.buf("oaT"), P.buf("osT"), P.buf("omT")
        yT = carve([128, 6, NT], BF16); B_yT = P.buf("yT")
        nseq = len(seqs)
        mkTs = [carve([128, 4, 256], BF16) if sample else None for _ in range(nseq)]
        mvbs = [carve([128, 2, 512], BF16) if sample else None for _ in range(nseq)]
        B_mkvs = [P.buf("mkv") for _ in range(nseq)]
        markS = off["v"]
        qT = carve([128, 6, NT], BF16); qmT = carve([128, 4, NT], BF16)
        mark0 = off["v"]
        if sample:
            for s in range(nseq):
                mem_kv_sample(mkTs[s], mvbs[s], B_mkvs[s], s)
        else:
            mkTs[0], mvbs[0], B_mkvs[0] = pm["mkT"], pm["mvb"], pm["B"]
        if not skip_s1:
            for t in range(nt):
                rms_and_T(x_src[tok0 + t * 128:tok0 + (t + 1) * 128, :], B_xsrc, t, hT, B_hT, 0)
        ck(f"b{tok0 // 512 if not sample else 4}s1")
        if sample:
            for s in range(nseq):
                kf32 = carve([128, 256], F32); kb16 = carve([128, 256], BF16); Bk = P.buf("kc")
                P.dma(kf32, c_ak[s], [], [Bk], Bk)
                P.op("dve", lambda e, kb16=kb16, kf32=kf32: e.tensor_copy(out=kb16, in_=kf32), [Bk], [Bk])
                pt, pb = npsb()
                def fn(e, pt=pt, kb16=kb16):
                    ins = None
                    for t in range(2):
                        ins = e.transpose(pt[:, t * 128:(t + 1) * 128], kb16[:, t * 128:(t + 1) * 128], identb[:, :])
                    return ins
                P.op("pe", fn, [Bk, B_const], [pb])
                evac(kTt[:, :, s * 192:s * 192 + 128], pt[:, 0:256].rearrange("p (t n) -> p t n", t=2), [pb], [B_kT])
                vf32 = carve([64, 2, 256], F32); Bv = P.buf("vc")
                P.dma(vf32, c_av[s].rearrange("(c p) n -> p c n", p=64), [], [Bv], Bv)
                vv = vf32.rearrange("p c (h d) -> p c h d", h=NKV)
                for dup in range(2):
                    P.op("dve", lambda e, s=s, dup=dup, vv=vv: e.tensor_copy(out=v64[:, s * 3:s * 3 + 2, :, dup * 64:(dup + 1) * 64], in_=vv),
                         [Bv], [B_v64])
                P.dma(o_sk[s, 0:64, :], c_ak[s, 64:128, :], [], [], B_out)
                P.dma(o_sv[s, 0:64, :], c_av[s, 64:128, :], [], [], B_out)
        if sample:
            kcol = [s * 192 + 128 for s in range(nseq)]
            vch = [s * 3 + 2 for s in range(nseq)]
        else:
            kcol = [128]
            vch = [2]
        def dst_q(ct):
            return qT[:, ct, :], B_q
        pre11 = list(pre11 or [])
        proj_fm(dst_q, win_b, B_win, 0, 768, hT, B_hT, NT)
        if pre11:
            pre11.pop(0)()
        for cb in range(0, 256, CW):
            wt, wb = wload(win_b, B_win, 0, KT, 768 + cb, CW)
            for ct in range(2):
                pt, pb = nps()
                mm(pt[:, 0:NT], [(wt[:, k, ct * 128:(ct + 1) * 128], hT[:, k, 0:NT]) for k in range(KT)], [wb, B_hT], [pb])
                for si, (c0, n) in enumerate(seqs):
                    evac(kTt[:, ct, kcol[si]:kcol[si] + n], pt[:, c0:c0 + n], [pb], [B_kT])
            if last or sample:
                for si, (c0, n) in enumerate(seqs):
                    r0, nr = (c0 + n - 128, 128) if not sample else (c0, 64)
                    pt, pb = nps()
                    mm(pt[0:nr, 0:256], [(hT[:, k, r0:r0 + nr], wt[:, k, 0:256]) for k in range(KT)], [wb, B_hT], [pb])
                    P.op("act", lambda e, pt=pt, nr=nr: e.activation(out=stg[0:nr, 0:256], in_=pt[0:nr, 0:256], func=AF.Copy), [pb], [B_stg])
                    dst = o_sk[si, 64:128, :] if sample else o_pk[:, :]
                    P.dma(dst, stg[0:nr, 0:256], [B_stg], [], B_stg)
        wt, wb = wload(win_b, B_win, 0, KT, 1024, CW)
        for si, (c0, n) in enumerate(seqs):
            for c in range(n // 64):
                pt, pb = nps()
                mm(pt[0:64, 0:256], [(hT[:, k, c0 + c * 64:c0 + (c + 1) * 64], wt[:, k, 0:256]) for k in range(KT)], [wb, B_hT], [pb])
                pv = pt[0:64, 0:256].rearrange("p (h d) -> p h d", h=NKV)
                for dup in range(2):
                    evac(v64[:, vch[si] + c, :, dup * 64:(dup + 1) * 64], pv, [pb], [B_v64])
                is_out = sample or (last and c >= n // 64 - 2)
                if is_out:
                    P.op("act", lambda e, pt=pt: e.activation(out=stg[0:64, 256:512], in_=pt[0:64, 0:256], func=AF.Copy), [pb], [B_stg])
                    if sample:
                        dst = o_sv[si, 64:128, :]
                    else:
                        cc = c - (n // 64 - 2)
                        dst = o_pv[cc * 64:(cc + 1) * 64, :]
                    P.dma(dst, stg[0:64, 256:512], [B_stg], [], B_stg)
        if pre11:
            pre11.pop(0)()
        proj_fm(lambda ct: (uT[:, ct, :], B_u), win_b, B_win, 1280, 768, hT, B_hT, NT)
        if pre11:
            pre11.pop(0)()
        proj_fm(lambda ct: (qmT[:, ct, :], B_qm), win_b, B_win, 2048, 512, hT, B_hT, NT)
        while pre11:
            pre11.pop(0)()

        ck(f"b{tok0 // 512 if not sample else 4}s2")
        rewind(mark0)
        scs = [carve([64, 3, 192], F32) for _ in range(2)]
        pTs = [carve([64, 3, 192], BF16) for _ in range(2)]
        dns = [carve([128, 192], F32) for _ in range(2)]
        B_scs = [[P.buf(f"sc{a_}{k_}") for k_ in range(3)] for a_ in range(2)]
        B_pTs = [[P.buf(f"pT{a_}{k_}") for k_ in range(3)] for a_ in range(2)]
        B_dns = [P.buf("dn0"), P.buf("dn1")]
        ai = 0
        for si, (c0, n) in enumerate(seqs):
            nch = n // 64
            for c in range(nch):
                kcs = [kc for kc in range(3) if sample or (not first) or (c - 2 + kc) >= 0]
                for h in range(NKV):
                    sc, pT, dn = scs[ai % 2], pTs[ai % 2], dns[ai % 2]
                    B_sc, B_pT, B_dn = B_scs[ai % 2], B_pTs[ai % 2], B_dns[ai % 2]
                    ai += 1
                    half = (h % 2) * 64
                    kt_ = h // 2
                    t0 = (h // 2) * 3
                    rhs = qT[half:half + 64, t0:t0 + 3, c0 + c * 64:c0 + (c + 1) * 64]
                    pss = []
                    for kc in kcs:
                        kk = kcol[si] + (c - 2 + kc) * 64
                        pt, pb = nps()
                        mm(pt[0:64, 0:192], [(kTt[half:half + 64, kt_, kk:kk + 64], rhs)], [B_kT, B_q], [pb])
                        pss.append((kc, pt, pb))
                    for kc, pt, pb in pss:
                        bv = biasT[:, kc, :, 3 * h:3 * h + 3].rearrange("p i g -> p g i")
                        P.op("dve", lambda e, pt=pt, kc=kc, bv=bv, sc=sc: e.scalar_tensor_tensor(
                            out=sc[:, kc, :].rearrange("p (g i) -> p g i", g=3), in0=pt[0:64, 0:192].rearrange("p (g i) -> p g i", g=3),
                            scalar=HD ** -0.5, in1=bv, op0=ALU.mult, op1=ALU.add), [pb, B_const], [B_sc[kc]])
                    for kc, pt, pb in pss:
                        P.op("act", lambda e, kc=kc, sc=sc, pT=pT: e.activation(out=pT[:, kc, :], in_=sc[:, kc, :], func=AF.Exp), [B_sc[kc]], [B_pT[kc]])
                    po, pob = nps()
                    mm(po[:, 0:192], [(v64[:, vch[si] + c - 2 + kc, h, :], pT[:, kc, :]) for kc in kcs], [B_v64] + [B_pT[kc] for kc in kcs], [pob])
                    pd, pdb = nps()
                    mm(pd[:, 0:192], [(ones_b[0:64, :], pT[:, kc, :]) for kc in kcs], [B_const] + [B_pT[kc] for kc in kcs], [pdb])
                    P.op("dve", lambda e, pd=pd, h=h, half=half, dn=dn: e.tensor_tensor(
                        out=dn[half:half + 64, :].rearrange("p (g i) -> p g i", g=3),
                        in0=pd[half:half + 64, 0:192].rearrange("p (g i) -> p g i", g=3),
                        in1=esk[half:half + 64, 3 * h:3 * h + 3].unsqueeze(2).to_broadcast([64, 3, 64]), op=ALU.add),
                        [pdb, B_const], [B_dn])
                    P.op("dve", lambda e, half=half, dn=dn: e.reciprocal(out=dn[half:half + 64, :], in_=dn[half:half + 64, :]), [B_dn], [B_dn])
                    P.op("dve", lambda e, po=po, half=half, t0=t0, c=c, c0=c0, dn=dn: e.tensor_tensor(
                        out=oaT[half:half + 64, t0:t0 + 3, c0 + c * 64:c0 + (c + 1) * 64],
                        in0=po[half:half + 64, 0:192].rearrange("p (g i) -> p g i", g=3),
                        in1=dn[half:half + 64, :].rearrange("p (g i) -> p g i", g=3), op=ALU.mult), [pob, B_dn], [B_oa])
        if not sample and not last:
            P.op("dve", lambda e: e.tensor_copy(out=kTt[:, :, 0:128], in_=kTt[:, :, NT:NT + 128]), [B_kT], [B_kT])
            P.op("dve", lambda e: e.tensor_copy(out=v64[:, 0:2, :, :], in_=v64[:, 8:10, :, :]), [B_v64], [B_v64])

        ck(f"b{tok0 // 512 if not sample else 4}s3")
        rewind(mark0)
        pm_ = carve([128, 2, 512], BF16); B_pm = P.buf("pmem")
        rcp = carve([128, 512], F32); B_rcp = P.buf("rcp")
        for si, (c0, n) in enumerate(seqs):
            for hh in range(4):
                for mtile in range(2):
                    pt, pb = nps()
                    mm(pt[:, 0:n], [(mkTs[si][:, hh, mtile * 128:(mtile + 1) * 128], qmT[:, hh, c0:c0 + n])], [B_mkvs[si], B_qm], [pb])
                    P.op("act", lambda e, pt=pt, mtile=mtile, n=n: e.activation(out=pm_[:, mtile, 0:n], in_=pt[:, 0:n], func=AF.Exp,
                                                                           scale=128 ** -0.5), [pb], [B_pm])
                po, pob = nps()
                mm(po[:, 0:n], [(mvbs[si][:, mtile, hh * 128:(hh + 1) * 128], pm_[:, mtile, 0:n]) for mtile in range(2)], [B_mkvs[si], B_pm], [pob])
                pd, pdb = nps()
                mm(pd[:, 0:n], [(ones_b[:, :], pm_[:, mtile, 0:n]) for mtile in range(2)], [B_const, B_pm], [pdb])
                P.op("dve", lambda e, pd=pd, n=n: e.reciprocal(out=rcp[:, 0:n], in_=pd[:, 0:n]), [pdb], [B_rcp])
                P.op("dve", lambda e, po=po, n=n, hh=hh, c0=c0: e.tensor_tensor(out=omT[:, hh, c0:c0 + n], in0=po[:, 0:n], in1=rcp[:, 0:n],
                                                                         op=ALU.mult), [pob, B_rcp], [B_om])

        ck(f"b{tok0 // 512 if not sample else 4}s4")
        rewind(markS)
        tq2 = [[carve([128, NT], F32) for _ in range(2)] for _ in range(2)]
        rin2 = [[carve([128, NT], F32) for _ in range(2)] for _ in range(2)]
        w2 = [[carve([128, NT], F32) for _ in range(2)] for _ in range(2)]
        pq2 = tq2
        sbf2 = [[carve([128, NT], BF16) for _ in range(2)] for _ in range(2)]
        cr2 = [carve([128, 4], F32) for _ in range(2)]
        B_tq2 = [P.buf("tqa"), P.buf("tqb")]; B_rin2 = [P.buf("rina"), P.buf("rinb")]; B_pq2 = B_tq2
        B_w2 = [P.buf("w2a"), P.buf("w2b")]; B_sb2 = [P.buf("sb2a"), P.buf("sb2b")]; B_cr2 = [P.buf("cr2a"), P.buf("cr2b")]
        ns_ = NT // nseq
        if sample:
            sst_s = [carve([128, 2, NGP], F32) for _ in range(nseq)]
            B_ssts = [P.buf("ssts0"), P.buf("ssts1")]
            for s in range(nseq):
                to_state_b(sst_s[s][:, 0, :], st_re[s], B_ssts[s])
                to_state_b(sst_s[s][:, 1, :], st_im[s], B_ssts[s])
        v3 = lambda t: t.rearrange("p (s n) -> p s n", s=nseq)

        def gp_stages(tile_, g4, par):
            gp = tile_ * 4 + g4
            tab, Btab = tabs[par], B_tabs[par]
            w_, Bw = w2[par], B_w2[par]; sbf, Bsb = sbf2[par], B_sb2[par]; cr, Bcr = cr2[par], B_cr2[par]
            rin, B_rin = rin2[par], B_rin2[par]; tq, B_tq = tq2[par], B_tq2[par]
            py, pyb = psy
            cosb = tab[:, 0, 0:ns_].unsqueeze(1).to_broadcast([128, nseq, ns_])
            sinb = tab[:, 1, 0:ns_].unsqueeze(1).to_broadcast([128, nseq, ns_])
            hold = {}
            S = lambda fn, r, w: (lambda: P.op("dve", fn, r, w))
            G = lambda fn, r, w: (lambda: P.op("dve", fn, r, w))

            def st0():
                P.dma(tab[:, :, 0:ns_], rot_tab[gp].rearrange("c p n -> p c n")[:, :, 0:ns_], [B_rot], [Btab], Btab)
                hold["pr"] = psf[2 * par]; hold["pi"] = psf[2 * par + 1]
                mm(hold["pr"][0][:, 0:NT], [(Bre[:, gp, :], uT[:, tile_, 0:NT])], [B_const, B_u], [hold["pr"][1]])
                mm(hold["pi"][0][:, 0:NT], [(Bim[:, gp, :], uT[:, tile_, 0:NT])], [B_const, B_u], [hold["pi"][1]])
            stages = [[st0]]
            PR = lambda: hold["pr"][0][:, 0:NT]
            PI = lambda: hold["pi"][0][:, 0:NT]
            stages.append([lambda: P.op("dve", lambda e, x=PR(): e.tensor_tensor(out=v3(rin[0]), in0=v3(x), in1=cosb, op=ALU.mult), [hold["pr"][1], Btab, B_rin], [B_rin])])
            stages.append([lambda: P.op("dve", lambda e, x=PI(): e.tensor_tensor(out=v3(tq[0]), in0=v3(x), in1=sinb, op=ALU.mult), [hold["pi"][1], Btab, B_tq], [B_tq])])
            stages.append([lambda: P.op("dve", lambda e, x=PI(): e.tensor_tensor(out=v3(rin[1]), in0=v3(x), in1=cosb, op=ALU.mult), [hold["pi"][1], Btab, B_rin], [B_rin])])
            stages.append([lambda: P.op("dve", lambda e, x=PR(): e.tensor_tensor(out=v3(tq[1]), in0=v3(x), in1=sinb, op=ALU.mult), [hold["pr"][1], Btab, B_tq], [B_tq])])
            stages.append([S(lambda e: e.tensor_tensor(out=rin[0], in0=rin[0], in1=tq[0], op=ALU.add), [B_tq, B_rin], [B_rin])])
            stages.append([S(lambda e: e.tensor_tensor(out=rin[1], in0=rin[1], in1=tq[1], op=ALU.subtract), [B_tq, B_rin], [B_rin])])
            for si, (c0, n) in enumerate(seqs):
                stt, bst = (sst_s[si], B_ssts[si]) if sample else (sst, B_sst)
                for ri in range(2):
                    stages.append([S(lambda e, ri=ri, c0=c0, n=n, stt=stt: e.tensor_tensor_scan(
                        out=w_[ri][:, c0:c0 + n], data0=rS[:, gp:gp + 1].to_broadcast([128, n]), data1=rin[ri][:, c0:c0 + n],
                        initial=stt[:, ri, gp:gp + 1], op0=ALU.mult, op1=ALU.add), [B_rin, B_const, bst, Bw], [Bw])])
                cl = tab[:, 0, n - 1:n]; sl = tab[:, 1, n - 1:n]; e1 = c0 + n - 1
                stages.append([S(lambda e, sl=sl, e1=e1: e.tensor_scalar(out=cr[:, 2:3], in0=w_[1][:, e1:e1 + 1], scalar1=sl, scalar2=None, op0=ALU.mult), [Bw, Btab, Bcr], [Bcr])])
                stages.append([S(lambda e, sl=sl, e1=e1: e.tensor_scalar(out=cr[:, 3:4], in0=w_[0][:, e1:e1 + 1], scalar1=sl, scalar2=None, op0=ALU.mult), [Bw, Btab, Bcr], [Bcr])])
                stages.append([S(lambda e, cl=cl, e1=e1, stt=stt: e.scalar_tensor_tensor(out=stt[:, 0, gp:gp + 1], in0=w_[0][:, e1:e1 + 1], scalar=cl, in1=cr[:, 2:3],
                                                                          op0=ALU.mult, op1=ALU.subtract), [Bw, Btab, Bcr], [bst])])
                stages.append([S(lambda e, cl=cl, e1=e1, stt=stt: e.scalar_tensor_tensor(out=stt[:, 1, gp:gp + 1], in0=w_[1][:, e1:e1 + 1], scalar=cl, in1=cr[:, 3:4],
                                                                          op0=ALU.mult, op1=ALU.add), [Bw, Btab, Bcr], [bst])])
            pq, B_pq = pq2[par], B_pq2[par]
            stages.append([G(lambda e: e.tensor_tensor(out=v3(pq[0]), in0=v3(w_[0]), in1=cosb, op=ALU.mult), [Bw, Btab, B_pq], [B_pq])])
            stages.append([G(lambda e: e.tensor_tensor(out=v3(pq[1]), in0=v3(w_[1]), in1=sinb, op=ALU.mult), [Bw, Btab, B_pq], [B_pq])])
            stages.append([G(lambda e: e.tensor_tensor(out=sbf[0], in0=pq[0], in1=pq[1], op=ALU.subtract), [B_pq, Bsb], [Bsb])])
            stages.append([G(lambda e: e.tensor_tensor(out=v3(pq[0]), in0=v3(w_[0]), in1=sinb, op=ALU.mult), [Bw, Btab, B_pq], [B_pq])])
            stages.append([G(lambda e: e.tensor_tensor(out=v3(pq[1]), in0=v3(w_[1]), in1=cosb, op=ALU.mult), [Bw, Btab, B_pq], [B_pq])])
            stages.append([G(lambda e: e.tensor_tensor(out=sbf[1], in0=pq[0], in1=pq[1], op=ALU.add), [B_pq, Bsb], [Bsb])])

            def fy(e, first_mm=(g4 == 0)):
                e.matmul(py[:, 0:NT], lhsT=Cre[:, gp, :], rhs=sbf[0], start=first_mm, stop=False)
                ins = e.matmul(py[:, 0:NT], lhsT=Cim[:, gp, :], rhs=sbf[1], start=False, stop=False)
                if g4 == 3:
                    ins = e.matmul(py[:, 0:NT], lhsT=Ddiag[:, tile_, :], rhs=uT[:, tile_, 0:NT], start=False, stop=True)
                return ins
            final = lambda: P.op("pe", fy, [Bsb, B_const, B_u], [pyb])
            return stages, final

        sgst = [carve([128, CW], F32) for _ in range(4)]; B_sgst = [P.buf(f"sgst{i}") for i in range(4)]
        gsi = {"i": 0}

        gbanks = [psf[4], (psb[0][0][:, :].bitcast(F32), psb[0][1]), (psb[1][0][:, :].bitcast(F32), psb[1][1])]
        grr = {"i": 0, "ssm": True}

        def nps_gate():
            if not grr["ssm"]:
                return nps()
            grr["i"] = (grr["i"] + 1) % 3
            return gbanks[grr["i"]]

        def gate_chunk(cc, i):
            gw = wload(win_b, B_win, 0, KT, 2560 + i * D + cc * CW, CW)
            for t in range(nt):
                pt, pb = nps_gate()
                mm(pt[:, 0:CW], [(hT[:, k, t * 128:(t + 1) * 128], gw[0][:, k, 0:CW]) for k in range(KT)], [gw[1], B_hT], [pb])
                j = gsi["i"] % 4; gsi["i"] += 1
                P.op("act", lambda e, pt=pt, j=j: e.activation(out=sgst[j], in_=pt[:, 0:CW], func=AF.Sigmoid), [pb], [B_sgst[j]])
                P.dma(sgscr[cc, t, :, i, :], sgst[j], [B_sgst[j]], [B_sgscr], B_sgst[j], par=True, q="act")
        gate_list = [(cc, i) for cc in range(D // CW) for i in range(3)]
        gpos = {"i": 0}

        def gate_some(n):
            for _ in range(n):
                if gpos["i"] < len(gate_list):
                    gate_chunk(*gate_list[gpos["i"]]); gpos["i"] += 1
        for tile_ in range(6):
            for pair in range(2):
                A, fa = gp_stages(tile_, pair * 2, 0)
                Bq, fb = gp_stages(tile_, pair * 2 + 1, 1)
                for i in range(max(len(A), len(Bq))):
                    for lst in (A, Bq):
                        if i < len(lst):
                            for th_ in lst[i]:
                                th_()
                    if i == 0:
                        gate_some(2)
                fa(); fb()
            P.op("act", lambda e, tile_=tile_: e.activation(out=yT[:, tile_, :], in_=psy[0][:, 0:NT], func=AF.Gelu_apprx_tanh), [psy[1]], [B_yT])
        if sample:
            for s in range(nseq):
                from_state(sst_s[s][:, 0, :], B_ssts[s], o_sre[s]); from_state(sst_s[s][:, 1, :], B_ssts[s], o_sim[s])
        elif last:
            from_state(sst[:, 0, :], B_sst, o_pre[:, :]); from_state(sst[:, 1, :], B_sst, o_pim[:, :])
        ck(f"b{tok0 // 512 if not sample else 4}s5")
        rewind(markS)
        wt, wb = wload(wglu_b, B_wglu, 0, 6, 0, CW)
        wt2, wb2 = wload(wglu_b, B_wglu, 0, 6, 256, CW)
        wt3, wb3 = wload(wglu_b, B_wglu, 0, 6, 512, CW)
        sg = carve([128, NT], F32); B_sg = P.buf("sg")
        for ct in range(6):
            w_t, w_b = ((wt, wb), (wt2, wb2), (wt3, wb3))[ct // 2]
            pt, pb = nps()
            mm(pt[:, 0:NT], [(w_t[:, k, (ct % 2) * 128:(ct % 2) * 128 + 128], yT[:, k, 0:NT]) for k in range(6)], [w_b, B_yT], [pb])
            P.op("act", lambda e, pt=pt: e.activation(out=sg[:, 0:NT], in_=pt[:, 0:NT], func=AF.Sigmoid), [pb], [B_sg])
            P.op("dve", lambda e, ct=ct: e.tensor_tensor(out=osT[:, ct, :], in0=yT[:, ct, :], in1=sg[:, 0:NT], op=ALU.mult), [B_sg, B_yT], [B_os])

        ck(f"b{tok0 // 512 if not sample else 4}s6")
        grr["ssm"] = False
        gate_some(len(gate_list))
        sgts = [carve([128, nt, 3, CW], F32) for _ in range(2)]; B_sgts = [P.buf("sgt0"), P.buf("sgt1")]
        mchs = [carve([128, CW], F32) for _ in range(3)]; B_mchs = [P.buf(f"mch{i}") for i in range(3)]; mt2s = [carve([128, CW], F32) for _ in range(3)]
        mi = 0
        ssq = sb_keep["ssq"]; B_ssq = P.buf("ssq")
        P.op("dve", lambda e: e.memset(ssq, 0.0), [], [B_ssq])
        branch = ((oaT, B_oa, 0, 6), (osT, B_os, 6, 6), (omT, B_om, 12, 4))
        for cc in range(D // CW):
            sgt, B_sgt = sgts[cc % 2], B_sgts[cc % 2]
            P.dma(sgt, sgscr[cc, 0:nt].rearrange("t p i c -> p t i c"), [B_sgscr], [B_sgt], B_sgt)
            ow = wload(wout_b, B_wout, 0, KT, cc * CW, CW)
            for t in range(nt):
                pbr = []
                for i, (oT, Bo, k0, nk) in enumerate(branch):
                    pt, pb = nps()
                    mm(pt[:, 0:CW], [(oT[:, k, t * 128:(t + 1) * 128], ow[0][:, k0 + k, 0:CW]) for k in range(nk)], [ow[1], Bo], [pb])
                    pbr.append((pt, pb))
                mch, mt2, B_mch = mchs[mi % 3], mt2s[mi % 3], B_mchs[mi % 3]; mi += 1
                P.op("dve", lambda e, p0=pbr[0][0], t=t, sgt=sgt: e.tensor_tensor(out=mch, in0=p0[:, 0:CW], in1=sgt[:, t, 0, :], op=ALU.mult), [pbr[0][1], B_sgt], [B_mch])
                for i in (1, 2):
                    P.op("dve", lambda e, p=pbr[i][0], i=i, t=t, sgt=sgt: e.tensor_tensor(out=mt2, in0=p[:, 0:CW], in1=sgt[:, t, i, :], op=ALU.mult), [pbr[i][1], B_sgt, B_mch], [B_mch])
                    P.op("dve", lambda e: e.tensor_tensor(out=mch, in0=mch, in1=mt2, op=ALU.add), [B_mch], [B_mch])
                P.op("act", lambda e, t=t, cc=cc: e.activation(out=mt2, in_=mch, func=AF.Square, accum_out=ssq[:, t, cc:cc + 1]), [B_mch, B_ssq], [B_mch, B_ssq])
                P.dma(mixs[t * 128:(t + 1) * 128, cc * CW:(cc + 1) * CW], mch, [B_mch], [B_mixs], B_mch, par=True, q="act")
        ck(f"b{tok0 // 512 if not sample else 4}s7")
        barrier()
        hT2 = carve([128, KT, NT], BF16); B_h2 = P.buf("hT")
        actT = carve([128, FT, NT], BF16); B_act = P.buf("actT")
        ssq2 = sb_keep["ssq2"]; B_ssq2 = P.buf("ssq2")
        ssq_keep = ssq

        def post_norm_residual(t, ssq_t, B_sq, gi, res_src, B_res, out_dst, B_o, then_norm):
            P.op("dve", lambda e: e.tensor_reduce(out=stat[:, 2:3], in_=ssq_t, axis=mybir.AxisListType.X, op=ALU.add), [B_sq], [B_stat])
            rstd_from(2, 3, 128)
            P.dma(mt[:, :], mixs[t * 128:(t + 1) * 128, :], [B_mixs], [B_mt], B_mt)
            P.dma(xt[:, :], res_src, [B_res], [B_xt], B_xt)
            P.op("dve", lambda e: e.tensor_tensor(out=mt[:, :], in0=mt[:, :], in1=gbc[:, gi, :], op=ALU.mult), [B_mt, B_const], [B_mt])
            P.op("dve", lambda e: e.scalar_tensor_tensor(out=xt[:, :], in0=mt[:, :], scalar=stat[:, 3:4], in1=xt[:, :], op0=ALU.mult,
                                                         op1=ALU.add), [B_mt, B_stat, B_xt], [B_xt])
            P.dma(out_dst, xt[:, :], [B_xt], [B_o], P.buf("xst"), q="act")
            if then_norm:
                norm_T(xt, B_xt, t, hT2, B_h2, 1)
        for t in range(nt):
            post_norm_residual(t, ssq_keep[:, t, :], B_ssq, 0, x_src[tok0 + t * 128:tok0 + (t + 1) * 128, :], B_xsrc,
                               y_dst[tok0 + t * 128:tok0 + (t + 1) * 128, :], B_y, True)
        ck(f"b{tok0 // 512 if not sample else 4}s8")
        asb = carve([128, nseq, 2 + NT // nseq], F32); B_asb = P.buf("asb")
        acc = carve([128, NT], F32); B_acc = P.buf("acc")
        gl = carve([128, NT], F32); B_gl = P.buf("gl")
        mchs = [carve([128, CW], F32) for _ in range(4)]; B_mchs = [P.buf(f"mch{i}") for i in range(4)]; mt2s = [carve([128, CW], F32) for _ in range(4)]
        mi = 0
        ns = NT // nseq
        if sample:
            cv_s = carve([128, nseq, FT, 2], F32); B_cvs = P.buf("cvs")
            for s in range(nseq):
                P.dma(nat[0:88, 0:128], st_cv[s].rearrange("i (f p) -> (i f) p", p=128), [], [B_nat], B_nat)
                pt, pb = nps()
                P.op("pe", lambda e, pt=pt: e.transpose(pt[:, 0:88], nat[0:88, 0:128], ident[0:88, 0:88]), [B_nat, B_const], [pb])
                P.op("dve", lambda e, pt=pt, s=s: e.tensor_copy(out=cv_s[:, s, :, :].rearrange("p f i -> p i f"),
                                                               in_=pt[:, 0:88].rearrange("p (i f) -> p i f", i=2)), [pb], [B_cvs])
        for fg in range(FT // 2):
            wa = wload(wup_b, B_wup, 0, KT, fg * CW, CW)
            wb_ = wload(wup_b, B_wup, 0, KT, DFF + fg * CW, CW)
            for j in range(2):
                f = fg * 2 + j
                pa, pab = nps()
                mm(pa[:, 0:NT], [(wa[0][:, k, j * 128:(j + 1) * 128], hT2[:, k, 0:NT]) for k in range(KT)], [wa[1], B_h2], [pab])
                pbv, pbb = nps()
                mm(pbv[:, 0:NT], [(wb_[0][:, k, j * 128:(j + 1) * 128], hT2[:, k, 0:NT]) for k in range(KT)], [wb_[1], B_h2], [pbb])
                hist = cv_s[:, :, f, :] if sample else cvst[:, f, :].unsqueeze(1)
                bh = B_cvs if sample else B_cvst
                P.op("dve", lambda e, hist=hist: e.tensor_copy(out=asb[:, :, 0:2], in_=hist), [bh, B_asb], [B_asb])
                P.op("act", lambda e, pa=pa: e.activation(out=asb[:, :, 2:2 + ns], in_=pa[:, 0:NT].rearrange("p (s n) -> p s n", s=nseq),
                                                          func=AF.Copy), [pab, B_asb], [B_asb])
                P.op("act", lambda e, hist=hist: e.activation(out=hist, in_=asb[:, :, ns:ns + 2], func=AF.Copy), [B_asb], [bh])
                a3 = acc.rearrange("p (s n) -> p s n", s=nseq)
                P.op("dve", lambda e, f=f: e.tensor_scalar(out=a3, in0=asb[:, :, 2:2 + ns], scalar1=cw[:, 2, f:f + 1], scalar2=cw[:, 3, f:f + 1],
                                                           op0=ALU.mult, op1=ALU.add), [B_asb, B_const], [B_acc])
                P.op("dve", lambda e, f=f: e.scalar_tensor_tensor(out=a3, in0=asb[:, :, 1:1 + ns], scalar=cw[:, 1, f:f + 1], in1=a3,
                                                                  op0=ALU.mult, op1=ALU.add), [B_asb, B_const, B_acc], [B_acc])
                P.op("dve", lambda e, f=f: e.scalar_tensor_tensor(out=a3, in0=asb[:, :, 0:ns], scalar=cw[:, 0, f:f + 1], in1=a3,
                                                                  op0=ALU.mult, op1=ALU.add), [B_asb, B_const, B_acc], [B_acc])
                P.op("act", lambda e: e.activation(out=gl, in_=acc, func=AF.Gelu_apprx_tanh), [B_acc], [B_gl])
                P.op("dve", lambda e, pbv=pbv, f=f: e.tensor_tensor(out=actT[:, f, :], in0=pbv[:, 0:NT], in1=gl, op=ALU.mult), [pbb, B_gl], [B_act])
        if sample or last:
            for s in range(nseq):
                src = cv_s[:, s, :, :] if sample else cvst[:, :, :]
                bh = B_cvs if sample else B_cvst
                P.op("dve", lambda e, src=src: e.tensor_copy(out=nat[:, 0:88].rearrange("p (i f) -> p i f", i=2),
                                                             in_=src.rearrange("p f i -> p i f")), [bh, B_nat], [B_nat])
                pt, pb = nps()
                P.op("pe", lambda e, pt=pt: e.transpose(pt[0:88, 0:128], nat[:, 0:88], ident[:, :]), [B_nat, B_const], [pb])
                P.op("act", lambda e, pt=pt: e.activation(out=stg[0:88, 0:128], in_=pt[0:88, 0:128], func=AF.Copy), [pb], [B_stg])
                for i in range(2):
                    dst = (o_scv[s, i:i + 1, :] if sample else o_pcv[i:i + 1, :]).rearrange("o (f p) -> (o f) p", p=128)
                    P.dma(dst, stg[i * 44:(i + 1) * 44, 0:128], [B_stg], [], B_stg)
        ck(f"b{tok0 // 512 if not sample else 4}s9")
        P.op("dve", lambda e: e.memset(ssq2, 0.0), [], [B_ssq2])
        for cc in range(D // CW):
            wds = []
            for k0 in range(0, FT, KT):
                nk = min(KT, FT - k0)
                wds.append((k0, nk, wload(wdn_b, B_wdn, k0, nk, cc * CW, CW)))
            for t0_ in range(0, nt, 2):
                ts_ = list(range(t0_, min(nt, t0_ + 2)))
                pts = {t: nps() for t in ts_}
                for (k0, nk, wd) in wds:
                    for t in ts_:
                        def fn(e, t=t, k0=k0, nk=nk, wd=wd, pt=pts[t][0]):
                            ins = None
                            for k in range(nk):
                                ins = e.matmul(pt[:, 0:CW], lhsT=actT[:, k0 + k, t * 128:(t + 1) * 128], rhs=wd[0][:, k, 0:CW],
                                               start=(k0 + k == 0), stop=(k0 + k == FT - 1))
                            return ins
                        P.op("pe", fn, [wd[1], B_act], [pts[t][1]])
                for t in ts_:
                    mch, mt2, B_mch = mchs[mi % 4], mt2s[mi % 4], B_mchs[mi % 4]; mi += 1
                    P.op("act", lambda e, t=t, pt=pts[t][0]: e.activation(out=mch, in_=pt[:, 0:CW], func=AF.Copy), [pts[t][1]], [B_mch])
                    P.op("act", lambda e, t=t, cc=cc: e.activation(out=mt2, in_=mch, func=AF.Square, accum_out=ssq2[:, t, cc:cc + 1]), [B_mch, B_ssq2], [B_mch, B_ssq2])
                    P.dma(mixs[t * 128:(t + 1) * 128, cc * CW:(cc + 1) * CW], mch, [B_mch], [B_mixs], B_mch, par=True, q="act")
            if next_x is not None and cc % 2 == 0:
                tn = cc // 2
                rms_and_T(next_x[0][next_x[1] + tn * 128:next_x[1] + (tn + 1) * 128, :], B_xsrc, tn, hT2, B_h2, 0)
        ck(f"b{tok0 // 512 if not sample else 4}s10")
        def s11(t):
            rows = y_dst[tok0 + t * 128:tok0 + (t + 1) * 128, :]
            post_norm_residual(t, ssq2[:, t, :], B_ssq2, 1, rows, B_y, rows, B_y, False)
        thunks = [(lambda t=t: s11(t)) for t in range(nt)]
        if defer11:
            return thunks
        for th_ in thunks:
            th_()
        return []

    def to_state_b(dst, src48x64, bdst):
        P.dma(nat[0:48, 0:64], src48x64, [], [B_nat], B_nat)
        P.dma(nat[0:48, 64:128], src48x64, [], [B_nat], B_nat)
        pt, pb = nps()
        P.op("pe", lambda e: e.transpose(pt[:, 0:48], nat[0:48, 0:128], ident[0:48, 0:48]), [B_nat, B_const], [pb])
        pv = pt[:, 0:48].rearrange("p (g two) -> p g two", two=2)
        P.op("dve", lambda e: e.tensor_copy(out=dst[0:64, :], in_=pv[0:64, :, 0]), [pb], [bdst])
        P.op("dve", lambda e: e.tensor_copy(out=dst[64:128, :], in_=pv[64:128, :, 1]), [pb], [bdst])

    def from_state(src, bsrc, dst48x64):
        nv2 = nat[:, 0:48].rearrange("p (g two) -> p g two", two=2)
        P.op("dve", lambda e: e.memset(nat[:, 0:48], 0.0), [B_nat], [B_nat])
        P.op("dve", lambda e: e.tensor_copy(out=nv2[0:64, :, 0], in_=src[0:64, :]), [bsrc, B_nat], [B_nat])
        P.op("dve", lambda e: e.tensor_copy(out=nv2[64:128, :, 1], in_=src[64:128, :]), [bsrc, B_nat], [B_nat])
        pt, pb = nps()
        P.op("pe", lambda e: e.transpose(pt[0:48, 0:128], nat[:, 0:48], ident[:, :]), [B_nat, B_const], [pb])
        P.op("dve", lambda e: e.tensor_copy(out=stg[0:48, 64:192], in_=pt[0:48, 0:128]), [pb], [B_stg])
        P.op("dve", lambda e: e.tensor_tensor(out=stg[0:48, 0:64], in0=stg[0:48, 64:128], in1=stg[0:48, 128:192], op=ALU.add), [B_stg], [B_stg])
        P.dma(dst48x64, stg[0:48, 0:64], [B_stg], [], B_stg)

    sb_keep = {"ssq": sb("ssqk", [128, 4, 8])[:], "ssq2": sb("ssqk2", [128, 4, 8])[:]}
    pm = {"mkT": sb("pmkT", [128, 4, 256], BF16), "mvb": sb("pmvb", [128, 2, 512], BF16), "B": P.buf("pmkv")}

    barrier()

    mem_kv_prompt(pm["mkT"], pm["mvb"], pm["B"])
    ck("memkv")

    pend = block(x_s, B_const, y_s, B_ys, 0, 128, [(0, 64), (64, 64)], True, True, True, defer11=True)
    pass0["on"] = False
    for b in range(4):
        pend = block(x_p, B_const, y_p, B_yp, b * 512, 512, [(0, 512)], b == 0, b == 3, False,
                     skip_s1=(b > 0), next_x=((x_p, (b + 1) * 512) if b < 3 else None), pre11=pend, defer11=(b < 3))

    global _P
    _P = P
    P.emit()
    es.close()
    return nc


_CACHE = {}


def _onehot():
    half, max_exact, nb = 16, 8, 32
    i = np.arange(64)[:, None, None]
    kc = np.arange(3)[None, :, None]
    j = np.arange(64)[None, None, :]
    rel = (kc * 64 + j) - 128 - i
    n = np.abs(rel)
    large = max_exact + (np.log(np.maximum(n, 1).astype(np.float32) / max_exact) / math.log(128 / max_exact) * (half - max_exact)).astype(np.int32)
    large = np.minimum(large, half - 1)
    bucket = np.where(rel > 0, half, 0) + np.where(n < max_exact, n, large)
    oh = (bucket[None] == np.arange(nb)[:, None, None, None]).astype(np.float32)
    return np.ascontiguousarray(oh.reshape(nb, 64 * 3 * 64))


def kernel(**inp):
    f = lambda a: np.ascontiguousarray(np.asarray(a, dtype=np.float32))
    if "nc" not in _CACHE:
        _CACHE["nc"] = build_program()
    nc = _CACHE["nc"]
    oh = _onehot()
    shared = {
        "rbt": f(inp["rel_bias_table"]), "oh": oh,
        "g_pm": f(inp["norm_pre_mix"]), "g_qm": f(inp["norm_post_mix"]), "g_pf": f(inp["norm_pre_ffn"]),
        "g_qf": f(inp["norm_post_ffn"]), "g_mem": f(inp["norm_mem"]),
        "w_in": f(inp["w_in"][0]), "sinks": f(inp["attn_sinks"]),
        "a_re": f(inp["ssm_a_re"][0]), "a_im": f(inp["ssm_a_im"][0]), "log_dt": f(inp["ssm_log_dt"]),
        "b_re": f(inp["ssm_b_re"][0]), "b_im": f(inp["ssm_b_im"][0]),
        "cc_re": f(inp["ssm_c_re"][0]), "cc_im": f(inp["ssm_c_im"][0]),
        "ssm_d": f(inp["ssm_d"][0].reshape(6, 128)), "w_glu": f(inp["w_glu"][0]), "w_mkv": f(inp["w_mem_kv"][0]),
        "w_out": f(inp["w_out"][0]), "w_up": f(inp["w_up"][0]), "conv_w": f(inp["conv_w"][0]),
        "conv_b": f(inp["conv_b"]), "w_down": f(inp["w_down"][0]),
    }
    in_maps = []
    for c in range(8):
        m = dict(shared)
        s = slice(2 * c, 2 * c + 2)
        m.update({
            "x_p": f(inp["x_prompt"][c]), "x_s": f(inp["x_sample"][s].reshape(128, D)),
            "c_ak": f(inp["cache_attn_k"][0, s].reshape(2, 128, 256)), "c_av": f(inp["cache_attn_v"][0, s].reshape(2, 128, 256)),
            "c_mk": f(inp["cache_mem_k"][0, s].reshape(2, 256, 512)), "c_mv": f(inp["cache_mem_v"][0, s].reshape(2, 256, 512)),
            "st_re": f(inp["state_ssm_re"][0, s]), "st_im": f(inp["state_ssm_im"][0, s]),
            "st_cv": f(inp["state_conv"][0, s]), "memp": f(inp["mem_prompt"][c]),
        })
        in_maps.append(m)
    res = run_bass_kernel_spmd(nc, in_maps, core_ids=list(range(8))).results
    g = lambda k: np.stack([np.asarray(r[k], dtype=np.float32) for r in res])
    cat = lambda k: np.concatenate([np.asarray(r[k], dtype=np.float32) for r in res], axis=0)
    return (
        g("y_p").reshape(8, 2048, D), cat("y_s").reshape(16, 64, D),
        g("o_pk").reshape(1, 8, 128, NKV, HD), g("o_pv").reshape(1, 8, 128, NKV, HD),
        g("o_pre").reshape(1, 8, 48, 64), g("o_pim").reshape(1, 8, 48, 64), g("o_pcv").reshape(1, 8, 2, DFF),
        g("o_pmk").reshape(1, 8, 256, 4, 128), g("o_pmv").reshape(1, 8, 256, 4, 128),
        cat("o_sk").reshape(1, 16, 128, NKV, HD), cat("o_sv").reshape(1, 16, 128, NKV, HD),
        cat("o_sre").reshape(1, 16, 48, 64), cat("o_sim").reshape(1, 16, 48, 64), cat("o_scv").reshape(1, 16, 2, DFF),
    )
```

```python
import math
from contextlib import ExitStack
import numpy as np
import concourse.bass as bass
import concourse.mybir as mybir
from concourse.bass_utils import run_bass_kernel_spmd

F32 = mybir.dt.float32
BF16 = mybir.dt.bfloat16
I32 = mybir.dt.int32
AF = mybir.ActivationFunctionType
ALU = mybir.AluOpType

D = 2048
KT = 16
NQ, NKV, HD = 12, 4, 64
DFF = 5632
FT = 44
INC = 8704
L = 64
NGP = 24
QPERM = [0, 3, 1, 4, 2, 5, 6, 9, 7, 10, 8, 11]
EPS = 1e-6
CW = 256


import types


def _snap(fn):
    if getattr(fn, "__closure__", None) is None:
        return fn
    cells = []
    for c in fn.__closure__:
        try:
            cells.append(types.CellType(c.cell_contents))
        except ValueError:
            cells.append(c)
    g = types.FunctionType(fn.__code__, fn.__globals__, fn.__name__, fn.__defaults__, tuple(cells))
    g.__kwdefaults__ = fn.__kwdefaults__
    return g


class Buf:
    def __init__(self, name):
        self.name = name
        self.lastw = None
        self.readers = []
        self.sem = None
        self.cnt = 0
        self.pw = []
        self.nobar = name in ("win", "wglu", "wmkv", "wout", "wup", "wdn", "outs")
        self.excl = name.startswith("ps")


class Prog:
    def __init__(self, nc, es):
        self.nc = nc
        self.es = es
        self.ops = {e: [] for e in ("pe", "act", "dve", "pool", "sp")}
        self.count = {e: 0 for e in ("pe", "act", "dve", "pool", "sp")}
        self.esem = {e: es.enter_context(nc.semaphore("s_" + e)) for e in ("pe", "act", "dve", "pool", "sp")}
        self.dsems = []
        self.bufs = {}
        self.stopped = False

    def buf(self, name):
        if name not in self.bufs:
            self.bufs[name] = Buf(name)
        return self.bufs[name]

    def _deps(self, eng, reads, writes):
        deps = set()
        for b in reads:
            if b.lastw is not None:
                deps.add(b.lastw)
            deps.update(b.pw)
            if b.excl:
                for r in b.readers:
                    if not (r[0] == "eng" and r[1] == eng):
                        deps.add(r)
        for b in writes:
            if b.lastw is not None:
                deps.add(b.lastw)
            deps.update(b.pw)
            for r in b.readers:
                deps.add(r)
        if eng == "pe":
            deps = {d for d in deps if not (d[0] == "eng" and d[1] == "pe")}
        return deps

    def _commit(self, me, reads, writes, par=False):
        for b in reads:
            b.readers.append(me)
        for b in writes:
            if par:
                b.pw.append(me)
            else:
                b.lastw = me
                b.pw = []
            b.readers = []

    def op(self, eng, fn, reads=(), writes=()):
        if self.stopped:
            return
        deps = self._deps(eng, reads, writes)
        self.count[eng] += 1
        me = ("eng", eng, self.count[eng])
        self.ops[eng].append((deps, _snap(fn), None))
        self._commit(me, reads, writes)

    def dma(self, out, in_, reads, writes, sembuf, q="sp", par=False):
        if self.stopped:
            return
        deps = self._deps(q, reads, writes)
        if par:
            drop = set()
            for b in writes:
                drop.update(b.pw)
                if b.lastw is not None and b.lastw[0] == "dma" and b.lastw[1] is sembuf and not b.readers:
                    drop.add(b.lastw)
            deps = {d for d in deps if d not in drop}
        if sembuf.sem is None:
            sembuf.sem = self.es.enter_context(self.nc.semaphore("d_" + sembuf.name))
            self.dsems.append(sembuf)
        sembuf.cnt += 16
        me = ("dma", sembuf, sembuf.cnt)
        self.ops[q].append((deps, (lambda e, o=out, i=in_: e.dma_start(out=o, in_=i)), sembuf))
        self._commit(me, reads, writes, par)

    def barrier(self):
        if self.stopped:
            return
        nop = lambda en: en.nop()
        deps = {("eng", e, self.count[e]) for e in ("pe", "act", "dve", "pool") if self.count[e]}
        deps |= {("dma", b, b.cnt) for b in self.dsems if not b.nobar}
        self.count["sp"] += 1
        self.ops["sp"].append((deps, nop, None))
        me = ("eng", "sp", self.count["sp"])
        for e in ("pe", "act", "dve", "pool"):
            self.count[e] += 1
            self.ops[e].append(({me}, nop, None))

    def emit(self):
        nc = self.nc
        with nc.Block() as block:
            def run(engname, e):
                known = {}
                for deps, fn, sembuf in self.ops[engname]:
                    need = {}
                    for d in deps:
                        key = (d[0], d[1] if d[0] == "eng" else id(d[1]))
                        sem = self.esem[d[1]] if d[0] == "eng" else d[1].sem
                        if known.get(key, 0) >= d[2]:
                            continue
                        if key not in need or need[key][1] < d[2]:
                            need[key] = (sem, d[2])
                    for key, (sem, v) in need.items():
                        e.wait_ge(sem, v)
                        known[key] = v
                    ins = fn(e)
                    if sembuf is not None:
                        ins.then_inc(sembuf.sem, 16)
                    else:
                        ins.then_inc(self.esem[engname], 1)
                if engname == "sp":
                    for en in ("pe", "act", "dve", "pool"):
                        if self.count[en]:
                            e.wait_ge(self.esem[en], self.count[en])
                    for b in self.dsems:
                        e.wait_ge(b.sem, b.cnt)

            @block.tensor
            def _(e):
                run("pe", e)

            @block.scalar
            def _(e):
                run("act", e)

            @block.vector
            def _(e):
                run("dve", e)

            @block.gpsimd
            def _(e):
                run("pool", e)

            @block.sync
            def _(e):
                run("sp", e)


STOP = None


def build_program():
    nc = bass.Bass("TRN2", target_bir_lowering=False)
    es = ExitStack()
    P = Prog(nc, es)

    def ck(name):
        if STOP == name:
            P.stopped = True

    def din(name, shape, dt=F32):
        return nc.dram_tensor(name, list(shape), dt, kind="ExternalInput").ap()

    def dout(name, shape):
        return nc.dram_tensor(name, list(shape), F32, kind="ExternalOutput").ap()

    def dscr(name, shape, dt):
        return nc.dram_tensor(name, list(shape), dt).ap()

    x_p = din("x_p", [2048, D]); x_s = din("x_s", [128, D])
    c_ak = din("c_ak", [2, 128, 256]); c_av = din("c_av", [2, 128, 256])
    c_mk = din("c_mk", [2, 256, 512]); c_mv = din("c_mv", [2, 256, 512])
    st_re = din("st_re", [2, 48, 64]); st_im = din("st_im", [2, 48, 64])
    st_cv = din("st_cv", [2, 2, DFF])
    memp = din("memp", [256, D])
    rbt = din("rbt", [32, 12]); oh = din("oh", [32, 64 * 3 * 64])
    g_pm = din("g_pm", [1, D]); g_qm = din("g_qm", [1, D]); g_pf = din("g_pf", [1, D]); g_qf = din("g_qf", [1, D])
    g_mem = din("g_mem", [1, D])
    w_in = din("w_in", [D, INC]); sinks = din("sinks", [1, 12])
    a_re = din("a_re", [48, 64]); a_im = din("a_im", [48, 64]); log_dt = din("log_dt", [1, 48])
    b_re = din("b_re", [48, 64, 16]); b_im = din("b_im", [48, 64, 16])
    cc_re = din("cc_re", [48, 16, 64]); cc_im = din("cc_im", [48, 16, 64])
    ssm_d = din("ssm_d", [6, 128]); w_glu = din("w_glu", [768, 768]); w_mkv = din("w_mkv", [D, 1024])
    w_out = din("w_out", [D, D]); w_up = din("w_up", [D, 2 * DFF]); conv_w = din("conv_w", [3, DFF])
    conv_b = din("conv_b", [1, DFF]); w_down = din("w_down", [DFF, D])
    y_p = dout("y_p", [2048, D]); y_s = dout("y_s", [128, D])
    o_pk = dout("o_pk", [128, 256]); o_pv = dout("o_pv", [128, 256])
    o_pre = dout("o_pre", [48, 64]); o_pim = dout("o_pim", [48, 64]); o_pcv = dout("o_pcv", [2, DFF])
    o_pmk = dout("o_pmk", [256, 512]); o_pmv = dout("o_pmv", [256, 512])
    o_sk = dout("o_sk", [2, 128, 256]); o_sv = dout("o_sv", [2, 128, 256])
    o_sre = dout("o_sre", [2, 48, 64]); o_sim = dout("o_sim", [2, 48, 64]); o_scv = dout("o_scv", [2, 2, DFF])
    win_b = dscr("win_b", [D, INC], BF16); wglu_b = dscr("wglu_b", [768, 768], BF16)
    wmkv_b = dscr("wmkv_b", [D, 1024], BF16); wout_b = dscr("wout_b", [D, D], BF16)
    wup_b = dscr("wup_b", [D, 2 * DFF], BF16); wdn_b = dscr("wdn_b", [DFF, D], BF16)
    mixs = dscr("mixs", [512, D], F32)
    rot_tab = dscr("rot_tab", [NGP, 2, 128, 512], F32)
    sgscr = dscr("sgscr", [D // CW, 4, 128, 3, CW], F32)
    B_win, B_wglu, B_wmkv, B_wout, B_wup, B_wdn, B_mixs = [P.buf(n) for n in
        ("win", "wglu", "wmkv", "wout", "wup", "wdn", "mixs")]
    origmap = {"win": w_in, "wglu": w_glu, "wmkv": w_mkv, "wout": w_out, "wup": w_up, "wdn": w_down}
    B_yp, B_ys = P.buf("yp"), P.buf("ys")
    B_rot = P.buf("rot"); B_sgscr = P.buf("sgscr")
    B_out = P.buf("outs")

    def sb(name, shape, dt=F32):
        return es.enter_context(nc.sbuf_tensor(name, list(shape), dt))

    def ps(name, shape, dt=F32):
        return es.enter_context(nc.psum_tensor(name, list(shape), dt))

    psf = [(ps(f"psf{i}", [128, 512]), P.buf(f"psf{i}")) for i in range(5)]
    psy = (ps("psy", [128, 512]), P.buf("psy"))
    psb = [(ps(f"psb{i}", [128, 1024], BF16), P.buf(f"psb{i}")) for i in range(2)]
    rr = {"f": 0, "b": 0, "w": 0, "ev": 0}

    def nps():
        rr["f"] = (rr["f"] + 1) % 5
        return psf[rr["f"]]

    def npsb():
        rr["b"] = (rr["b"] + 1) % 2
        return psb[rr["b"]]

    NW = 4
    wbufs = [(sb(f"wb{i}", [128, KT, CW], BF16), P.buf(f"wb{i}")) for i in range(NW)]

    pass0 = {"on": True}

    def wload(src, bsrc, k0, nk, c0, ncol):
        rr["w"] = (rr["w"] + 1) % NW
        t, b = wbufs[rr["w"]]
        v = src.rearrange("(kt p) n -> p kt n", p=128)
        if not pass0["on"]:
            P.dma(t[:, 0:nk, 0:ncol], v[:, k0:k0 + nk, c0:c0 + ncol], [bsrc], [b], b)
            return t, b
        orig = origmap[bsrc.name]
        ov = orig.rearrange("(kt p) n -> p kt n", p=128)
        first = [True]

        bsw = P.buf(f"wbsw{rr['w']}"); bst = P.buf(f"wbst{rr['w']}")

        def ld(dst, srcap):
            P.dma(dst, srcap, [], [b], bsw, q="pool", par=not first[0])
            first[0] = False
        if bsrc.name == "win" and c0 < 768:
            for j in range(ncol // 64):
                h = QPERM[(c0 // 64) + j]
                ld(t[:, 0:nk, j * 64:(j + 1) * 64], ov[:, k0:k0 + nk, h * 64:(h + 1) * 64])
        elif bsrc.name == "wout":
            for k in range(k0, k0 + nk):
                if k < 6:
                    for hf in range(2):
                        h = QPERM[2 * k + hf]
                        ld(t[hf * 64:(hf + 1) * 64, k - k0, 0:ncol], orig[h * 64:(h + 1) * 64, c0:c0 + ncol])
            ka = max(k0, 6)
            if ka < k0 + nk:
                ld(t[:, ka - k0:nk, 0:ncol], ov[:, ka:k0 + nk, c0:c0 + ncol])
        else:
            ld(t[:, 0:nk, 0:ncol], ov[:, k0:k0 + nk, c0:c0 + ncol])
        if bsrc.name != "wmkv":
            P.dma(v[:, k0:k0 + nk, c0:c0 + ncol], t[:, 0:nk, 0:ncol], [b], [bsrc], bst, par=True)
        return t, b

    def evac(out, in_, reads, writes, scale=None):
        rr["ev"] ^= 1
        if rr["ev"]:
            if scale is None:
                P.op("act", lambda e: e.activation(out=out, in_=in_, func=AF.Copy), reads, writes)
            else:
                P.op("act", lambda e: e.activation(out=out, in_=in_, func=AF.Copy, scale=scale), reads, writes)
        else:
            if scale is None:
                P.op("dve", lambda e: e.tensor_copy(out=out, in_=in_), reads, writes)
            else:
                P.op("dve", lambda e: e.tensor_scalar(out=out, in0=in_, scalar1=scale, scalar2=None, op0=ALU.mult), reads, writes)

    def mm(out, pairs, reads, writes):
        def fn(e, out=out, pairs=pairs):
            n = len(pairs)
            ins = None
            for i, (l, r) in enumerate(pairs):
                ins = e.matmul(out, lhsT=l, rhs=r, start=(i == 0), stop=(i == n - 1))
            return ins
        P.op("pe", fn, reads, writes)

    ident = sb("ident", [128, 128]); identb = sb("identb", [128, 128], BF16)
    B_const = P.buf("const")
    ones_b = sb("ones_b", [128, 128], BF16)
    gT = sb("gT", [128, 3, KT])
    gbc = sb("gbc", [128, 2, D], BF16)
    cw = sb("cw", [128, 4, FT])
    dT = sb("dT", [128, 6])
    Ddiag = sb("Ddiag", [128, 6, 128], BF16)
    epsb = sb("epsb", [128, 1])
    biasT = sb("biasT", [64, 3, 64, 12])
    esk = sb("esk", [128, 12])
    rS = sb("rS", [128, NGP])
    tabs = [sb(f"tab{i}", [128, 2, 512]) for i in range(2)]; B_tabs = [P.buf("tab0"), P.buf("tab1")]
    Bre = sb("Bre", [128, NGP, 128], BF16); Bim = sb("Bim", [128, NGP, 128], BF16)
    Cre = sb("Cre", [128, NGP, 128], BF16); Cim = sb("Cim", [128, NGP, 128], BF16)
    sst = sb("sst", [128, 2, NGP])
    cvst = sb("cvst", [128, FT, 2])
    kTt = sb("kTt", [128, 2, 128 + 512], BF16)
    v64 = sb("v64", [64, 10, NKV, 128], BF16)
    B_kT, B_v64, B_sst, B_cvst = P.buf("kT"), P.buf("v64"), P.buf("sst"), P.buf("cvst")
    xt = sb("xt", [128, D]); mt = sb("mt", [128, D]); B_xt, B_mt = P.buf("xt"), P.buf("mt")
    hb = sb("hb", [128, D], BF16); B_hb = P.buf("hb")
    stat = sb("stat", [128, 8]); B_stat = P.buf("stat")
    stg = sb("stg", [128, 512]); B_stg = P.buf("stg")
    ARENA = 82 * 1024
    arena = sb("arena", [128, ARENA // 4])
    B_arena = P.buf("arena")
    off = {"v": 0}

    def carve(shape, dt):
        n = 1
        for s in shape[1:]:
            n *= s
        nbytes = n * (2 if dt == BF16 else 4)
        nbytes = (nbytes + 31) // 32 * 32
        o = off["v"]
        assert o + nbytes <= ARENA, (o, nbytes)
        off["v"] = o + nbytes
        flat = arena[0:shape[0], o // 4:(o + nbytes) // 4]
        if dt != F32:
            flat = flat.bitcast(dt)
        flat = flat[:, 0:n]
        if len(shape) == 2:
            return flat
        names = " ".join(f"d{i}" for i in range(1, len(shape)))
        return flat.rearrange(f"p ({names}) -> p {names}", **{f"d{i}": shape[i] for i in range(2, len(shape))})

    def barrier():
        P.barrier()
        off["v"] = 0

    def rewind(mark):
        P.barrier()
        off["v"] = mark

    def cast(dst, src, bdst, rows=None):
        P.dma(dst, src, [], [bdst], bdst, q="pool")

    def casts_a():
        cast(wmkv_b[:, :], w_mkv[:, :], B_wmkv)
        for s_ in range(12):
            h = QPERM[s_]
            cast(win_b[:, s_ * 64:(s_ + 1) * 64], w_in[:, h * 64:(h + 1) * 64], B_win)
        for r in range(4):
            cast(win_b[r * 512:(r + 1) * 512, 768:INC], w_in[r * 512:(r + 1) * 512, 768:INC], B_win)

    def casts_b():
        cast(wglu_b[:, :], w_glu[:, :], B_wglu)
        for s_ in range(12):
            h = QPERM[s_]
            cast(wout_b[s_ * 64:(s_ + 1) * 64, :], w_out[h * 64:(h + 1) * 64, :], B_wout)
        cast(wout_b[768:D, :], w_out[768:D, :], B_wout)
        for r in range(4):
            cast(wup_b[r * 512:(r + 1) * 512, :], w_up[r * 512:(r + 1) * 512, :], B_wup)
        for r in range(4):
            cast(wdn_b[r * 1408:(r + 1) * 1408, :], w_down[r * 1408:(r + 1) * 1408, :], B_wdn)

    ck("cast")
    P.op("dve", lambda e: e.memset(epsb[:], EPS), [], [B_const])
    idc = carve([128, 128], F32); idr = carve([128, 128], F32); idx = None
    nat = sb("nat_p", [128, 128]); B_nat = P.buf("nat")
    P.op("pool", lambda e: e.iota(idc[:], pattern=[[1, 128]], base=0, channel_multiplier=0,
                                  allow_small_or_imprecise_dtypes=True), [], [B_const])
    P.op("pool", lambda e: e.iota(idr[:], pattern=[[0, 128]], base=0, channel_multiplier=1,
                                  allow_small_or_imprecise_dtypes=True), [], [B_const])
    for i, g in enumerate((g_qm, g_qf)):
        P.dma(gbc[:, i, :], g.broadcast_to([128, D]), [], [B_const], B_const, q="pool")
    P.op("dve", lambda e: e.tensor_tensor(out=ident[:], in0=idc[:], in1=idr[:], op=ALU.is_equal), [B_const], [B_const])
    P.op("dve", lambda e: e.tensor_copy(out=identb[:], in_=ident[:]), [B_const], [B_const])
    P.op("dve", lambda e: e.memset(ones_b[:], 1.0), [], [B_const])
    P.op("dve", lambda e: e.memset(sst[:], 0.0), [], [B_sst])
    P.op("dve", lambda e: e.memset(cvst[:], 0.0), [], [B_cvst])


    def load_T(dst, src_rows_ap, nrows, ncols_in=128):
        P.dma(nat[0:nrows, 0:128], src_rows_ap, [], [B_nat], B_nat)
        pt, pb = nps()
        P.op("pe", lambda e: e.transpose(pt[:, 0:nrows], nat[0:nrows, 0:128], ident[0:nrows, 0:nrows]),
             [B_nat, B_const], [pb])
        P.op("dve", lambda e: e.tensor_copy(out=dst, in_=pt[:, 0:nrows]), [pb], [B_const])

    for i, g in enumerate((g_pm, g_pf, g_mem)):
        load_T(gT[:, i, :], g.rearrange("o (k p) -> (o k) p", p=128), 16)
    for i in range(3):
        load_T(cw[:, i, :], conv_w[i:i + 1, :].rearrange("o (k p) -> (o k) p", p=128), FT)
    load_T(cw[:, 3, :], conv_b.rearrange("o (k p) -> (o k) p", p=128), FT)
    load_T(dT[:, :], ssm_d[:, :], 6)
    for t in range(6):
        P.op("dve", lambda e, t=t: e.tensor_scalar(out=Ddiag[:, t, :], in0=ident[:], scalar1=dT[:, t:t + 1],
                                                   scalar2=None, op0=ALU.mult), [B_const], [B_const])
    P.dma(nat[:, 0:12], sinks.broadcast_to([128, 12]), [], [B_nat], B_nat)
    P.op("act", lambda e: e.activation(out=esk[:], in_=nat[:, 0:12], func=AF.Exp), [B_nat], [B_const])

    oht = carve([32, 64 * 3 * 64], F32); B_oh = P.buf("oh")
    tb = carve([32, 12], F32)
    P.dma(oht[:, :], oh[:, :], [], [B_oh], B_oh)
    P.dma(tb[:, :], rbt[:, :], [], [B_oh], B_oh)
    ohv = oht.rearrange("b (i k j) -> b i k j", i=64, k=3)
    for kc in range(3):
        for ih in range(2):
            pt, pb = nps()
            def fn(e, pt=pt, kc=kc, ih=ih):
                ins = None
                for ii in range(32):
                    i = ih * 32 + ii
                    ins = e.matmul(pt[0:64, ii * 12:(ii + 1) * 12], lhsT=ohv[:, i, kc, :], rhs=tb[:, :],
                                   start=True, stop=True)
                return ins
            P.op("pe", fn, [B_oh], [pb])
            P.op("dve", lambda e, pt=pt, kc=kc, ih=ih: e.tensor_copy(
                out=biasT[:, kc, ih * 32:(ih + 1) * 32, :].rearrange("p i h -> p (i h)"), in_=pt[0:64, 0:384]),
                [pb], [B_const])

    ck("const")
    barrier()

    def to_state(dst, src48x64):
        P.dma(nat[0:48, 0:64], src48x64, [], [B_nat], B_nat)
        P.dma(nat[0:48, 64:128], src48x64, [], [B_nat], B_nat, par=True)
        pt, pb = nps()
        P.op("pe", lambda e: e.transpose(pt[:, 0:48], nat[0:48, 0:128], ident[0:48, 0:48]), [B_nat, B_const], [pb])
        pv = pt[:, 0:48].rearrange("p (g two) -> p g two", two=2)
        P.op("dve", lambda e: e.tensor_copy(out=dst[0:64, :], in_=pv[0:64, :, 0]), [pb], [B_const])
        P.op("dve", lambda e: e.tensor_copy(out=dst[64:128, :], in_=pv[64:128, :, 1]), [pb], [B_const])

    lre = carve([128, NGP], F32); lim = carve([128, NGP], F32); dts = carve([128, NGP], F32)
    th = carve([128, NGP], F32)
    to_state(lre, a_re[:, :]); to_state(lim, a_im[:, :])
    P.dma(nat[:, 0:48], log_dt.broadcast_to([128, 48]), [], [B_nat], B_nat)
    nv = nat[:, 0:48].rearrange("p (g two) -> p g two", two=2)
    P.op("act", lambda e: e.activation(out=dts[0:64, :], in_=nv[0:64, :, 0], func=AF.Exp), [B_nat], [B_const])
    P.op("act", lambda e: e.activation(out=dts[64:128, :], in_=nv[64:128, :, 1], func=AF.Exp), [B_nat], [B_const])
    P.op("dve", lambda e: e.tensor_tensor(out=th[:, :], in0=lim, in1=dts, op=ALU.mult), [B_const], [B_const])
    tmpa = carve([128, NGP], F32)
    P.op("dve", lambda e: e.tensor_tensor(out=tmpa, in0=lre, in1=dts, op=ALU.mult), [B_const], [B_const])
    P.op("act", lambda e: e.activation(out=rS[:, :], in_=tmpa, func=AF.Exp), [B_const], [B_const])
    idx = carve([128, 512], F32)
    P.op("pool", lambda e: e.iota(idx, pattern=[[1, 512]], base=1, channel_multiplier=0,
                                  allow_small_or_imprecise_dtypes=True), [], [B_const])
    LIM = 3.14159
    c0t = carve([128, NGP], F32); s0t = carve([128, NGP], F32)
    tmps = [[carve([128, 512], F32), carve([128, 512], I32), carve([128, 512], F32), carve([128, 512], F32)] for _ in range(3)]
    B_tt = [P.buf("tt0"), P.buf("tt1"), P.buf("tt2")]
    it = 0
    for gp in range(NGP):
        for fn_i, shift in ((0, math.pi / 2), (1, 0.0)):
            ang, ki, kf, sn = tmps[it % 3]; Bt = B_tt[it % 3]; it += 1
            P.op("dve", lambda e, ang=ang, gp=gp, shift=shift: e.tensor_scalar(out=ang, in0=idx, scalar1=th[:, gp:gp + 1], scalar2=shift,
                                                                              op0=ALU.mult, op1=ALU.add), [B_const, Bt], [Bt])
            P.op("dve", lambda e, ang=ang, ki=ki: e.tensor_scalar(out=ki, in0=ang, scalar1=1.0 / (2 * math.pi), scalar2=None, op0=ALU.mult), [Bt], [Bt])
            P.op("dve", lambda e, kf=kf, ki=ki: e.tensor_copy(out=kf, in_=ki), [Bt], [Bt])
            P.op("dve", lambda e, ang=ang, kf=kf: e.scalar_tensor_tensor(out=ang, in0=kf, scalar=-2 * math.pi, in1=ang, op0=ALU.mult, op1=ALU.add), [Bt], [Bt])
            P.op("dve", lambda e, ang=ang: e.tensor_scalar(out=ang, in0=ang, scalar1=-LIM, scalar2=LIM, op0=ALU.max, op1=ALU.min), [Bt], [Bt])
            P.op("act", lambda e, ang=ang, sn=sn: e.activation(out=sn, in_=ang, func=AF.Sin), [Bt], [Bt])
            dstc = c0t if fn_i == 0 else s0t
            P.op("act", lambda e, sn=sn, dstc=dstc, gp=gp: e.activation(out=dstc[:, gp:gp + 1], in_=sn[:, 0:1], func=AF.Copy), [Bt], [B_const])
            P.dma(rot_tab[gp, fn_i], sn, [Bt], [B_rot], Bt)
    nre = carve([128, NGP], F32); nim = carve([128, NGP], F32); den = carve([128, NGP], F32)
    cre = carve([128, NGP], F32); cim = carve([128, NGP], F32); t1 = carve([128, NGP], F32); t2 = carve([128, NGP], F32)
    V = lambda fn, r=(B_const,), w=(B_const,): P.op("dve", fn, list(r), list(w))
    V(lambda e: e.tensor_tensor(out=nre, in0=rS[:, :], in1=c0t, op=ALU.mult))
    V(lambda e: e.tensor_scalar(out=nre, in0=nre, scalar1=-1.0, scalar2=None, op0=ALU.add))
    V(lambda e: e.tensor_tensor(out=nim, in0=rS[:, :], in1=s0t, op=ALU.mult))
    V(lambda e: e.tensor_tensor(out=den, in0=lre, in1=lre, op=ALU.mult))
    V(lambda e: e.tensor_tensor(out=t1, in0=lim, in1=lim, op=ALU.mult))
    V(lambda e: e.tensor_tensor(out=den, in0=den, in1=t1, op=ALU.add))
    V(lambda e: e.reciprocal(out=den, in_=den))
    V(lambda e: e.tensor_tensor(out=t1, in0=nre, in1=lre, op=ALU.mult))
    V(lambda e: e.tensor_tensor(out=t2, in0=nim, in1=lim, op=ALU.mult))
    V(lambda e: e.tensor_tensor(out=t1, in0=t1, in1=t2, op=ALU.add))
    V(lambda e: e.tensor_tensor(out=cre, in0=t1, in1=den, op=ALU.mult))
    V(lambda e: e.tensor_tensor(out=t1, in0=nim, in1=lre, op=ALU.mult))
    V(lambda e: e.tensor_tensor(out=t2, in0=nre, in1=lim, op=ALU.mult))
    V(lambda e: e.tensor_tensor(out=t1, in0=t1, in1=t2, op=ALU.subtract))
    V(lambda e: e.tensor_tensor(out=cim, in0=t1, in1=den, op=ALU.mult))
    Zre = carve([128, NGP, 128], F32); Zim = carve([128, NGP, 128], F32); Zt = carve([128, NGP, 128], F32)
    B_Z = P.buf("Z")
    P.op("dve", lambda e: e.memset(Zre, 0.0), [], [B_Z])
    P.op("dve", lambda e: e.memset(Zim, 0.0), [], [B_Z])
    for (Z, src) in ((Zre, b_re), (Zim, b_im)):
        sv = src.rearrange("(t j two) p c -> two j p t c", j=4, two=2)
        Zv = Z.rearrange("p (t j) m -> p j t m", j=4)
        for gpar in range(2):
            for j in range(4):
                c0 = 32 * j + 16 * gpar
                P.dma(Zv[64 * gpar:64 * gpar + 64, j, :, c0:c0 + 16], sv[gpar, j], [], [B_Z], B_Z, par=True)
    bc = lambda t: t.unsqueeze(2).to_broadcast([128, NGP, 128])
    VZ = lambda fn: P.op("dve", fn, [B_Z, B_const], [B_Z])
    VZ(lambda e: e.tensor_tensor(out=Zt, in0=Zim, in1=bc(cim), op=ALU.mult))
    VZ(lambda e: e.tensor_tensor(out=Zim, in0=Zim, in1=bc(cre), op=ALU.mult))
    Zt2 = carve([128, NGP, 128], F32)
    VZ(lambda e: e.tensor_tensor(out=Zt2, in0=Zre, in1=bc(cim), op=ALU.mult))
    VZ(lambda e: e.tensor_tensor(out=Zim, in0=Zim, in1=Zt2, op=ALU.add))
    VZ(lambda e: e.tensor_tensor(out=Zre, in0=Zre, in1=bc(cre), op=ALU.mult))
    VZ(lambda e: e.tensor_tensor(out=Zre, in0=Zre, in1=Zt, op=ALU.subtract))
    for (Z, dst) in ((Zre, Bre), (Zim, Bim)):
        for gp in range(NGP):
            pt, pb = nps()
            P.op("pe", lambda e, pt=pt, Z=Z, gp=gp: e.transpose(pt[:, 0:128], Z[:, gp, :], ident[:, :]), [B_Z, B_const], [pb])
            evac(dst[:, gp, :], pt[:, 0:128], [pb], [B_const])
    B_W = P.buf("W")
    for (src, dst, sc) in ((cc_re, Cre, None), (cc_im, Cim, -1.0)):
        Wt = Zt if src is cc_re else Zt2
        P.op("dve", lambda e, Wt=Wt: e.memset(Wt, 0.0), [B_Z], [B_W, B_Z])
        sv = src.rearrange("(t j two) c p -> two j c t p", j=4, two=2)
        Wv = Wt.rearrange("p (t j) m -> p j t m", j=4)
        for gpar in range(2):
            for j in range(4):
                r0 = 32 * j + 16 * gpar
                P.dma(Wv[r0:r0 + 16, j, :, 64 * gpar:64 * gpar + 64], sv[gpar, j], [], [B_W], B_W, par=True)
        for gp in range(NGP):
            pt, pb = nps()
            P.op("pe", lambda e, pt=pt, Wt=Wt, gp=gp: e.transpose(pt[:, 0:128], Wt[:, gp, :], ident[:, :]), [B_W, B_const], [pb])
            evac(dst[:, gp, :], pt[:, 0:128], [pb], [B_const], scale=sc)

    ck("ssmsetup")
    def rms_and_T(src_rows, bsrc, ntile, hT, B_hT, gidx, ntok_tile=128, extra=None):
        xb_, Bx_ = (xt, B_xt) if ntile % 2 == 0 else (mt, B_mt)
        P.dma(xb_[0:ntok_tile, :], src_rows, [bsrc], [Bx_], Bx_)
        norm_T(xb_, Bx_, ntile, hT, B_hT, gidx, ntok_tile)

    def rstd_from(col_in, col_out, n):
        P.op("act", lambda e: e.activation(out=stat[0:n, col_out:col_out + 1], in_=stat[0:n, col_in:col_in + 1], func=AF.Ln,
                                           scale=1.0 / D, bias=epsb[0:n, :]), [B_stat, B_const], [B_stat])
        P.op("act", lambda e: e.activation(out=stat[0:n, col_out:col_out + 1], in_=stat[0:n, col_out:col_out + 1], func=AF.Exp,
                                           scale=-0.5), [B_stat], [B_stat])

    ncall = {"n": 0}

    def norm_T(src, bsrc, ntile, hT, B_hT, gidx, n=128):
        ncall["n"] += 1
        ck(f"c{ncall['n']}_n0")
        P.op("dve", lambda e: e.memset(stat[:, 0:2], 0.0), [], [B_stat])
        P.op("act", lambda e: e.activation(out=hb[0:n, :], in_=src[0:n, :], func=AF.Square, accum_out=stat[0:n, 0:1]),
             [bsrc, B_stat], [B_hb, B_stat])
        rstd_from(0, 1, n)
        P.op("dve", lambda e: e.tensor_scalar(out=hb[0:n, :], in0=src[0:n, :], scalar1=stat[0:n, 1:2], scalar2=None,
                                              op0=ALU.mult), [bsrc, B_stat], [B_hb])
        ck(f"c{ncall['n']}_n1")
        for q4 in range(2):
            pt, pb = npsb()
            def fn(e, pt=pt, q4=q4):
                ins = None
                for j in range(8):
                    kt = q4 * 8 + j
                    ins = e.transpose(pt[:, j * 128:j * 128 + n], hb[0:n, kt * 128:(kt + 1) * 128], identb[0:n, 0:n])
                return ins
            P.op("pe", fn, [B_hb, B_const], [pb])
            ck(f"c{ncall['n']}_n2")
            for j in range(8):
                kt = q4 * 8 + j
                evac(hT[:, kt, ntile * 128:ntile * 128 + n], pt[:, j * 128:j * 128 + n], [pb, B_const], [B_hT],
                     scale=gT[:, gidx, kt:kt + 1])
                ck(f"c{ncall['n']}_n3_{q4}_{j}")

    def proj_fm(dst_fn, src, bsrc, c0, ncols, hT, B_hT, ntok, nk=KT, k0=0, tok0=0):
        for cb in range(0, ncols, CW):
            ncb = min(CW, ncols - cb)
            wt, wb = wload(src, bsrc, k0, nk, c0 + cb, ncb)
            for ct in range(ncb // 128):
                pt, pb = nps()
                mm(pt[:, 0:ntok], [(wt[:, k, ct * 128:(ct + 1) * 128], hT[:, k, tok0:tok0 + ntok]) for k in range(nk)],
                   [wb, B_hT], [pb])
                dst, bd = dst_fn((cb // 128) + ct)
                evac(dst, pt[:, 0:ntok], [pb], [bd])

    def mem_kv_prompt(mkT, mvb, B_mkv):
        hmT = carve([128, KT, 256], BF16); B_hm = P.buf("hmT")
        for t in range(2):
            rms_and_T(memp[t * 128:(t + 1) * 128, :], B_const, t, hmT, B_hm, 2)
        ck("mk1")
        for cb in range(0, 1024, CW):
            wt, wb = wload(wmkv_b, B_wmkv, 0, KT, cb, CW)
            ck("mk1a")
            for t in range(2):
                pt, pb = nps()
                mm(pt[:, 0:CW], [(hmT[:, k, t * 128:(t + 1) * 128], wt[:, k, 0:CW]) for k in range(KT)], [wb, B_hm], [pb])
                ck("mk1b")
                P.op("act", lambda e, pt=pt: e.activation(out=stg[:, 0:CW], in_=pt[:, 0:CW], func=AF.Copy), [pb], [B_stg])
                ck("mk1c")
                dst = o_pmk if cb < 512 else o_pmv
                P.dma(dst[t * 128:(t + 1) * 128, (cb % 512):(cb % 512) + CW], stg[:, 0:CW], [B_stg], [], B_stg)
                ck("mk1d")
                if cb >= 512:
                    P.op("dve", lambda e, pt=pt, t=t, cb=cb: e.tensor_copy(out=mvb[:, t, cb - 512:cb - 512 + CW], in_=pt[:, 0:CW]),
                         [pb], [B_mkv])
                ck(f"mk_{cb}_{t}")
        ck("mk2")
        proj_fm(lambda ct: (mkT[:, ct, :], B_mkv), wmkv_b, B_wmkv, 0, 512, hmT, B_hm, 256)

    def mem_kv_sample(mkT, mvb, B_mkv, s):
        f = carve([128, 2, 512], F32); fb = carve([128, 2, 512], BF16); B_f = P.buf("mkf")
        P.dma(f, c_mk[s].rearrange("(t p) n -> p t n", p=128), [], [B_f], B_f)
        P.op("dve", lambda e: e.tensor_copy(out=fb, in_=f), [B_f], [B_f])
        for hh in range(4):
            pt, pb = npsb()
            def fn(e, pt=pt, hh=hh):
                ins = None
                for t in range(2):
                    ins = e.transpose(pt[:, t * 128:(t + 1) * 128], fb[:, t, hh * 128:(hh + 1) * 128], identb[:, :])
                return ins
            P.op("pe", fn, [B_f, B_const], [pb])
            evac(mkT[:, hh, :], pt[:, 0:256], [pb], [B_mkv])
        f2 = carve([128, 2, 512], F32); B_f2 = P.buf("mvf")
        P.dma(f2, c_mv[s].rearrange("(t p) n -> p t n", p=128), [], [B_f2], B_f2)
        P.op("act", lambda e: e.activation(out=mvb, in_=f2, func=AF.Copy), [B_f2], [B_mkv])

    def block(x_src, B_xsrc, y_dst, B_y, tok0, NT, seqs, first, last, sample, skip_s1=False, next_x=None, pre11=None, defer11=False):
        nt = NT // 128
        barrier()
        hT = carve([128, KT, NT], BF16); B_hT = P.buf("hT")
        uT = carve([128, 6, NT], BF16)
        B_q, B_u, B_qm = P.buf("qT"), P.buf("uT"), P.buf("qmT")
        oaT = carve([128, 6, NT], BF16); osT = carve([128, 6, NT], BF16); omT = carve([128, 4, NT], BF16)
        B_oa, B_os, B_om = P.buf("oaT"), P.buf("osT"), P.buf("omT")
        yT = carve([128, 6, NT], BF16); B_yT = P.buf("yT")
        nseq = len(seqs)
        mkTs = [carve([128, 4, 256], BF16) if sample else None for _ in range(nseq)]
        mvbs = [carve([128, 2, 512], BF16) if sample else None for _ in range(nseq)]
        B_mkvs = [P.buf("mkv") for _ in range(nseq)]
        markS = off["v"]
        qT = carve([128, 6, NT], BF16); qmT = carve([128, 4, NT], BF16)
        mark0 = off["v"]
        if sample:
            for s in range(nseq):
                mem_kv_sample(mkTs[s], mvbs[s], B_mkvs[s], s)
        else:
            mkTs[0], mvbs[0], B_mkvs[0] = pm["mkT"], pm["mvb"], pm["B"]
        if not skip_s1:
            for t in range(nt):
                rms_and_T(x_src[tok0 + t * 128:tok0 + (t + 1) * 128, :], B_xsrc, t, hT, B_hT, 0)
        ck(f"b{tok0 // 512 if not sample else 4}s1")
        if sample:
            for s in range(nseq):
                kf32 = carve([128, 256], F32); kb16 = carve([128, 256], BF16); Bk = P.buf("kc")
                P.dma(kf32, c_ak[s], [], [Bk], Bk)
                P.op("dve", lambda e, kb16=kb16, kf32=kf32: e.tensor_copy(out=kb16, in_=kf32), [Bk], [Bk])
                pt, pb = npsb()
                def fn(e, pt=pt, kb16=kb16):
                    ins = None
                    for t in range(2):
                        ins = e.transpose(pt[:, t * 128:(t + 1) * 128], kb16[:, t * 128:(t + 1) * 128], identb[:, :])
                    return ins
                P.op("pe", fn, [Bk, B_const], [pb])
                evac(kTt[:, :, s * 192:s * 192 + 128], pt[:, 0:256].rearrange("p (t n) -> p t n", t=2), [pb], [B_kT])
                vf32 = carve([64, 2, 256], F32); Bv = P.buf("vc")
                P.dma(vf32, c_av[s].rearrange("(c p) n -> p c n", p=64), [], [Bv], Bv)
                vv = vf32.rearrange("p c (h d) -> p c h d", h=NKV)
                for dup in range(2):
                    P.op("dve", lambda e, s=s, dup=dup, vv=vv: e.tensor_copy(out=v64[:, s * 3:s * 3 + 2, :, dup * 64:(dup + 1) * 64], in_=vv),
                         [Bv], [B_v64])
                P.dma(o_sk[s, 0:64, :], c_ak[s, 64:128, :], [], [], B_out)
                P.dma(o_sv[s, 0:64, :], c_av[s, 64:128, :], [], [], B_out)
        if sample:
            kcol = [s * 192 + 128 for s in range(nseq)]
            vch = [s * 3 + 2 for s in range(nseq)]
        else:
            kcol = [128]
            vch = [2]
        def dst_q(ct):
            return qT[:, ct, :], B_q
        pre11 = list(pre11 or [])
        proj_fm(dst_q, win_b, B_win, 0, 768, hT, B_hT, NT)
        if pre11:
            pre11.pop(0)()
        for cb in range(0, 256, CW):
            wt, wb = wload(win_b, B_win, 0, KT, 768 + cb, CW)
            for ct in range(2):
                pt, pb = nps()
                mm(pt[:, 0:NT], [(wt[:, k, ct * 128:(ct + 1) * 128], hT[:, k, 0:NT]) for k in range(KT)], [wb, B_hT], [pb])
                for si, (c0, n) in enumerate(seqs):
                    evac(kTt[:, ct, kcol[si]:kcol[si] + n], pt[:, c0:c0 + n], [pb], [B_kT])
            if last or sample:
                for si, (c0, n) in enumerate(seqs):
                    r0, nr = (c0 + n - 128, 128) if not sample else (c0, 64)
                    pt, pb = nps()
                    mm(pt[0:nr, 0:256], [(hT[:, k, r0:r0 + nr], wt[:, k, 0:256]) for k in range(KT)], [wb, B_hT], [pb])
                    P.op("act", lambda e, pt=pt, nr=nr: e.activation(out=stg[0:nr, 0:256], in_=pt[0:nr, 0:256], func=AF.Copy), [pb], [B_stg])
                    dst = o_sk[si, 64:128, :] if sample else o_pk[:, :]
                    P.dma(dst, stg[0:nr, 0:256], [B_stg], [], B_stg)
        wt, wb = wload(win_b, B_win, 0, KT, 1024, CW)
        for si, (c0, n) in enumerate(seqs):
            for c in range(n // 64):
                pt, pb = nps()
                mm(pt[0:64, 0:256], [(hT[:, k, c0 + c * 64:c0 + (c + 1) * 64], wt[:, k, 0:256]) for k in range(KT)], [wb, B_hT], [pb])
                pv = pt[0:64, 0:256].rearrange("p (h d) -> p h d", h=NKV)
                for dup in range(2):
                    evac(v64[:, vch[si] + c, :, dup * 64:(dup + 1) * 64], pv, [pb], [B_v64])
                is_out = sample or (last and c >= n // 64 - 2)
                if is_out:
                    P.op("act", lambda e, pt=pt: e.activation(out=stg[0:64, 256:512], in_=pt[0:64, 0:256], func=AF.Copy), [pb], [B_stg])
                    if sample:
                        dst = o_sv[si, 64:128, :]
                    else:
                        cc = c - (n // 64 - 2)
                        dst = o_pv[cc * 64:(cc + 1) * 64, :]
                    P.dma(dst, stg[0:64, 256:512], [B_stg], [], B_stg)
        if pre11:
            pre11.pop(0)()
        proj_fm(lambda ct: (uT[:, ct, :], B_u), win_b, B_win, 1280, 768, hT, B_hT, NT)
        if pre11:
            pre11.pop(0)()
        proj_fm(lambda ct: (qmT[:, ct, :], B_qm), win_b, B_win, 2048, 512, hT, B_hT, NT)
        while pre11:
            pre11.pop(0)()

        ck(f"b{tok0 // 512 if not sample else 4}s2")
        if off["v"] != mark0:
            rewind(mark0)
        scs = [carve([64, 3, 192], F32) for _ in range(2)]
        pTs = [carve([64, 3, 192], BF16) for _ in range(2)]
        dns = [carve([128, 192], F32) for _ in range(2)]
        B_scs = [[P.buf(f"sc{a_}{k_}") for k_ in range(3)] for a_ in range(2)]
        B_pTs = [[P.buf(f"pT{a_}{k_}") for k_ in range(3)] for a_ in range(2)]
        B_dns = [P.buf("dn0"), P.buf("dn1")]
        ai = 0
        for si, (c0, n) in enumerate(seqs):
            nch = n // 64
            for c in range(nch):
                kcs = [kc for kc in range(3) if sample or (not first) or (c - 2 + kc) >= 0]
                for h in range(NKV):
                    sc, pT, dn = scs[ai % 2], pTs[ai % 2], dns[ai % 2]
                    B_sc, B_pT, B_dn = B_scs[ai % 2], B_pTs[ai % 2], B_dns[ai % 2]
                    ai += 1
                    half = (h % 2) * 64
                    kt_ = h // 2
                    t0 = (h // 2) * 3
                    rhs = qT[half:half + 64, t0:t0 + 3, c0 + c * 64:c0 + (c + 1) * 64]
                    pss = []
                    for kc in kcs:
                        kk = kcol[si] + (c - 2 + kc) * 64
                        pt, pb = nps()
                        mm(pt[0:64, 0:192], [(kTt[half:half + 64, kt_, kk:kk + 64], rhs)], [B_kT, B_q], [pb])
                        pss.append((kc, pt, pb))
                    for kc, pt, pb in pss:
                        bv = biasT[:, kc, :, 3 * h:3 * h + 3].rearrange("p i g -> p g i")
                        P.op("dve", lambda e, pt=pt, kc=kc, bv=bv, sc=sc: e.scalar_tensor_tensor(
                            out=sc[:, kc, :].rearrange("p (g i) -> p g i", g=3), in0=pt[0:64, 0:192].rearrange("p (g i) -> p g i", g=3),
                            scalar=HD ** -0.5, in1=bv, op0=ALU.mult, op1=ALU.add), [pb, B_const], [B_sc[kc]])
                    for kc, pt, pb in pss:
                        P.op("act", lambda e, kc=kc, sc=sc, pT=pT: e.activation(out=pT[:, kc, :], in_=sc[:, kc, :], func=AF.Exp), [B_sc[kc]], [B_pT[kc]])
                    po, pob = nps()
                    mm(po[:, 0:192], [(v64[:, vch[si] + c - 2 + kc, h, :], pT[:, kc, :]) for kc in kcs], [B_v64] + [B_pT[kc] for kc in kcs], [pob])
                    pd, pdb = nps()
                    mm(pd[:, 0:192], [(ones_b[0:64, :], pT[:, kc, :]) for kc in kcs], [B_const] + [B_pT[kc] for kc in kcs], [pdb])
                    P.op("dve", lambda e, pd=pd, h=h, half=half, dn=dn: e.tensor_tensor(
                        out=dn[half:half + 64, :].rearrange("p (g i) -> p g i", g=3),
                        in0=pd[half:half + 64, 0:192].rearrange("p (g i) -> p g i", g=3),
                        in1=esk[half:half + 64, 3 * h:3 * h + 3].unsqueeze(2).to_broadcast([64, 3, 64]), op=ALU.add),
                        [pdb, B_const], [B_dn])
                    P.op("dve", lambda e, half=half, dn=dn: e.reciprocal(out=dn[half:half + 64, :], in_=dn[half:half + 64, :]), [B_dn], [B_dn])
                    P.op("dve", lambda e, po=po, half=half, t0=t0, c=c, c0=c0, dn=dn: e.tensor_tensor(
                        out=oaT[half:half + 64, t0:t0 + 3, c0 + c * 64:c0 + (c + 1) * 64],
                        in0=po[half:half + 64, 0:192].rearrange("p (g i) -> p g i", g=3),
                        in1=dn[half:half + 64, :].rearrange("p (g i) -> p g i", g=3), op=ALU.mult), [pob, B_dn], [B_oa])
        if not sample and not last:
            P.op("dve", lambda e: e.tensor_copy(out=kTt[:, :, 0:128], in_=kTt[:, :, NT:NT + 128]), [B_kT], [B_kT])
            P.op("dve", lambda e: e.tensor_copy(out=v64[:, 0:2, :, :], in_=v64[:, 8:10, :, :]), [B_v64], [B_v64])

        ck(f"b{tok0 // 512 if not sample else 4}s3")
        pm_ = carve([128, 2, 512], BF16); B_pm = P.buf("pmem")
        rcp = carve([128, 512], F32); B_rcp = P.buf("rcp")
        for si, (c0, n) in enumerate(seqs):
            for hh in range(4):
                for mtile in range(2):
                    pt, pb = nps()
                    mm(pt[:, 0:n], [(mkTs[si][:, hh, mtile * 128:(mtile + 1) * 128], qmT[:, hh, c0:c0 + n])], [B_mkvs[si], B_qm], [pb])
                    P.op("act", lambda e, pt=pt, mtile=mtile, n=n: e.activation(out=pm_[:, mtile, 0:n], in_=pt[:, 0:n], func=AF.Exp,
                                                                           scale=128 ** -0.5), [pb], [B_pm])
                po, pob = nps()
                mm(po[:, 0:n], [(mvbs[si][:, mtile, hh * 128:(hh + 1) * 128], pm_[:, mtile, 0:n]) for mtile in range(2)], [B_mkvs[si], B_pm], [pob])
                pd, pdb = nps()
                mm(pd[:, 0:n], [(ones_b[:, :], pm_[:, mtile, 0:n]) for mtile in range(2)], [B_const, B_pm], [pdb])
                P.op("dve", lambda e, pd=pd, n=n: e.reciprocal(out=rcp[:, 0:n], in_=pd[:, 0:n]), [pdb], [B_rcp])
                P.op("dve", lambda e, po=po, n=n, hh=hh, c0=c0: e.tensor_tensor(out=omT[:, hh, c0:c0 + n], in0=po[:, 0:n], in1=rcp[:, 0:n],
                                                                         op=ALU.mult), [pob, B_rcp], [B_om])

        ck(f"b{tok0 // 512 if not sample else 4}s4")
        rewind(markS)
        tq2 = [[carve([128, NT], F32) for _ in range(2)] for _ in range(2)]
        rin2 = [[carve([128, NT], F32) for _ in range(2)] for _ in range(2)]
        w2 = [[carve([128, NT], F32) for _ in range(2)] for _ in range(2)]
        pq2 = tq2
        sbf2 = [[carve([128, NT], BF16) for _ in range(2)] for _ in range(2)]
        cr2 = [carve([128, 4], F32) for _ in range(2)]
        B_tq2 = [P.buf("tqa"), P.buf("tqb")]; B_rin2 = [P.buf("rina"), P.buf("rinb")]; B_pq2 = B_tq2
        B_w2 = [P.buf("w2a"), P.buf("w2b")]; B_sb2 = [P.buf("sb2a"), P.buf("sb2b")]; B_cr2 = [P.buf("cr2a"), P.buf("cr2b")]
        ns_ = NT // nseq
        if sample:
            sst_s = [carve([128, 2, NGP], F32) for _ in range(nseq)]
            B_ssts = [P.buf("ssts0"), P.buf("ssts1")]
            for s in range(nseq):
                to_state_b(sst_s[s][:, 0, :], st_re[s], B_ssts[s])
                to_state_b(sst_s[s][:, 1, :], st_im[s], B_ssts[s])
        v3 = lambda t: t.rearrange("p (s n) -> p s n", s=nseq)

        def gp_stages(tile_, g4, par):
            gp = tile_ * 4 + g4
            tab, Btab = tabs[par], B_tabs[par]
            w_, Bw = w2[par], B_w2[par]; sbf, Bsb = sbf2[par], B_sb2[par]; cr, Bcr = cr2[par], B_cr2[par]
            rin, B_rin = rin2[par], B_rin2[par]; tq, B_tq = tq2[par], B_tq2[par]
            py, pyb = psy
            cosb = tab[:, 0, 0:ns_].unsqueeze(1).to_broadcast([128, nseq, ns_])
            sinb = tab[:, 1, 0:ns_].unsqueeze(1).to_broadcast([128, nseq, ns_])
            hold = {}
            S = lambda fn, r, w: (lambda: P.op("dve", fn, r, w))
            G = lambda fn, r, w: (lambda: P.op("dve", fn, r, w))

            def st0():
                P.dma(tab[:, :, 0:ns_], rot_tab[gp].rearrange("c p n -> p c n")[:, :, 0:ns_], [B_rot], [Btab], Btab)
                hold["pr"] = psf[2 * par]; hold["pi"] = psf[2 * par + 1]
                mm(hold["pr"][0][:, 0:NT], [(Bre[:, gp, :], uT[:, tile_, 0:NT])], [B_const, B_u], [hold["pr"][1]])
                mm(hold["pi"][0][:, 0:NT], [(Bim[:, gp, :], uT[:, tile_, 0:NT])], [B_const, B_u], [hold["pi"][1]])
            stages = [[st0]]
            PR = lambda: hold["pr"][0][:, 0:NT]
            PI = lambda: hold["pi"][0][:, 0:NT]
            stages.append([lambda: P.op("dve", lambda e, x=PR(): e.tensor_tensor(out=v3(rin[0]), in0=v3(x), in1=cosb, op=ALU.mult), [hold["pr"][1], Btab, B_rin], [B_rin])])
            stages.append([lambda: P.op("dve", lambda e, x=PI(): e.tensor_tensor(out=v3(tq[0]), in0=v3(x), in1=sinb, op=ALU.mult), [hold["pi"][1], Btab, B_tq], [B_tq])])
            stages.append([lambda: P.op("dve", lambda e, x=PI(): e.tensor_tensor(out=v3(rin[1]), in0=v3(x), in1=cosb, op=ALU.mult), [hold["pi"][1], Btab, B_rin], [B_rin])])
            stages.append([lambda: P.op("dve", lambda e, x=PR(): e.tensor_tensor(out=v3(tq[1]), in0=v3(x), in1=sinb, op=ALU.mult), [hold["pr"][1], Btab, B_tq], [B_tq])])
            stages.append([S(lambda e: e.tensor_tensor(out=rin[0], in0=rin[0], in1=tq[0], op=ALU.add), [B_tq, B_rin], [B_rin])])
            stages.append([S(lambda e: e.tensor_tensor(out=rin[1], in0=rin[1], in1=tq[1], op=ALU.subtract), [B_tq, B_rin], [B_rin])])
            for si, (c0, n) in enumerate(seqs):
                stt, bst = (sst_s[si], B_ssts[si]) if sample else (sst, B_sst)
                for ri in range(2):
                    stages.append([S(lambda e, ri=ri, c0=c0, n=n, stt=stt: e.tensor_tensor_scan(
                        out=w_[ri][:, c0:c0 + n], data0=rS[:, gp:gp + 1].to_broadcast([128, n]), data1=rin[ri][:, c0:c0 + n],
                        initial=stt[:, ri, gp:gp + 1], op0=ALU.mult, op1=ALU.add), [B_rin, B_const, bst, Bw], [Bw])])
                cl = tab[:, 0, n - 1:n]; sl = tab[:, 1, n - 1:n]; e1 = c0 + n - 1
                stages.append([S(lambda e, sl=sl, e1=e1: e.tensor_scalar(out=cr[:, 2:3], in0=w_[1][:, e1:e1 + 1], scalar1=sl, scalar2=None, op0=ALU.mult), [Bw, Btab, Bcr], [Bcr])])
                stages.append([S(lambda e, sl=sl, e1=e1: e.tensor_scalar(out=cr[:, 3:4], in0=w_[0][:, e1:e1 + 1], scalar1=sl, scalar2=None, op0=ALU.mult), [Bw, Btab, Bcr], [Bcr])])
                stages.append([S(lambda e, cl=cl, e1=e1, stt=stt: e.scalar_tensor_tensor(out=stt[:, 0, gp:gp + 1], in0=w_[0][:, e1:e1 + 1], scalar=cl, in1=cr[:, 2:3],
                                                                          op0=ALU.mult, op1=ALU.subtract), [Bw, Btab, Bcr], [bst])])
                stages.append([S(lambda e, cl=cl, e1=e1, stt=stt: e.scalar_tensor_tensor(out=stt[:, 1, gp:gp + 1], in0=w_[1][:, e1:e1 + 1], scalar=cl, in1=cr[:, 3:4],
                                                                          op0=ALU.mult, op1=ALU.add), [Bw, Btab, Bcr], [bst])])
            pq, B_pq = pq2[par], B_pq2[par]
            stages.append([G(lambda e: e.tensor_tensor(out=v3(pq[0]), in0=v3(w_[0]), in1=cosb, op=ALU.mult), [Bw, Btab, B_pq], [B_pq])])
            stages.append([G(lambda e: e.tensor_tensor(out=v3(pq[1]), in0=v3(w_[1]), in1=sinb, op=ALU.mult), [Bw, Btab, B_pq], [B_pq])])
            stages.append([G(lambda e: e.tensor_tensor(out=sbf[0], in0=pq[0], in1=pq[1], op=ALU.subtract), [B_pq, Bsb], [Bsb])])
            stages.append([G(lambda e: e.tensor_tensor(out=v3(pq[0]), in0=v3(w_[0]), in1=sinb, op=ALU.mult), [Bw, Btab, B_pq], [B_pq])])
            stages.append([G(lambda e: e.tensor_tensor(out=v3(pq[1]), in0=v3(w_[1]), in1=cosb, op=ALU.mult), [Bw, Btab, B_pq], [B_pq])])
            stages.append([G(lambda e: e.tensor_tensor(out=sbf[1], in0=pq[0], in1=pq[1], op=ALU.add), [B_pq, Bsb], [Bsb])])

            def fy(e, first_mm=(g4 == 0)):
                e.matmul(py[:, 0:NT], lhsT=Cre[:, gp, :], rhs=sbf[0], start=first_mm, stop=False)
                ins = e.matmul(py[:, 0:NT], lhsT=Cim[:, gp, :], rhs=sbf[1], start=False, stop=False)
                if g4 == 3:
                    ins = e.matmul(py[:, 0:NT], lhsT=Ddiag[:, tile_, :], rhs=uT[:, tile_, 0:NT], start=False, stop=True)
                return ins
            final = lambda: P.op("pe", fy, [Bsb, B_const, B_u], [pyb])
            return stages, final

        sgst = [carve([128, CW], F32) for _ in range(4)]; B_sgst = [P.buf(f"sgst{i}") for i in range(4)]
        gsi = {"i": 0}

        gbanks = [psf[4], (psb[0][0][:, :].bitcast(F32), psb[0][1]), (psb[1][0][:, :].bitcast(F32), psb[1][1])]
        grr = {"i": 0, "ssm": True}

        def nps_gate():
            if not grr["ssm"]:
                return nps()
            grr["i"] = (grr["i"] + 1) % 3
            return gbanks[grr["i"]]

        def gate_chunk(cc, i):
            gw = wload(win_b, B_win, 0, KT, 2560 + i * D + cc * CW, CW)
            for t in range(nt):
                pt, pb = nps_gate()
                mm(pt[:, 0:CW], [(hT[:, k, t * 128:(t + 1) * 128], gw[0][:, k, 0:CW]) for k in range(KT)], [gw[1], B_hT], [pb])
                j = gsi["i"] % 4; gsi["i"] += 1
                P.op("act", lambda e, pt=pt, j=j: e.activation(out=sgst[j], in_=pt[:, 0:CW], func=AF.Sigmoid), [pb], [B_sgst[j]])
                P.dma(sgscr[cc, t, :, i, :], sgst[j], [B_sgst[j]], [B_sgscr], B_sgst[j], par=True, q="act")
        gate_list = [(cc, i) for cc in range(D // CW) for i in range(3)]
        gpos = {"i": 0}

        def gate_some(n):
            for _ in range(n):
                if gpos["i"] < len(gate_list):
                    gate_chunk(*gate_list[gpos["i"]]); gpos["i"] += 1
        for tile_ in range(6):
            for pair in range(2):
                A, fa = gp_stages(tile_, pair * 2, 0)
                Bq, fb = gp_stages(tile_, pair * 2 + 1, 1)
                for i in range(max(len(A), len(Bq))):
                    for lst in (A, Bq):
                        if i < len(lst):
                            for th_ in lst[i]:
                                th_()
                    if i == 0:
                        gate_some(2)
                fa(); fb()
            P.op("act", lambda e, tile_=tile_: e.activation(out=yT[:, tile_, :], in_=psy[0][:, 0:NT], func=AF.Gelu_apprx_tanh), [psy[1]], [B_yT])
        if sample:
            for s in range(nseq):
                from_state(sst_s[s][:, 0, :], B_ssts[s], o_sre[s]); from_state(sst_s[s][:, 1, :], B_ssts[s], o_sim[s])
        elif last:
            from_state(sst[:, 0, :], B_sst, o_pre[:, :]); from_state(sst[:, 1, :], B_sst, o_pim[:, :])
        ck(f"b{tok0 // 512 if not sample else 4}s5")
        rewind(markS)
        wt, wb = wload(wglu_b, B_wglu, 0, 6, 0, CW)
        wt2, wb2 = wload(wglu_b, B_wglu, 0, 6, 256, CW)
        wt3, wb3 = wload(wglu_b, B_wglu, 0, 6, 512, CW)
        sg = carve([128, NT], F32); B_sg = P.buf("sg")
        for ct in range(6):
            w_t, w_b = ((wt, wb), (wt2, wb2), (wt3, wb3))[ct // 2]
            pt, pb = nps()
            mm(pt[:, 0:NT], [(w_t[:, k, (ct % 2) * 128:(ct % 2) * 128 + 128], yT[:, k, 0:NT]) for k in range(6)], [w_b, B_yT], [pb])
            P.op("act", lambda e, pt=pt: e.activation(out=sg[:, 0:NT], in_=pt[:, 0:NT], func=AF.Sigmoid), [pb], [B_sg])
            P.op("dve", lambda e, ct=ct: e.tensor_tensor(out=osT[:, ct, :], in0=yT[:, ct, :], in1=sg[:, 0:NT], op=ALU.mult), [B_sg, B_yT], [B_os])

        ck(f"b{tok0 // 512 if not sample else 4}s6")
        grr["ssm"] = False
        gate_some(len(gate_list))
        sgts = [carve([128, nt, 3, CW], F32) for _ in range(2)]; B_sgts = [P.buf("sgt0"), P.buf("sgt1")]
        mchs = [carve([128, CW], F32) for _ in range(3)]; B_mchs = [P.buf(f"mch{i}") for i in range(3)]; mt2s = [carve([128, CW], F32) for _ in range(3)]
        mi = 0
        ssq = sb_keep["ssq"]; B_ssq = P.buf("ssq")
        P.op("dve", lambda e: e.memset(ssq, 0.0), [], [B_ssq])
        branch = ((oaT, B_oa, 0, 6), (osT, B_os, 6, 6), (omT, B_om, 12, 4))
        for cc in range(D // CW):
            sgt, B_sgt = sgts[cc % 2], B_sgts[cc % 2]
            P.dma(sgt, sgscr[cc, 0:nt].rearrange("t p i c -> p t i c"), [B_sgscr], [B_sgt], B_sgt)
            ow = wload(wout_b, B_wout, 0, KT, cc * CW, CW)
            for t in range(nt):
                pbr = []
                for i, (oT, Bo, k0, nk) in enumerate(branch):
                    pt, pb = nps()
                    mm(pt[:, 0:CW], [(oT[:, k, t * 128:(t + 1) * 128], ow[0][:, k0 + k, 0:CW]) for k in range(nk)], [ow[1], Bo], [pb])
                    pbr.append((pt, pb))
                mch, mt2, B_mch = mchs[mi % 3], mt2s[mi % 3], B_mchs[mi % 3]; mi += 1
                P.op("dve", lambda e, p0=pbr[0][0], t=t, sgt=sgt: e.tensor_tensor(out=mch, in0=p0[:, 0:CW], in1=sgt[:, t, 0, :], op=ALU.mult), [pbr[0][1], B_sgt], [B_mch])
                for i in (1, 2):
                    P.op("dve", lambda e, p=pbr[i][0], i=i, t=t, sgt=sgt: e.tensor_tensor(out=mt2, in0=p[:, 0:CW], in1=sgt[:, t, i, :], op=ALU.mult), [pbr[i][1], B_sgt, B_mch], [B_mch])
                    P.op("dve", lambda e: e.tensor_tensor(out=mch, in0=mch, in1=mt2, op=ALU.add), [B_mch], [B_mch])
                P.op("act", lambda e, t=t, cc=cc: e.activation(out=mt2, in_=mch, func=AF.Square, accum_out=ssq[:, t, cc:cc + 1]), [B_mch, B_ssq], [B_mch, B_ssq])
                P.dma(mixs[t * 128:(t + 1) * 128, cc * CW:(cc + 1) * CW], mch, [B_mch], [B_mixs], B_mch, par=True, q="act")
        ck(f"b{tok0 // 512 if not sample else 4}s7")
        barrier()
        hT2 = carve([128, KT, NT], BF16); B_h2 = P.buf("hT")
        actT = carve([128, FT, NT], BF16); B_act = P.buf("actT")
        ssq2 = sb_keep["ssq2"]; B_ssq2 = P.buf("ssq2")
        ssq_keep = ssq

        def post_norm_residual(t, ssq_t, B_sq, gi, res_src, B_res, out_dst, B_o, then_norm):
            P.op("dve", lambda e: e.tensor_reduce(out=stat[:, 2:3], in_=ssq_t, axis=mybir.AxisListType.X, op=ALU.add), [B_sq], [B_stat])
            rstd_from(2, 3, 128)
            P.dma(mt[:, :], mixs[t * 128:(t + 1) * 128, :], [B_mixs], [B_mt], B_mt)
            P.dma(xt[:, :], res_src, [B_res], [B_xt], B_xt)
            P.op("dve", lambda e: e.tensor_tensor(out=mt[:, :], in0=mt[:, :], in1=gbc[:, gi, :], op=ALU.mult), [B_mt, B_const], [B_mt])
            P.op("dve", lambda e: e.scalar_tensor_tensor(out=xt[:, :], in0=mt[:, :], scalar=stat[:, 3:4], in1=xt[:, :], op0=ALU.mult,
                                                         op1=ALU.add), [B_mt, B_stat, B_xt], [B_xt])
            P.dma(out_dst, xt[:, :], [B_xt], [B_o], P.buf("xst"), q="act")
            if then_norm:
                norm_T(xt, B_xt, t, hT2, B_h2, 1)
        for t in range(nt):
            post_norm_residual(t, ssq_keep[:, t, :], B_ssq, 0, x_src[tok0 + t * 128:tok0 + (t + 1) * 128, :], B_xsrc,
                               y_dst[tok0 + t * 128:tok0 + (t + 1) * 128, :], B_y, True)
        ck(f"b{tok0 // 512 if not sample else 4}s8")
        asb = carve([128, nseq, 2 + NT // nseq], F32); B_asb = P.buf("asb")
        acc = carve([128, NT], F32); B_acc = P.buf("acc")
        gl = carve([128, NT], F32); B_gl = P.buf("gl")
        mchs = [carve([128, CW], F32) for _ in range(4)]; B_mchs = [P.buf(f"mch{i}") for i in range(4)]; mt2s = [carve([128, CW], F32) for _ in range(4)]
        mi = 0
        ns = NT // nseq
        if sample:
            cv_s = carve([128, nseq, FT, 2], F32); B_cvs = P.buf("cvs")
            for s in range(nseq):
                P.dma(nat[0:88, 0:128], st_cv[s].rearrange("i (f p) -> (i f) p", p=128), [], [B_nat], B_nat)
                pt, pb = nps()
                P.op("pe", lambda e, pt=pt: e.transpose(pt[:, 0:88], nat[0:88, 0:128], ident[0:88, 0:88]), [B_nat, B_const], [pb])
                P.op("dve", lambda e, pt=pt, s=s: e.tensor_copy(out=cv_s[:, s, :, :].rearrange("p f i -> p i f"),
                                                               in_=pt[:, 0:88].rearrange("p (i f) -> p i f", i=2)), [pb], [B_cvs])
        for fg in range(FT // 2):
            wa = wload(wup_b, B_wup, 0, KT, fg * CW, CW)
            wb_ = wload(wup_b, B_wup, 0, KT, DFF + fg * CW, CW)
            for j in range(2):
                f = fg * 2 + j
                pa, pab = nps()
                mm(pa[:, 0:NT], [(wa[0][:, k, j * 128:(j + 1) * 128], hT2[:, k, 0:NT]) for k in range(KT)], [wa[1], B_h2], [pab])
                pbv, pbb = nps()
                mm(pbv[:, 0:NT], [(wb_[0][:, k, j * 128:(j + 1) * 128], hT2[:, k, 0:NT]) for k in range(KT)], [wb_[1], B_h2], [pbb])
                hist = cv_s[:, :, f, :] if sample else cvst[:, f, :].unsqueeze(1)
                bh = B_cvs if sample else B_cvst
                P.op("dve", lambda e, hist=hist: e.tensor_copy(out=asb[:, :, 0:2], in_=hist), [bh, B_asb], [B_asb])
                P.op("act", lambda e, pa=pa: e.activation(out=asb[:, :, 2:2 + ns], in_=pa[:, 0:NT].rearrange("p (s n) -> p s n", s=nseq),
                                                          func=AF.Copy), [pab, B_asb], [B_asb])
                P.op("act", lambda e, hist=hist: e.activation(out=hist, in_=asb[:, :, ns:ns + 2], func=AF.Copy), [B_asb], [bh])
                a3 = acc.rearrange("p (s n) -> p s n", s=nseq)
                P.op("dve", lambda e, f=f: e.tensor_scalar(out=a3, in0=asb[:, :, 2:2 + ns], scalar1=cw[:, 2, f:f + 1], scalar2=cw[:, 3, f:f + 1],
                                                           op0=ALU.mult, op1=ALU.add), [B_asb, B_const], [B_acc])
                P.op("dve", lambda e, f=f: e.scalar_tensor_tensor(out=a3, in0=asb[:, :, 1:1 + ns], scalar=cw[:, 1, f:f + 1], in1=a3,
                                                                  op0=ALU.mult, op1=ALU.add), [B_asb, B_const, B_acc], [B_acc])
                P.op("dve", lambda e, f=f: e.scalar_tensor_tensor(out=a3, in0=asb[:, :, 0:ns], scalar=cw[:, 0, f:f + 1], in1=a3,
                                                                  op0=ALU.mult, op1=ALU.add), [B_asb, B_const, B_acc], [B_acc])
                P.op("act", lambda e: e.activation(out=gl, in_=acc, func=AF.Gelu_apprx_tanh), [B_acc], [B_gl])
                P.op("dve", lambda e, pbv=pbv, f=f: e.tensor_tensor(out=actT[:, f, :], in0=pbv[:, 0:NT], in1=gl, op=ALU.mult), [pbb, B_gl], [B_act])
        if sample or last:
            for s in range(nseq):
                src = cv_s[:, s, :, :] if sample else cvst[:, :, :]
                bh = B_cvs if sample else B_cvst
                P.op("dve", lambda e, src=src: e.tensor_copy(out=nat[:, 0:88].rearrange("p (i f) -> p i f", i=2),
                                                             in_=src.rearrange("p f i -> p i f")), [bh, B_nat], [B_nat])
                pt, pb = nps()
                P.op("pe", lambda e, pt=pt: e.transpose(pt[0:88, 0:128], nat[:, 0:88], ident[:, :]), [B_nat, B_const], [pb])
                P.op("act", lambda e, pt=pt: e.activation(out=stg[0:88, 0:128], in_=pt[0:88, 0:128], func=AF.Copy), [pb], [B_stg])
                for i in range(2):
                    dst = (o_scv[s, i:i + 1, :] if sample else o_pcv[i:i + 1, :]).rearrange("o (f p) -> (o f) p", p=128)
                    P.dma(dst, stg[i * 44:(i + 1) * 44, 0:128], [B_stg], [], B_stg)
        ck(f"b{tok0 // 512 if not sample else 4}s9")
        P.op("dve", lambda e: e.memset(ssq2, 0.0), [], [B_ssq2])
        for cc in range(D // CW):
            wds = []
            for k0 in range(0, FT, KT):
                nk = min(KT, FT - k0)
                wds.append((k0, nk, wload(wdn_b, B_wdn, k0, nk, cc * CW, CW)))
            for t0_ in range(0, nt, 2):
                ts_ = list(range(t0_, min(nt, t0_ + 2)))
                pts = {t: nps() for t in ts_}
                for (k0, nk, wd) in wds:
                    for t in ts_:
                        def fn(e, t=t, k0=k0, nk=nk, wd=wd, pt=pts[t][0]):
                            ins = None
                            for k in range(nk):
                                ins = e.matmul(pt[:, 0:CW], lhsT=actT[:, k0 + k, t * 128:(t + 1) * 128], rhs=wd[0][:, k, 0:CW],
                                               start=(k0 + k == 0), stop=(k0 + k == FT - 1))
                            return ins
                        P.op("pe", fn, [wd[1], B_act], [pts[t][1]])
                for t in ts_:
                    mch, mt2, B_mch = mchs[mi % 4], mt2s[mi % 4], B_mchs[mi % 4]; mi += 1
                    P.op("act", lambda e, t=t, pt=pts[t][0]: e.activation(out=mch, in_=pt[:, 0:CW], func=AF.Copy), [pts[t][1]], [B_mch])
                    P.op("act", lambda e, t=t, cc=cc: e.activation(out=mt2, in_=mch, func=AF.Square, accum_out=ssq2[:, t, cc:cc + 1]), [B_mch, B_ssq2], [B_mch, B_ssq2])
                    P.dma(mixs[t * 128:(t + 1) * 128, cc * CW:(cc + 1) * CW], mch, [B_mch], [B_mixs], B_mch, par=True, q="act")
            if next_x is not None and cc % 2 == 0:
                tn = cc // 2
                rms_and_T(next_x[0][next_x[1] + tn * 128:next_x[1] + (tn + 1) * 128, :], B_xsrc, tn, hT2, B_h2, 0)
        ck(f"b{tok0 // 512 if not sample else 4}s10")
        def s11(t):
            rows = y_dst[tok0 + t * 128:tok0 + (t + 1) * 128, :]
            post_norm_residual(t, ssq2[:, t, :], B_ssq2, 1, rows, B_y, rows, B_y, False)
        thunks = [(lambda t=t: s11(t)) for t in range(nt)]
        if defer11:
            return thunks
        for th_ in thunks:
            th_()
        return []

    def to_state_b(dst, src48x64, bdst):
        P.dma(nat[0:48, 0:64], src48x64, [], [B_nat], B_nat)
        P.dma(nat[0:48, 64:128], src48x64, [], [B_nat], B_nat)
        pt, pb = nps()
        P.op("pe", lambda e: e.transpose(pt[:, 0:48], nat[0:48, 0:128], ident[0:48, 0:48]), [B_nat, B_const], [pb])
        pv = pt[:, 0:48].rearrange("p (g two) -> p g two", two=2)
        P.op("dve", lambda e: e.tensor_copy(out=dst[0:64, :], in_=pv[0:64, :, 0]), [pb], [bdst])
        P.op("dve", lambda e: e.tensor_copy(out=dst[64:128, :], in_=pv[64:128, :, 1]), [pb], [bdst])

    def from_state(src, bsrc, dst48x64):
        nv2 = nat[:, 0:48].rearrange("p (g two) -> p g two", two=2)
        P.op("dve", lambda e: e.memset(nat[:, 0:48], 0.0), [B_nat], [B_nat])
        P.op("dve", lambda e: e.tensor_copy(out=nv2[0:64, :, 0], in_=src[0:64, :]), [bsrc, B_nat], [B_nat])
        P.op("dve", lambda e: e.tensor_copy(out=nv2[64:128, :, 1], in_=src[64:128, :]), [bsrc, B_nat], [B_nat])
        pt, pb = nps()
        P.op("pe", lambda e: e.transpose(pt[0:48, 0:128], nat[:, 0:48], ident[:, :]), [B_nat, B_const], [pb])
        P.op("dve", lambda e: e.tensor_copy(out=stg[0:48, 64:192], in_=pt[0:48, 0:128]), [pb], [B_stg])
        P.op("dve", lambda e: e.tensor_tensor(out=stg[0:48, 0:64], in0=stg[0:48, 64:128], in1=stg[0:48, 128:192], op=ALU.add), [B_stg], [B_stg])
        P.dma(dst48x64, stg[0:48, 0:64], [B_stg], [], B_stg)

    sb_keep = {"ssq": sb("ssqk", [128, 4, 8])[:], "ssq2": sb("ssqk2", [128, 4, 8])[:]}
    pm = {"mkT": sb("pmkT", [128, 4, 256], BF16), "mvb": sb("pmvb", [128, 2, 512], BF16), "B": P.buf("pmkv")}

    barrier()

    mem_kv_prompt(pm["mkT"], pm["mvb"], pm["B"])
    ck("memkv")

    pend = []
    for b in range(4):
        pend = block(x_p, B_const, y_p, B_yp, b * 512, 512, [(0, 512)], b == 0, b == 3, False,
                     skip_s1=(b > 0), next_x=((x_p, (b + 1) * 512) if b < 3 else None), pre11=pend, defer11=True)
        pass0["on"] = False
    block(x_s, B_const, y_s, B_ys, 0, 128, [(0, 64), (64, 64)], True, True, True, pre11=pend)

    global _P
    _P = P
    P.emit()
    es.close()
    return nc


_CACHE = {}


def _onehot():
    half, max_exact, nb = 16, 8, 32
    i = np.arange(64)[:, None, None]
    kc = np.arange(3)[None, :, None]
    j = np.arange(64)[None, None, :]
    rel = (kc * 64 + j) - 128 - i
    n = np.abs(rel)
    large = max_exact + (np.log(np.maximum(n, 1).astype(np.float32) / max_exact) / math.log(128 / max_exact) * (half - max_exact)).astype(np.int32)
    large = np.minimum(large, half - 1)
    bucket = np.where(rel > 0, half, 0) + np.where(n < max_exact, n, large)
    oh = (bucket[None] == np.arange(nb)[:, None, None, None]).astype(np.float32)
    return np.ascontiguousarray(oh.reshape(nb, 64 * 3 * 64))


def kernel(**inp):
    f = lambda a: np.ascontiguousarray(np.asarray(a, dtype=np.float32))
    if "nc" not in _CACHE:
        _CACHE["nc"] = build_program()
    nc = _CACHE["nc"]
    oh = _onehot()
    shared = {
        "rbt": f(inp["rel_bias_table"]), "oh": oh,
        "g_pm": f(inp["norm_pre_mix"]), "g_qm": f(inp["norm_post_mix"]), "g_pf": f(inp["norm_pre_ffn"]),
        "g_qf": f(inp["norm_post_ffn"]), "g_mem": f(inp["norm_mem"]),
        "w_in": f(inp["w_in"][0]), "sinks": f(inp["attn_sinks"]),
        "a_re": f(inp["ssm_a_re"][0]), "a_im": f(inp["ssm_a_im"][0]), "log_dt": f(inp["ssm_log_dt"]),
        "b_re": f(inp["ssm_b_re"][0]), "b_im": f(inp["ssm_b_im"][0]),
        "cc_re": f(inp["ssm_c_re"][0]), "cc_im": f(inp["ssm_c_im"][0]),
        "ssm_d": f(inp["ssm_d"][0].reshape(6, 128)), "w_glu": f(inp["w_glu"][0]), "w_mkv": f(inp["w_mem_kv"][0]),
        "w_out": f(inp["w_out"][0]), "w_up": f(inp["w_up"][0]), "conv_w": f(inp["conv_w"][0]),
        "conv_b": f(inp["conv_b"]), "w_down": f(inp["w_down"][0]),
    }
    in_maps = []
    for c in range(8):
        m = dict(shared)
        s = slice(2 * c, 2 * c + 2)
        m.update({
            "x_p": f(inp["x_prompt"][c]), "x_s": f(inp["x_sample"][s].reshape(128, D)),
            "c_ak": f(inp["cache_attn_k"][0, s].reshape(2, 128, 256)), "c_av": f(inp["cache_attn_v"][0, s].reshape(2, 128, 256)),
            "c_mk": f(inp["cache_mem_k"][0, s].reshape(2, 256, 512)), "c_mv": f(inp["cache_mem_v"][0, s].reshape(2, 256, 512)),
            "st_re": f(inp["state_ssm_re"][0, s]), "st_im": f(inp["state_ssm_im"][0, s]),
            "st_cv": f(inp["state_conv"][0, s]), "memp": f(inp["mem_prompt"][c]),
        })
        in_maps.append(m)
    res = run_bass_kernel_spmd(nc, in_maps, core_ids=list(range(8))).results
    g = lambda k: np.stack([np.asarray(r[k], dtype=np.float32) for r in res])
    cat = lambda k: np.concatenate([np.asarray(r[k], dtype=np.float32) for r in res], axis=0)
    return (
        g("y_p").reshape(8, 2048, D), cat("y_s").reshape(16, 64, D),
        g("o_pk").reshape(1, 8, 128, NKV, HD), g("o_pv").reshape(1, 8, 128, NKV, HD),
        g("o_pre").reshape(1, 8, 48, 64), g("o_pim").reshape(1, 8, 48, 64), g("o_pcv").reshape(1, 8, 2, DFF),
        g("o_pmk").reshape(1, 8, 256, 4, 128), g("o_pmv").reshape(1, 8, 256, 4, 128),
        cat("o_sk").reshape(1, 16, 128, NKV, HD), cat("o_sv").reshape(1, 16, 128, NKV, HD),
        cat("o_sre").reshape(1, 16, 48, 64), cat("o_sim").reshape(1, 16, 48, 64), cat("o_scv").reshape(1, 16, 2, DFF),
    )
```

```python
import math
from contextlib import ExitStack
import numpy as np
import concourse.bass as bass
import concourse.mybir as mybir
from concourse.bass_utils import run_bass_kernel_spmd

F32 = mybir.dt.float32
BF16 = mybir.dt.bfloat16
I32 = mybir.dt.int32
AF = mybir.ActivationFunctionType
ALU = mybir.AluOpType

D = 2048
KT = 16
NQ, NKV, HD = 12, 4, 64
DFF = 5632
FT = 44
INC = 8704
L = 64
NGP = 24
QPERM = [0, 3, 1, 4, 2, 5, 6, 9, 7, 10, 8, 11]
EPS = 1e-6
CW = 256


import types


def _snap(fn):
    if getattr(fn, "__closure__", None) is None:
        return fn
    cells = []
    for c in fn.__closure__:
        try:
            cells.append(types.CellType(c.cell_contents))
        except ValueError:
            cells.append(c)
    g = types.FunctionType(fn.__code__, fn.__globals__, fn.__name__, fn.__defaults__, tuple(cells))
    g.__kwdefaults__ = fn.__kwdefaults__
    return g


class Buf:
    def __init__(self, name):
        self.name = name
        self.lastw = None
        self.readers = []
        self.sem = None
        self.cnt = 0
        self.pw = []
        self.nobar = name in ("win", "wglu", "wmkv", "wout", "wup", "wdn", "outs")
        self.excl = name.startswith("ps")


class Prog:
    def __init__(self, nc, es):
        self.nc = nc
        self.es = es
        self.ops = {e: [] for e in ("pe", "act", "dve", "pool", "sp")}
        self.count = {e: 0 for e in ("pe", "act", "dve", "pool", "sp")}
        self.esem = {e: es.enter_context(nc.semaphore("s_" + e)) for e in ("pe", "act", "dve", "pool", "sp")}
        self.dsems = []
        self.bufs = {}
        self.stopped = False

    def buf(self, name):
        if name not in self.bufs:
            self.bufs[name] = Buf(name)
        return self.bufs[name]

    def _deps(self, eng, reads, writes):
        deps = set()
        for b in reads:
            if b.lastw is not None:
                deps.add(b.lastw)
            deps.update(b.pw)
            if b.excl:
                for r in b.readers:
                    if not (r[0] == "eng" and r[1] == eng):
                        deps.add(r)
        for b in writes:
            if b.lastw is not None:
                deps.add(b.lastw)
            deps.update(b.pw)
            for r in b.readers:
                deps.add(r)
        if eng == "pe":
            deps = {d for d in deps if not (d[0] == "eng" and d[1] == "pe")}
        return deps

    def _commit(self, me, reads, writes, par=False):
        for b in reads:
            b.readers.append(me)
        for b in writes:
            if par:
                b.pw.append(me)
            else:
                b.lastw = me
                b.pw = []
            b.readers = []

    def op(self, eng, fn, reads=(), writes=()):
        if self.stopped:
            return
        deps = self._deps(eng, reads, writes)
        self.count[eng] += 1
        me = ("eng", eng, self.count[eng])
        self.ops[eng].append((deps, _snap(fn), None))
        self._commit(me, reads, writes)

    def dma(self, out, in_, reads, writes, sembuf, q="sp", par=False):
        if self.stopped:
            return
        deps = self._deps(q, reads, writes)
        if par:
            drop = set()
            for b in writes:
                drop.update(b.pw)
                if b.lastw is not None and b.lastw[0] == "dma" and b.lastw[1] is sembuf and not b.readers:
                    drop.add(b.lastw)
            deps = {d for d in deps if d not in drop}
        if sembuf.sem is None:
            sembuf.sem = self.es.enter_context(self.nc.semaphore("d_" + sembuf.name))
            self.dsems.append(sembuf)
        sembuf.cnt += 16
        me = ("dma", sembuf, sembuf.cnt)
        self.ops[q].append((deps, (lambda e, o=out, i=in_: e.dma_start(out=o, in_=i)), sembuf))
        self._commit(me, reads, writes, par)

    def barrier(self):
        if self.stopped:
            return
        nop = lambda en: en.nop()
        deps = {("eng", e, self.count[e]) for e in ("pe", "act", "dve", "pool") if self.count[e]}
        deps |= {("dma", b, b.cnt) for b in self.dsems if not b.nobar}
        self.count["sp"] += 1
        self.ops["sp"].append((deps, nop, None))
        me = ("eng", "sp", self.count["sp"])
        for e in ("pe", "act", "dve", "pool"):
            self.count[e] += 1
            self.ops[e].append(({me}, nop, None))

    def emit(self):
        nc = self.nc
        with nc.Block() as block:
            def run(engname, e):
                known = {}
                for deps, fn, sembuf in self.ops[engname]:
                    need = {}
                    for d in deps:
                        key = (d[0], d[1] if d[0] == "eng" else id(d[1]))
                        sem = self.esem[d[1]] if d[0] == "eng" else d[1].sem
                        if known.get(key, 0) >= d[2]:
                            continue
                        if key not in need or need[key][1] < d[2]:
                            need[key] = (sem, d[2])
                    for key, (sem, v) in need.items():
                        e.wait_ge(sem, v)
                        known[key] = v
                    ins = fn(e)
                    if sembuf is not None:
                        ins.then_inc(sembuf.sem, 16)
                    else:
                        ins.then_inc(self.esem[engname], 1)
                if engname == "sp":
                    for en in ("pe", "act", "dve", "pool"):
                        if self.count[en]:
                            e.wait_ge(self.esem[en], self.count[en])
                    for b in self.dsems:
                        e.wait_ge(b.sem, b.cnt)

            @block.tensor
            def _(e):
                run("pe", e)

            @block.scalar
            def _(e):
                run("act", e)

            @block.vector
            def _(e):
                run("dve", e)

            @block.gpsimd
            def _(e):
                run("pool", e)

            @block.sync
            def _(e):
                run("sp", e)


STOP = None


def build_program():
    nc = bass.Bass("TRN2", target_bir_lowering=False)
    es = ExitStack()
    P = Prog(nc, es)

    def ck(name):
        if STOP == name:
            P.stopped = True

    def din(name, shape, dt=F32):
        return nc.dram_tensor(name, list(shape), dt, kind="ExternalInput").ap()

    def dout(name, shape):
        return nc.dram_tensor(name, list(shape), F32, kind="ExternalOutput").ap()

    def dscr(name, shape, dt):
        return nc.dram_tensor(name, list(shape), dt).ap()

    x_p = din("x_p", [2048, D]); x_s = din("x_s", [128, D])
    c_ak = din("c_ak", [2, 128, 256]); c_av = din("c_av", [2, 128, 256])
    c_mk = din("c_mk", [2, 256, 512]); c_mv = din("c_mv", [2, 256, 512])
    st_re = din("st_re", [2, 48, 64]); st_im = din("st_im", [2, 48, 64])
    st_cv = din("st_cv", [2, 2, DFF])
    memp = din("memp", [256, D])
    rbt = din("rbt", [32, 12]); oh = din("oh", [32, 64 * 3 * 64])
    g_pm = din("g_pm", [1, D]); g_qm = din("g_qm", [1, D]); g_pf = din("g_pf", [1, D]); g_qf = din("g_qf", [1, D])
    g_mem = din("g_mem", [1, D])
    w_in = din("w_in", [D, INC]); sinks = din("sinks", [1, 12])
    a_re = din("a_re", [48, 64]); a_im = din("a_im", [48, 64]); log_dt = din("log_dt", [1, 48])
    b_re = din("b_re", [48, 64, 16]); b_im = din("b_im", [48, 64, 16])
    cc_re = din("cc_re", [48, 16, 64]); cc_im = din("cc_im", [48, 16, 64])
    ssm_d = din("ssm_d", [6, 128]); w_glu = din("w_glu", [768, 768]); w_mkv = din("w_mkv", [D, 1024])
    w_out = din("w_out", [D, D]); w_up = din("w_up", [D, 2 * DFF]); conv_w = din("conv_w", [3, DFF])
    conv_b = din("conv_b", [1, DFF]); w_down = din("w_down", [DFF, D])
    y_p = dout("y_p", [2048, D]); y_s = dout("y_s", [128, D])
    o_pk = dout("o_pk", [128, 256]); o_pv = dout("o_pv", [128, 256])
    o_pre = dout("o_pre", [48, 64]); o_pim = dout("o_pim", [48, 64]); o_pcv = dout("o_pcv", [2, DFF])
    o_pmk = dout("o_pmk", [256, 512]); o_pmv = dout("o_pmv", [256, 512])
    o_sk = dout("o_sk", [2, 128, 256]); o_sv = dout("o_sv", [2, 128, 256])
    o_sre = dout("o_sre", [2, 48, 64]); o_sim = dout("o_sim", [2, 48, 64]); o_scv = dout("o_scv", [2, 2, DFF])
    win_b = dscr("win_b", [D, INC], BF16); wglu_b = dscr("wglu_b", [768, 768], BF16)
    wmkv_b = dscr("wmkv_b", [D, 1024], BF16); wout_b = dscr("wout_b", [D, D], BF16)
    wup_b = dscr("wup_b", [D, 2 * DFF], BF16); wdn_b = dscr("wdn_b", [DFF, D], BF16)
    mixs = dscr("mixs", [512, D], F32)
    rot_tab = dscr("rot_tab", [NGP, 2, 128, 512], F32)
    sgscr = dscr("sgscr", [D // CW, 4, 128, 3, CW], F32)
    B_win, B_wglu, B_wmkv, B_wout, B_wup, B_wdn, B_mixs = [P.buf(n) for n in
        ("win", "wglu", "wmkv", "wout", "wup", "wdn", "mixs")]
    origmap = {"win": w_in, "wglu": w_glu, "wmkv": w_mkv, "wout": w_out, "wup": w_up, "wdn": w_down}
    B_yp, B_ys = P.buf("yp"), P.buf("ys")
    B_rot = P.buf("rot"); B_sgscr = P.buf("sgscr")
    B_out = P.buf("outs")

    def sb(name, shape, dt=F32):
        return es.enter_context(nc.sbuf_tensor(name, list(shape), dt))

    def ps(name, shape, dt=F32):
        return es.enter_context(nc.psum_tensor(name, list(shape), dt))

    psf = [(ps(f"psf{i}", [128, 512]), P.buf(f"psf{i}")) for i in range(5)]
    psy = (ps("psy", [128, 512]), P.buf("psy"))
    psb = [(ps(f"psb{i}", [128, 1024], BF16), P.buf(f"psb{i}")) for i in range(2)]
    rr = {"f": 0, "b": 0, "w": 0, "ev": 0}

    def nps():
        rr["f"] = (rr["f"] + 1) % 5
        return psf[rr["f"]]

    def npsb():
        rr["b"] = (rr["b"] + 1) % 2
        return psb[rr["b"]]

    NW = 4
    wbufs = [(sb(f"wb{i}", [128, KT, CW], BF16), P.buf(f"wb{i}")) for i in range(NW)]

    pass0 = {"on": True}

    def wload(src, bsrc, k0, nk, c0, ncol):
        rr["w"] = (rr["w"] + 1) % NW
        t, b = wbufs[rr["w"]]
        v = src.rearrange("(kt p) n -> p kt n", p=128)
        if not pass0["on"]:
            P.dma(t[:, 0:nk, 0:ncol], v[:, k0:k0 + nk, c0:c0 + ncol], [bsrc], [b], b)
            return t, b
        orig = origmap[bsrc.name]
        ov = orig.rearrange("(kt p) n -> p kt n", p=128)
        first = [True]

        bsw = P.buf(f"wbsw{rr['w']}"); bst = P.buf(f"wbst{rr['w']}")

        def ld(dst, srcap):
            P.dma(dst, srcap, [], [b], bsw, q="pool", par=not first[0])
            first[0] = False
        if bsrc.name == "win" and c0 < 768:
            for j in range(ncol // 64):
                h = QPERM[(c0 // 64) + j]
                ld(t[:, 0:nk, j * 64:(j + 1) * 64], ov[:, k0:k0 + nk, h * 64:(h + 1) * 64])
        elif bsrc.name == "wout":
            for k in range(k0, k0 + nk):
                if k < 6:
                    for hf in range(2):
                        h = QPERM[2 * k + hf]
                        ld(t[hf * 64:(hf + 1) * 64, k - k0, 0:ncol], orig[h * 64:(h + 1) * 64, c0:c0 + ncol])
            ka = max(k0, 6)
            if ka < k0 + nk:
                ld(t[:, ka - k0:nk, 0:ncol], ov[:, ka:k0 + nk, c0:c0 + ncol])
        else:
            ld(t[:, 0:nk, 0:ncol], ov[:, k0:k0 + nk, c0:c0 + ncol])
        if bsrc.name != "wmkv":
            P.dma(v[:, k0:k0 + nk, c0:c0 + ncol], t[:, 0:nk, 0:ncol], [b], [bsrc], bst, par=True)
        return t, b

    def evac(out, in_, reads, writes, scale=None):
        rr["ev"] ^= 1
        if rr["ev"]:
            if scale is None:
                P.op("act", lambda e: e.activation(out=out, in_=in_, func=AF.Copy), reads, writes)
            else:
                P.op("act", lambda e: e.activation(out=out, in_=in_, func=AF.Copy, scale=scale), reads, writes)
        else:
            if scale is None:
                P.op("dve", lambda e: e.tensor_copy(out=out, in_=in_), reads, writes)
            else:
                P.op("dve", lambda e: e.tensor_scalar(out=out, in0=in_, scalar1=scale, scalar2=None, op0=ALU.mult), reads, writes)

    def mm(out, pairs, reads, writes):
        def fn(e, out=out, pairs=pairs):
            n = len(pairs)
            ins = None
            for i, (l, r) in enumerate(pairs):
                ins = e.matmul(out, lhsT=l, rhs=r, start=(i == 0), stop=(i == n - 1))
            return ins
        P.op("pe", fn, reads, writes)

    ident = sb("ident", [128, 128]); identb = sb("identb", [128, 128], BF16)
    B_const = P.buf("const")
    ones_b = sb("ones_b", [128, 128], BF16)
    gT = sb("gT", [128, 3, KT])
    gbc = sb("gbc", [128, 2, D], BF16)
    cw = sb("cw", [128, 4, FT])
    dT = sb("dT", [128, 6])
    Ddiag = sb("Ddiag", [128, 6, 128], BF16)
    epsb = sb("epsb", [128, 1])
    biasT = sb("biasT", [64, 3, 64, 12])
    esk = sb("esk", [128, 12])
    rS = sb("rS", [128, NGP])
    tabs = [sb(f"tab{i}", [128, 2, 512]) for i in range(2)]; B_tabs = [P.buf("tab0"), P.buf("tab1")]
    Bre = sb("Bre", [128, NGP, 128], BF16); Bim = sb("Bim", [128, NGP, 128], BF16)
    Cre = sb("Cre", [128, NGP, 128], BF16); Cim = sb("Cim", [128, NGP, 128], BF16)
    sst = sb("sst", [128, 2, NGP])
    cvst = sb("cvst", [128, FT, 2])
    kTt = sb("kTt", [128, 2, 128 + 512], BF16)
    v64 = sb("v64", [64, 10, NKV, 128], BF16)
    B_kT, B_v64, B_sst, B_cvst = P.buf("kT"), P.buf("v64"), P.buf("sst"), P.buf("cvst")
    xt = sb("xt", [128, D]); mt = sb("mt", [128, D]); B_xt, B_mt = P.buf("xt"), P.buf("mt")
    hb = sb("hb", [128, D], BF16); B_hb = P.buf("hb")
    stat = sb("stat", [128, 8]); B_stat = P.buf("stat")
    stg = sb("stg", [128, 512]); B_stg = P.buf("stg")
    ARENA = 82 * 1024
    arena = sb("arena", [128, ARENA // 4])
    B_arena = P.buf("arena")
    off = {"v": 0}

    def carve(shape, dt):
        n = 1
        for s in shape[1:]:
            n *= s
        nbytes = n * (2 if dt == BF16 else 4)
        nbytes = (nbytes + 31) // 32 * 32
        o = off["v"]
        assert o + nbytes <= ARENA, (o, nbytes)
        off["v"] = o + nbytes
        flat = arena[0:shape[0], o // 4:(o + nbytes) // 4]
        if dt != F32:
            flat = flat.bitcast(dt)
        flat = flat[:, 0:n]
        if len(shape) == 2:
            return flat
        names = " ".join(f"d{i}" for i in range(1, len(shape)))
        return flat.rearrange(f"p ({names}) -> p {names}", **{f"d{i}": shape[i] for i in range(2, len(shape))})

    def barrier():
        P.barrier()
        off["v"] = 0

    def rewind(mark):
        P.barrier()
        off["v"] = mark

    def cast(dst, src, bdst, rows=None):
        P.dma(dst, src, [], [bdst], bdst, q="pool")

    def casts_a():
        cast(wmkv_b[:, :], w_mkv[:, :], B_wmkv)
        for s_ in range(12):
            h = QPERM[s_]
            cast(win_b[:, s_ * 64:(s_ + 1) * 64], w_in[:, h * 64:(h + 1) * 64], B_win)
        for r in range(4):
            cast(win_b[r * 512:(r + 1) * 512, 768:INC], w_in[r * 512:(r + 1) * 512, 768:INC], B_win)

    def casts_b():
        cast(wglu_b[:, :], w_glu[:, :], B_wglu)
        for s_ in range(12):
            h = QPERM[s_]
            cast(wout_b[s_ * 64:(s_ + 1) * 64, :], w_out[h * 64:(h + 1) * 64, :], B_wout)
        cast(wout_b[768:D, :], w_out[768:D, :], B_wout)
        for r in range(4):
            cast(wup_b[r * 512:(r + 1) * 512, :], w_up[r * 512:(r + 1) * 512, :], B_wup)
        for r in range(4):
            cast(wdn_b[r * 1408:(r + 1) * 1408, :], w_down[r * 1408:(r + 1) * 1408, :], B_wdn)

    ck("cast")
    P.op("dve", lambda e: e.memset(epsb[:], EPS), [], [B_const])
    idc = carve([128, 128], F32); idr = carve([128, 128], F32); idx = None
    nat = sb("nat_p", [128, 128]); B_nat = P.buf("nat")
    P.op("pool", lambda e: e.iota(idc[:], pattern=[[1, 128]], base=0, channel_multiplier=0,
                                  allow_small_or_imprecise_dtypes=True), [], [B_const])
    P.op("pool", lambda e: e.iota(idr[:], pattern=[[0, 128]], base=0, channel_multiplier=1,
                                  allow_small_or_imprecise_dtypes=True), [], [B_const])
    for i, g in enumerate((g_qm, g_qf)):
        P.dma(gbc[:, i, :], g.broadcast_to([128, D]), [], [B_const], B_const, q="pool")
    P.op("dve", lambda e: e.tensor_tensor(out=ident[:], in0=idc[:], in1=idr[:], op=ALU.is_equal), [B_const], [B_const])
    P.op("dve", lambda e: e.tensor_copy(out=identb[:], in_=ident[:]), [B_const], [B_const])
    P.op("dve", lambda e: e.memset(ones_b[:], 1.0), [], [B_const])
    P.op("dve", lambda e: e.memset(sst[:], 0.0), [], [B_sst])
    P.op("dve", lambda e: e.memset(cvst[:], 0.0), [], [B_cvst])


    def load_T(dst, src_rows_ap, nrows, ncols_in=128):
        P.dma(nat[0:nrows, 0:128], src_rows_ap, [], [B_nat], B_nat)
        pt, pb = nps()
        P.op("pe", lambda e: e.transpose(pt[:, 0:nrows], nat[0:nrows, 0:128], ident[0:nrows, 0:nrows]),
             [B_nat, B_const], [pb])
        P.op("dve", lambda e: e.tensor_copy(out=dst, in_=pt[:, 0:nrows]), [pb], [B_const])

    for i, g in enumerate((g_pm, g_pf, g_mem)):
        load_T(gT[:, i, :], g.rearrange("o (k p) -> (o k) p", p=128), 16)
    for i in range(3):
        load_T(cw[:, i, :], conv_w[i:i + 1, :].rearrange("o (k p) -> (o k) p", p=128), FT)
    load_T(cw[:, 3, :], conv_b.rearrange("o (k p) -> (o k) p", p=128), FT)
    load_T(dT[:, :], ssm_d[:, :], 6)
    for t in range(6):
        P.op("dve", lambda e, t=t: e.tensor_scalar(out=Ddiag[:, t, :], in0=ident[:], scalar1=dT[:, t:t + 1],
                                                   scalar2=None, op0=ALU.mult), [B_const], [B_const])
    P.dma(nat[:, 0:12], sinks.broadcast_to([128, 12]), [], [B_nat], B_nat)
    P.op("act", lambda e: e.activation(out=esk[:], in_=nat[:, 0:12], func=AF.Exp), [B_nat], [B_const])

    oht = carve([32, 64 * 3 * 64], F32); B_oh = P.buf("oh")
    tb = carve([32, 12], F32)
    P.dma(oht[:, :], oh[:, :], [], [B_oh], B_oh)
    P.dma(tb[:, :], rbt[:, :], [], [B_oh], B_oh)
    ohv = oht.rearrange("b (i k j) -> b i k j", i=64, k=3)
    for kc in range(3):
        for ih in range(2):
            pt, pb = nps()
            def fn(e, pt=pt, kc=kc, ih=ih):
                ins = None
                for ii in range(32):
                    i = ih * 32 + ii
                    ins = e.matmul(pt[0:64, ii * 12:(ii + 1) * 12], lhsT=ohv[:, i, kc, :], rhs=tb[:, :],
                                   start=True, stop=True)
                return ins
            P.op("pe", fn, [B_oh], [pb])
            P.op("dve", lambda e, pt=pt, kc=kc, ih=ih: e.tensor_copy(
                out=biasT[:, kc, ih * 32:(ih + 1) * 32, :].rearrange("p i h -> p (i h)"), in_=pt[0:64, 0:384]),
                [pb], [B_const])

    ck("const")
    barrier()

    def to_state(dst, src48x64):
        P.dma(nat[0:48, 0:64], src48x64, [], [B_nat], B_nat)
        P.dma(nat[0:48, 64:128], src48x64, [], [B_nat], B_nat, par=True)
        pt, pb = nps()
        P.op("pe", lambda e: e.transpose(pt[:, 0:48], nat[0:48, 0:128], ident[0:48, 0:48]), [B_nat, B_const], [pb])
        pv = pt[:, 0:48].rearrange("p (g two) -> p g two", two=2)
        P.op("dve", lambda e: e.tensor_copy(out=dst[0:64, :], in_=pv[0:64, :, 0]), [pb], [B_const])
        P.op("dve", lambda e: e.tensor_copy(out=dst[64:128, :], in_=pv[64:128, :, 1]), [pb], [B_const])

    lre = carve([128, NGP], F32); lim = carve([128, NGP], F32); dts = carve([128, NGP], F32)
    th = carve([128, NGP], F32)
    to_state(lre, a_re[:, :]); to_state(lim, a_im[:, :])
    P.dma(nat[:, 0:48], log_dt.broadcast_to([128, 48]), [], [B_nat], B_nat)
    nv = nat[:, 0:48].rearrange("p (g two) -> p g two", two=2)
    P.op("act", lambda e: e.activation(out=dts[0:64, :], in_=nv[0:64, :, 0], func=AF.Exp), [B_nat], [B_const])
    P.op("act", lambda e: e.activation(out=dts[64:128, :], in_=nv[64:128, :, 1], func=AF.Exp), [B_nat], [B_const])
    P.op("dve", lambda e: e.tensor_tensor(out=th[:, :], in0=lim, in1=dts, op=ALU.mult), [B_const], [B_const])
    tmpa = carve([128, NGP], F32)
    P.op("dve", lambda e: e.tensor_tensor(out=tmpa, in0=lre, in1=dts, op=ALU.mult), [B_const], [B_const])
    P.op("act", lambda e: e.activation(out=rS[:, :], in_=tmpa, func=AF.Exp), [B_const], [B_const])
    idx = carve([128, 512], F32)
    P.op("pool", lambda e: e.iota(idx, pattern=[[1, 512]], base=1, channel_multiplier=0,
                                  allow_small_or_imprecise_dtypes=True), [], [B_const])
    LIM = 3.14159
    c0t = carve([128, NGP], F32); s0t = carve([128, NGP], F32)
    tmps = [[carve([128, 512], F32), carve([128, 512], I32), carve([128, 512], F32), carve([128, 512], F32)] for _ in range(3)]
    B_tt = [P.buf("tt0"), P.buf("tt1"), P.buf("tt2")]
    it = 0
    for gp in range(NGP):
        for fn_i, shift in ((0, math.pi / 2), (1, 0.0)):
            ang, ki, kf, sn = tmps[it % 3]; Bt = B_tt[it % 3]; it += 1
            P.op("dve", lambda e, ang=ang, gp=gp, shift=shift: e.tensor_scalar(out=ang, in0=idx, scalar1=th[:, gp:gp + 1], scalar2=shift,
                                                                              op0=ALU.mult, op1=ALU.add), [B_const, Bt], [Bt])
            P.op("dve", lambda e, ang=ang, ki=ki: e.tensor_scalar(out=ki, in0=ang, scalar1=1.0 / (2 * math.pi), scalar2=None, op0=ALU.mult), [Bt], [Bt])
            P.op("dve", lambda e, kf=kf, ki=ki: e.tensor_copy(out=kf, in_=ki), [Bt], [Bt])
            P.op("dve", lambda e, ang=ang, kf=kf: e.scalar_tensor_tensor(out=ang, in0=kf, scalar=-2 * math.pi, in1=ang, op0=ALU.mult, op1=ALU.add), [Bt], [Bt])
            P.op("dve", lambda e, ang=ang: e.tensor_scalar(out=ang, in0=ang, scalar1=-LIM, scalar2=LIM, op0=ALU.max, op1=ALU.min), [Bt], [Bt])
            P.op("act", lambda e, ang=ang, sn=sn: e.activation(out=sn, in_=ang, func=AF.Sin), [Bt], [Bt])
            dstc = c0t if fn_i == 0 else s0t
            P.op("act", lambda e, sn=sn, dstc=dstc, gp=gp: e.activation(out=dstc[:, gp:gp + 1], in_=sn[:, 0:1], func=AF.Copy), [Bt], [B_const])
            P.dma(rot_tab[gp, fn_i], sn, [Bt], [B_rot], Bt)
    nre = carve([128, NGP], F32); nim = carve([128, NGP], F32); den = carve([128, NGP], F32)
    cre = carve([128, NGP], F32); cim = carve([128, NGP], F32); t1 = carve([128, NGP], F32); t2 = carve([128, NGP], F32)
    V = lambda fn, r=(B_const,), w=(B_const,): P.op("dve", fn, list(r), list(w))
    V(lambda e: e.tensor_tensor(out=nre, in0=rS[:, :], in1=c0t, op=ALU.mult))
    V(lambda e: e.tensor_scalar(out=nre, in0=nre, scalar1=-1.0, scalar2=None, op0=ALU.add))
    V(lambda e: e.tensor_tensor(out=nim, in0=rS[:, :], in1=s0t, op=ALU.mult))
    V(lambda e: e.tensor_tensor(out=den, in0=lre, in1=lre, op=ALU.mult))
    V(lambda e: e.tensor_tensor(out=t1, in0=lim, in1=lim, op=ALU.mult))
    V(lambda e: e.tensor_tensor(out=den, in0=den, in1=t1, op=ALU.add))
    V(lambda e: e.reciprocal(out=den, in_=den))
    V(lambda e: e.tensor_tensor(out=t1, in0=nre, in1=lre, op=ALU.mult))
    V(lambda e: e.tensor_tensor(out=t2, in0=nim, in1=lim, op=ALU.mult))
    V(lambda e: e.tensor_tensor(out=t1, in0=t1, in1=t2, op=ALU.add))
    V(lambda e: e.tensor_tensor(out=cre, in0=t1, in1=den, op=ALU.mult))
    V(lambda e: e.tensor_tensor(out=t1, in0=nim, in1=lre, op=ALU.mult))
    V(lambda e: e.tensor_tensor(out=t2, in0=nre, in1=lim, op=ALU.mult))
    V(lambda e: e.tensor_tensor(out=t1, in0=t1, in1=t2, op=ALU.subtract))
    V(lambda e: e.tensor_tensor(out=cim, in0=t1, in1=den, op=ALU.mult))
    Zre = carve([128, NGP, 128], F32); Zim = carve([128, NGP, 128], F32); Zt = carve([128, NGP, 128], F32)
    B_Z = P.buf("Z")
    P.op("dve", lambda e: e.memset(Zre, 0.0), [], [B_Z])
    P.op("dve", lambda e: e.memset(Zim, 0.0), [], [B_Z])
    for (Z, src) in ((Zre, b_re), (Zim, b_im)):
        sv = src.rearrange("(t j two) p c -> two j p t c", j=4, two=2)
        Zv = Z.rearrange("p (t j) m -> p j t m", j=4)
        for gpar in range(2):
            for j in range(4):
                c0 = 32 * j + 16 * gpar
                P.dma(Zv[64 * gpar:64 * gpar + 64, j, :, c0:c0 + 16], sv[gpar, j], [], [B_Z], B_Z, par=True)
    bc = lambda t: t.unsqueeze(2).to_broadcast([128, NGP, 128])
    VZ = lambda fn: P.op("dve", fn, [B_Z, B_const], [B_Z])
    VZ(lambda e: e.tensor_tensor(out=Zt, in0=Zim, in1=bc(cim), op=ALU.mult))
    VZ(lambda e: e.tensor_tensor(out=Zim, in0=Zim, in1=bc(cre), op=ALU.mult))
    Zt2 = carve([128, NGP, 128], F32)
    VZ(lambda e: e.tensor_tensor(out=Zt2, in0=Zre, in1=bc(cim), op=ALU.mult))
    VZ(lambda e: e.tensor_tensor(out=Zim, in0=Zim, in1=Zt2, op=ALU.add))
    VZ(lambda e: e.tensor_tensor(out=Zre, in0=Zre, in1=bc(cre), op=ALU.mult))
    VZ(lambda e: e.tensor_tensor(out=Zre, in0=Zre, in1=Zt, op=ALU.subtract))
    for (Z, dst) in ((Zre, Bre), (Zim, Bim)):
        for gp in range(NGP):
            pt, pb = nps()
            P.op("pe", lambda e, pt=pt, Z=Z, gp=gp: e.transpose(pt[:, 0:128], Z[:, gp, :], ident[:, :]), [B_Z, B_const], [pb])
            evac(dst[:, gp, :], pt[:, 0:128], [pb], [B_const])
    B_W = P.buf("W")
    for (src, dst, sc) in ((cc_re, Cre, None), (cc_im, Cim, -1.0)):
        Wt = Zt if src is cc_re else Zt2
        P.op("dve", lambda e, Wt=Wt: e.memset(Wt, 0.0), [B_Z], [B_W, B_Z])
        sv = src.rearrange("(t j two) c p -> two j c t p", j=4, two=2)
        Wv = Wt.rearrange("p (t j) m -> p j t m", j=4)
        for gpar in range(2):
            for j in range(4):
                r0 = 32 * j + 16 * gpar
                P.dma(Wv[r0:r0 + 16, j, :, 64 * gpar:64 * gpar + 64], sv[gpar, j], [], [B_W], B_W, par=True)
        for gp in range(NGP):
            pt, pb = nps()
            P.op("pe", lambda e, pt=pt, Wt=Wt, gp=gp: e.transpose(pt[:, 0:128], Wt[:, gp, :], ident[:, :]), [B_W, B_const], [pb])
            evac(dst[:, gp, :], pt[:, 0:128], [pb], [B_const], scale=sc)

    ck("ssmsetup")
    def rms_and_T(src_rows, bsrc, ntile, hT, B_hT, gidx, ntok_tile=128, extra=None):
        xb_, Bx_ = (xt, B_xt) if ntile % 2 == 0 else (mt, B_mt)
        P.dma(xb_[0:ntok_tile, :], src_rows, [bsrc], [Bx_], Bx_)
        norm_T(xb_, Bx_, ntile, hT, B_hT, gidx, ntok_tile)

    def rstd_from(col_in, col_out, n):
        P.op("act", lambda e: e.activation(out=stat[0:n, col_out:col_out + 1], in_=stat[0:n, col_in:col_in + 1], func=AF.Ln,
                                           scale=1.0 / D, bias=epsb[0:n, :]), [B_stat, B_const], [B_stat])
        P.op("act", lambda e: e.activation(out=stat[0:n, col_out:col_out + 1], in_=stat[0:n, col_out:col_out + 1], func=AF.Exp,
                                           scale=-0.5), [B_stat], [B_stat])

    ncall = {"n": 0}

    def norm_T(src, bsrc, ntile, hT, B_hT, gidx, n=128):
        ncall["n"] += 1
        ck(f"c{ncall['n']}_n0")
        P.op("dve", lambda e: e.memset(stat[:, 0:2], 0.0), [], [B_stat])
        P.op("act", lambda e: e.activation(out=hb[0:n, :], in_=src[0:n, :], func=AF.Square, accum_out=stat[0:n, 0:1]),
             [bsrc, B_stat], [B_hb, B_stat])
        rstd_from(0, 1, n)
        P.op("dve", lambda e: e.tensor_scalar(out=hb[0:n, :], in0=src[0:n, :], scalar1=stat[0:n, 1:2], scalar2=None,
                                              op0=ALU.mult), [bsrc, B_stat], [B_hb])
        ck(f"c{ncall['n']}_n1")
        for q4 in range(2):
            pt, pb = npsb()
            def fn(e, pt=pt, q4=q4):
                ins = None
                for j in range(8):
                    kt = q4 * 8 + j
                    ins = e.transpose(pt[:, j * 128:j * 128 + n], hb[0:n, kt * 128:(kt + 1) * 128], identb[0:n, 0:n])
                return ins
            P.op("pe", fn, [B_hb, B_const], [pb])
            ck(f"c{ncall['n']}_n2")
            for j in range(8):
                kt = q4 * 8 + j
                evac(hT[:, kt, ntile * 128:ntile * 128 + n], pt[:, j * 128:j * 128 + n], [pb, B_const], [B_hT],
                     scale=gT[:, gidx, kt:kt + 1])
                ck(f"c{ncall['n']}_n3_{q4}_{j}")

    def proj_fm(dst_fn, src, bsrc, c0, ncols, hT, B_hT, ntok, nk=KT, k0=0, tok0=0):
        for cb in range(0, ncols, CW):
            ncb = min(CW, ncols - cb)
            wt, wb = wload(src, bsrc, k0, nk, c0 + cb, ncb)
            for ct in range(ncb // 128):
                pt, pb = nps()
                mm(pt[:, 0:ntok], [(wt[:, k, ct * 128:(ct + 1) * 128], hT[:, k, tok0:tok0 + ntok]) for k in range(nk)],
                   [wb, B_hT], [pb])
                dst, bd = dst_fn((cb // 128) + ct)
                evac(dst, pt[:, 0:ntok], [pb], [bd])

    def mem_kv_prompt(mkT, mvb, B_mkv):
        hmT = carve([128, KT, 256], BF16); B_hm = P.buf("hmT")
        for t in range(2):
            rms_and_T(memp[t * 128:(t + 1) * 128, :], B_const, t, hmT, B_hm, 2)
        ck("mk1")
        for cb in range(0, 1024, CW):
            wt, wb = wload(wmkv_b, B_wmkv, 0, KT, cb, CW)
            ck("mk1a")
            for t in range(2):
                pt, pb = nps()
                mm(pt[:, 0:CW], [(hmT[:, k, t * 128:(t + 1) * 128], wt[:, k, 0:CW]) for k in range(KT)], [wb, B_hm], [pb])
                ck("mk1b")
                P.op("act", lambda e, pt=pt: e.activation(out=stg[:, 0:CW], in_=pt[:, 0:CW], func=AF.Copy), [pb], [B_stg])
                ck("mk1c")
                dst = o_pmk if cb < 512 else o_pmv
                P.dma(dst[t * 128:(t + 1) * 128, (cb % 512):(cb % 512) + CW], stg[:, 0:CW], [B_stg], [], B_stg)
                ck("mk1d")
                if cb >= 512:
                    P.op("dve", lambda e, pt=pt, t=t, cb=cb: e.tensor_copy(out=mvb[:, t, cb - 512:cb - 512 + CW], in_=pt[:, 0:CW]),
                         [pb], [B_mkv])
                ck(f"mk_{cb}_{t}")
        ck("mk2")
        proj_fm(lambda ct: (mkT[:, ct, :], B_mkv), wmkv_b, B_wmkv, 0, 512, hmT, B_hm, 256)

    def mem_kv_sample(mkT, mvb, B_mkv, s):
        f = carve([128, 2, 512], F32); fb = carve([128, 2, 512], BF16); B_f = P.buf("mkf")
        P.dma(f, c_mk[s].rearrange("(t p) n -> p t n", p=128), [], [B_f], B_f)
        P.op("dve", lambda e: e.tensor_copy(out=fb, in_=f), [B_f], [B_f])
        for hh in range(4):
            pt, pb = npsb()
            def fn(e, pt=pt, hh=hh):
                ins = None
                for t in range(2):
                    ins = e.transpose(pt[:, t * 128:(t + 1) * 128], fb[:, t, hh * 128:(hh + 1) * 128], identb[:, :])
                return ins
            P.op("pe", fn, [B_f, B_const], [pb])
            evac(mkT[:, hh, :], pt[:, 0:256], [pb], [B_mkv])
        f2 = carve([128, 2, 512], F32); B_f2 = P.buf("mvf")
        P.dma(f2, c_mv[s].rearrange("(t p) n -> p t n", p=128), [], [B_f2], B_f2)
        P.op("act", lambda e: e.activation(out=mvb, in_=f2, func=AF.Copy), [B_f2], [B_mkv])

    def block(x_src, B_xsrc, y_dst, B_y, tok0, NT, seqs, first, last, sample, skip_s1=False, next_x=None, pre11=None, defer11=False):
        nt = NT // 128
        barrier()
        hT = carve([128, KT, NT], BF16); B_hT = P.buf("hT")
        uT = carve([128, 6, NT], BF16)
        B_q, B_u, B_qm = P.buf("qT"), P.buf("uT"), P.buf("qmT")
        oaT = carve([128, 6, NT], BF16); osT = carve([128, 6, NT], BF16); omT = carve([128, 4, NT], BF16)
        B_oa, B_os, B_om = P.buf("oaT"), P.buf("osT"), P.buf("omT")
        yT = carve([128, 6, NT], BF16); B_yT = P.buf("yT")
        nseq = len(seqs)
        mkTs = [carve([128, 4, 256], BF16) if sample else None for _ in range(nseq)]
        mvbs = [carve([128, 2, 512], BF16) if sample else None for _ in range(nseq)]
        B_mkvs = [P.buf("mkv") for _ in range(nseq)]
        markS = off["v"]
        qT = carve([128, 6, NT], BF16); qmT = carve([128, 4, NT], BF16)
        mark0 = off["v"]
        if sample:
            for s in range(nseq):
                mem_kv_sample(mkTs[s], mvbs[s], B_mkvs[s], s)
        else:
            mkTs[0], mvbs[0], B_mkvs[0] = pm["mkT"], pm["mvb"], pm["B"]
        if not skip_s1:
            for t in range(nt):
                rms_and_T(x_src[tok0 + t * 128:tok0 + (t + 1) * 128, :], B_xsrc, t, hT, B_hT, 0)
        ck(f"b{tok0 // 512 if not sample else 4}s1")
        if sample:
            for s in range(nseq):
                kf32 = carve([128, 256], F32); kb16 = carve([128, 256], BF16); Bk = P.buf("kc")
                P.dma(kf32, c_ak[s], [], [Bk], Bk)
                P.op("dve", lambda e, kb16=kb16, kf32=kf32: e.tensor_copy(out=kb16, in_=kf32), [Bk], [Bk])
                pt, pb = npsb()
                def fn(e, pt=pt, kb16=kb16):
                    ins = None
                    for t in range(2):
                        ins = e.transpose(pt[:, t * 128:(t + 1) * 128], kb16[:, t * 128:(t + 1) * 128], identb[:, :])
                    return ins
                P.op("pe", fn, [Bk, B_const], [pb])
                evac(kTt[:, :, s * 192:s * 192 + 128], pt[:, 0:256].rearrange("p (t n) -> p t n", t=2), [pb], [B_kT])
                vf32 = carve([64, 2, 256], F32); Bv = P.buf("vc")
                P.dma(vf32, c_av[s].rearrange("(c p) n -> p c n", p=64), [], [Bv], Bv)
                vv = vf32.rearrange("p c (h d) -> p c h d", h=NKV)
                for dup in range(2):
                    P.op("dve", lambda e, s=s, dup=dup, vv=vv: e.tensor_copy(out=v64[:, s * 3:s * 3 + 2, :, dup * 64:(dup + 1) * 64], in_=vv),
                         [Bv], [B_v64])
                P.dma(o_sk[s, 0:64, :], c_ak[s, 64:128, :], [], [], B_out)
                P.dma(o_sv[s, 0:64, :], c_av[s, 64:128, :], [], [], B_out)
        if sample:
            kcol = [s * 192 + 128 for s in range(nseq)]
            vch = [s * 3 + 2 for s in range(nseq)]
        else:
            kcol = [128]
            vch = [2]
        def dst_q(ct):
            return qT[:, ct, :], B_q
        pre11 = list(pre11 or [])
        proj_fm(dst_q, win_b, B_win, 0, 768, hT, B_hT, NT)
        if pre11:
            pre11.pop(0)()
        for cb in range(0, 256, CW):
            wt, wb = wload(win_b, B_win, 0, KT, 768 + cb, CW)
            for ct in range(2):
                pt, pb = nps()
                mm(pt[:, 0:NT], [(wt[:, k, ct * 128:(ct + 1) * 128], hT[:, k, 0:NT]) for k in range(KT)], [wb, B_hT], [pb])
                for si, (c0, n) in enumerate(seqs):
                    evac(kTt[:, ct, kcol[si]:kcol[si] + n], pt[:, c0:c0 + n], [pb], [B_kT])
            if last or sample:
                for si, (c0, n) in enumerate(seqs):
                    r0, nr = (c0 + n - 128, 128) if not sample else (c0, 64)
                    pt, pb = nps()
                    mm(pt[0:nr, 0:256], [(hT[:, k, r0:r0 + nr], wt[:, k, 0:256]) for k in range(KT)], [wb, B_hT], [pb])
                    P.op("act", lambda e, pt=pt, nr=nr: e.activation(out=stg[0:nr, 0:256], in_=pt[0:nr, 0:256], func=AF.Copy), [pb], [B_stg])
                    dst = o_sk[si, 64:128, :] if sample else o_pk[:, :]
                    P.dma(dst, stg[0:nr, 0:256], [B_stg], [], B_stg)
        wt, wb = wload(win_b, B_win, 0, KT, 1024, CW)
        for si, (c0, n) in enumerate(seqs):
            for c in range(n // 64):
                pt, pb = nps()
                mm(pt[0:64, 0:256], [(hT[:, k, c0 + c * 64:c0 + (c + 1) * 64], wt[:, k, 0:256]) for k in range(KT)], [wb, B_hT], [pb])
                pv = pt[0:64, 0:256].rearrange("p (h d) -> p h d", h=NKV)
                for dup in range(2):
                    evac(v64[:, vch[si] + c, :, dup * 64:(dup + 1) * 64], pv, [pb], [B_v64])
                is_out = sample or (last and c >= n // 64 - 2)
                if is_out:
                    P.op("act", lambda e, pt=pt: e.activation(out=stg[0:64, 256:512], in_=pt[0:64, 0:256], func=AF.Copy), [pb], [B_stg])
                    if sample:
                        dst = o_sv[si, 64:128, :]
                    else:
                        cc = c - (n // 64 - 2)
                        dst = o_pv[cc * 64:(cc + 1) * 64, :]
                    P.dma(dst, stg[0:64, 256:512], [B_stg], [], B_stg)
        if pre11:
            pre11.pop(0)()
        proj_fm(lambda ct: (uT[:, ct, :], B_u), win_b, B_win, 1280, 768, hT, B_hT, NT)
        if pre11:
            pre11.pop(0)()
        proj_fm(lambda ct: (qmT[:, ct, :], B_qm), win_b, B_win, 2048, 512, hT, B_hT, NT)
        while pre11:
            pre11.pop(0)()

        ck(f"b{tok0 // 512 if not sample else 4}s2")
        if off["v"] != mark0:
            rewind(mark0)
        scs = [carve([64, 3, 192], F32) for _ in range(2)]
        pTs = [carve([64, 3, 192], BF16) for _ in range(2)]
        dns = [carve([128, 192], F32) for _ in range(2)]
        B_scs = [[P.buf(f"sc{a_}{k_}") for k_ in range(3)] for a_ in range(2)]
        B_pTs = [[P.buf(f"pT{a_}{k_}") for k_ in range(3)] for a_ in range(2)]
        B_dns = [P.buf("dn0"), P.buf("dn1")]
        ai = 0
        for si, (c0, n) in enumerate(seqs):
            nch = n // 64
            for c in range(nch):
                kcs = [kc for kc in range(3) if sample or (not first) or (c - 2 + kc) >= 0]
                for h in range(NKV):
                    sc, pT, dn = scs[ai % 2], pTs[ai % 2], dns[ai % 2]
                    B_sc, B_pT, B_dn = B_scs[ai % 2], B_pTs[ai % 2], B_dns[ai % 2]
                    ai += 1
                    half = (h % 2) * 64
                    kt_ = h // 2
                    t0 = (h // 2) * 3
                    rhs = qT[half:half + 64, t0:t0 + 3, c0 + c * 64:c0 + (c + 1) * 64]
                    pss = []
                    for kc in kcs:
                        kk = kcol[si] + (c - 2 + kc) * 64
                        pt, pb = nps()
                        mm(pt[0:64, 0:192], [(kTt[half:half + 64, kt_, kk:kk + 64], rhs)], [B_kT, B_q], [pb])
                        pss.append((kc, pt, pb))
                    for kc, pt, pb in pss:
                        bv = biasT[:, kc, :, 3 * h:3 * h + 3].rearrange("p i g -> p g i")
                        P.op("dve", lambda e, pt=pt, kc=kc, bv=bv, sc=sc: e.scalar_tensor_tensor(
                            out=sc[:, kc, :].rearrange("p (g i) -> p g i", g=3), in0=pt[0:64, 0:192].rearrange("p (g i) -> p g i", g=3),
                            scalar=HD ** -0.5, in1=bv, op0=ALU.mult, op1=ALU.add), [pb, B_const], [B_sc[kc]])
                    for kc, pt, pb in pss:
                        P.op("act", lambda e, kc=kc, sc=sc, pT=pT: e.activation(out=pT[:, kc, :], in_=sc[:, kc, :], func=AF.Exp), [B_sc[kc]], [B_pT[kc]])
                    po, pob = nps()
                    mm(po[:, 0:192], [(v64[:, vch[si] + c - 2 + kc, h, :], pT[:, kc, :]) for kc in kcs], [B_v64] + [B_pT[kc] for kc in kcs], [pob])
                    pd, pdb = nps()
                    mm(pd[:, 0:192], [(ones_b[0:64, :], pT[:, kc, :]) for kc in kcs], [B_const] + [B_pT[kc] for kc in kcs], [pdb])
                    P.op("dve", lambda e, pd=pd, h=h, half=half, dn=dn: e.tensor_tensor(
                        out=dn[half:half + 64, :].rearrange("p (g i) -> p g i", g=3),
                        in0=pd[half:half + 64, 0:192].rearrange("p (g i) -> p g i", g=3),
                        in1=esk[half:half + 64, 3 * h:3 * h + 3].unsqueeze(2).to_broadcast([64, 3, 64]), op=ALU.add),
                        [pdb, B_const], [B_dn])
                    P.op("dve", lambda e, half=half, dn=dn: e.reciprocal(out=dn[half:half + 64, :], in_=dn[half:half + 64, :]), [B_dn], [B_dn])
                    P.op("dve", lambda e, po=po, half=half, t0=t0, c=c, c0=c0, dn=dn: e.tensor_tensor(
                        out=oaT[half:half + 64, t0:t0 + 3, c0 + c * 64:c0 + (c + 1) * 64],
                        in0=po[half:half + 64, 0:192].rearrange("p (g i) -> p g i", g=3),
                        in1=dn[half:half + 64, :].rearrange("p (g i) -> p g i", g=3), op=ALU.mult), [pob, B_dn], [B_oa])
        if not sample and not last:
            P.op("dve", lambda e: e.tensor_copy(out=kTt[:, :, 0:128], in_=kTt[:, :, NT:NT + 128]), [B_kT], [B_kT])
            P.op("dve", lambda e: e.tensor_copy(out=v64[:, 0:2, :, :], in_=v64[:, 8:10, :, :]), [B_v64], [B_v64])

        ck(f"b{tok0 // 512 if not sample else 4}s3")
        pm_ = carve([128, 2, 512], BF16); B_pm = P.buf("pmem")
        rcp = carve([128, 512], F32); B_rcp = P.buf("rcp")
        for si, (c0, n) in enumerate(seqs):
            for hh in range(4):
                for mtile in range(2):
                    pt, pb = nps()
                    mm(pt[:, 0:n], [(mkTs[si][:, hh, mtile * 128:(mtile + 1) * 128], qmT[:, hh, c0:c0 + n])], [B_mkvs[si], B_qm], [pb])
                    P.op("act", lambda e, pt=pt, mtile=mtile, n=n: e.activation(out=pm_[:, mtile, 0:n], in_=pt[:, 0:n], func=AF.Exp,
                                                                           scale=128 ** -0.5), [pb], [B_pm])
                po, pob = nps()
                mm(po[:, 0:n], [(mvbs[si][:, mtile, hh * 128:(hh + 1) * 128], pm_[:, mtile, 0:n]) for mtile in range(2)], [B_mkvs[si], B_pm], [pob])
                pd, pdb = nps()
                mm(pd[:, 0:n], [(ones_b[:, :], pm_[:, mtile, 0:n]) for mtile in range(2)], [B_const, B_pm], [pdb])
                P.op("dve", lambda e, pd=pd, n=n: e.reciprocal(out=rcp[:, 0:n], in_=pd[:, 0:n]), [pdb], [B_rcp])
                P.op("dve", lambda e, po=po, n=n, hh=hh, c0=c0: e.tensor_tensor(out=omT[:, hh, c0:c0 + n], in0=po[:, 0:n], in1=rcp[:, 0:n],
                                                                         op=ALU.mult), [pob, B_rcp], [B_om])

        ck(f"b{tok0 // 512 if not sample else 4}s4")
        rewind(markS)
        tq2 = [[carve([128, NT], F32) for _ in range(2)] for _ in range(2)]
        rin2 = [[carve([128, NT], F32) for _ in range(2)] for _ in range(2)]
        w2 = [[carve([128, NT], F32) for _ in range(2)] for _ in range(2)]
        pq2 = tq2
        sbf2 = [[carve([128, NT], BF16) for _ in range(2)] for _ in range(2)]
        cr2 = [carve([128, 4], F32) for _ in range(2)]
        B_tq2 = [P.buf("tqa"), P.buf("tqb")]; B_rin2 = [P.buf("rina"), P.buf("rinb")]; B_pq2 = B_tq2
        B_w2 = [P.buf("w2a"), P.buf("w2b")]; B_sb2 = [P.buf("sb2a"), P.buf("sb2b")]; B_cr2 = [P.buf("cr2a"), P.buf("cr2b")]
        ns_ = NT // nseq
        if sample:
            sst_s = [carve([128, 2, NGP], F32) for _ in range(nseq)]
            B_ssts = [P.buf("ssts0"), P.buf("ssts1")]
            for s in range(nseq):
                to_state_b(sst_s[s][:, 0, :], st_re[s], B_ssts[s])
                to_state_b(sst_s[s][:, 1, :], st_im[s], B_ssts[s])
        v3 = lambda t: t.rearrange("p (s n) -> p s n", s=nseq)

        def gp_stages(tile_, g4, par):
            gp = tile_ * 4 + g4
            tab, Btab = tabs[par], B_tabs[par]
            w_, Bw = w2[par], B_w2[par]; sbf, Bsb = sbf2[par], B_sb2[par]; cr, Bcr = cr2[par], B_cr2[par]
            rin, B_rin = rin2[par], B_rin2[par]; tq, B_tq = tq2[par], B_tq2[par]
            py, pyb = psy
            cosb = tab[:, 0, 0:ns_].unsqueeze(1).to_broadcast([128, nseq, ns_])
            sinb = tab[:, 1, 0:ns_].unsqueeze(1).to_broadcast([128, nseq, ns_])
            hold = {}
            S = lambda fn, r, w: (lambda: P.op("dve", fn, r, w))
            G = lambda fn, r, w: (lambda: P.op("dve", fn, r, w))
            CP = lambda fn, r, w: (lambda: P.op("pool", fn, r, w))

            def st0():
                P.dma(tab[:, :, 0:ns_], rot_tab[gp].rearrange("c p n -> p c n")[:, :, 0:ns_], [B_rot], [Btab], Btab)
                hold["pr"] = psf[2 * par]; hold["pi"] = psf[2 * par + 1]
                mm(hold["pr"][0][:, 0:NT], [(Bre[:, gp, :], uT[:, tile_, 0:NT])], [B_const, B_u], [hold["pr"][1]])
                mm(hold["pi"][0][:, 0:NT], [(Bim[:, gp, :], uT[:, tile_, 0:NT])], [B_const, B_u], [hold["pi"][1]])
            stages = [[st0]]
            PR = lambda: hold["pr"][0][:, 0:NT]
            PI = lambda: hold["pi"][0][:, 0:NT]
            stages.append([lambda: P.op("dve", lambda e, x=PR(): e.tensor_tensor(out=v3(rin[0]), in0=v3(x), in1=cosb, op=ALU.mult), [hold["pr"][1], Btab, B_rin], [B_rin])])
            stages.append([lambda: P.op("dve", lambda e, x=PI(): e.tensor_tensor(out=v3(tq[0]), in0=v3(x), in1=sinb, op=ALU.mult), [hold["pi"][1], Btab, B_tq], [B_tq])])
            stages.append([lambda: P.op("dve", lambda e, x=PI(): e.tensor_tensor(out=v3(rin[1]), in0=v3(x), in1=cosb, op=ALU.mult), [hold["pi"][1], Btab, B_rin], [B_rin])])
            stages.append([lambda: P.op("dve", lambda e, x=PR(): e.tensor_tensor(out=v3(tq[1]), in0=v3(x), in1=sinb, op=ALU.mult), [hold["pr"][1], Btab, B_tq], [B_tq])])
            stages.append([S(lambda e: e.tensor_tensor(out=rin[0], in0=rin[0], in1=tq[0], op=ALU.add), [B_tq, B_rin], [B_rin])])
            stages.append([S(lambda e: e.tensor_tensor(out=rin[1], in0=rin[1], in1=tq[1], op=ALU.subtract), [B_tq, B_rin], [B_rin])])
            for si, (c0, n) in enumerate(seqs):
                stt, bst = (sst_s[si], B_ssts[si]) if sample else (sst, B_sst)
                for ri in range(2):
                    stages.append([S(lambda e, ri=ri, c0=c0, n=n, stt=stt: e.tensor_tensor_scan(
                        out=w_[ri][:, c0:c0 + n], data0=rS[:, gp:gp + 1].to_broadcast([128, n]), data1=rin[ri][:, c0:c0 + n],
                        initial=stt[:, ri, gp:gp + 1], op0=ALU.mult, op1=ALU.add), [B_rin, B_const, bst, Bw], [Bw])])
                cl = tab[:, 0, n - 1:n]; sl = tab[:, 1, n - 1:n]; e1 = c0 + n - 1
                stages.append([CP(lambda e, sl=sl, e1=e1: e.tensor_scalar(out=cr[:, 2:3], in0=w_[1][:, e1:e1 + 1], scalar1=sl, scalar2=None, op0=ALU.mult), [Bw, Btab, Bcr], [Bcr])])
                stages.append([CP(lambda e, sl=sl, e1=e1: e.tensor_scalar(out=cr[:, 3:4], in0=w_[0][:, e1:e1 + 1], scalar1=sl, scalar2=None, op0=ALU.mult), [Bw, Btab, Bcr], [Bcr])])
                stages.append([CP(lambda e, cl=cl, e1=e1: e.tensor_scalar(out=cr[:, 0:1], in0=w_[0][:, e1:e1 + 1], scalar1=cl, scalar2=None, op0=ALU.mult), [Bw, Btab, Bcr], [Bcr])])
                stages.append([CP(lambda e, cl=cl, e1=e1: e.tensor_scalar(out=cr[:, 1:2], in0=w_[1][:, e1:e1 + 1], scalar1=cl, scalar2=None, op0=ALU.mult), [Bw, Btab, Bcr], [Bcr])])
                stages.append([CP(lambda e, stt=stt: e.tensor_tensor(out=stt[:, 0, gp:gp + 1], in0=cr[:, 0:1], in1=cr[:, 2:3], op=ALU.subtract), [Bcr], [bst])])
                stages.append([CP(lambda e, stt=stt: e.tensor_tensor(out=stt[:, 1, gp:gp + 1], in0=cr[:, 1:2], in1=cr[:, 3:4], op=ALU.add), [Bcr], [bst])])
            pq, B_pq = pq2[par], B_pq2[par]
            stages.append([G(lambda e: e.tensor_tensor(out=v3(pq[0]), in0=v3(w_[0]), in1=cosb, op=ALU.mult), [Bw, Btab, B_pq], [B_pq])])
            stages.append([G(lambda e: e.tensor_tensor(out=v3(pq[1]), in0=v3(w_[1]), in1=sinb, op=ALU.mult), [Bw, Btab, B_pq], [B_pq])])
            stages.append([G(lambda e: e.tensor_tensor(out=sbf[0], in0=pq[0], in1=pq[1], op=ALU.subtract), [B_pq, Bsb], [Bsb])])
            stages.append([G(lambda e: e.tensor_tensor(out=v3(pq[0]), in0=v3(w_[0]), in1=sinb, op=ALU.mult), [Bw, Btab, B_pq], [B_pq])])
            stages.append([G(lambda e: e.tensor_tensor(out=v3(pq[1]), in0=v3(w_[1]), in1=cosb, op=ALU.mult), [Bw, Btab, B_pq], [B_pq])])
            stages.append([G(lambda e: e.tensor_tensor(out=sbf[1], in0=pq[0], in1=pq[1], op=ALU.add), [B_pq, Bsb], [Bsb])])

            def fy(e, first_mm=(g4 == 0)):
                e.matmul(py[:, 0:NT], lhsT=Cre[:, gp, :], rhs=sbf[0], start=first_mm, stop=False)
                ins = e.matmul(py[:, 0:NT], lhsT=Cim[:, gp, :], rhs=sbf[1], start=False, stop=False)
                if g4 == 3:
                    ins = e.matmul(py[:, 0:NT], lhsT=Ddiag[:, tile_, :], rhs=uT[:, tile_, 0:NT], start=False, stop=True)
                return ins
            final = lambda: P.op("pe", fy, [Bsb, B_const, B_u], [pyb])
            return stages, final

        sgst = [carve([128, CW], F32) for _ in range(4)]; B_sgst = [P.buf(f"sgst{i}") for i in range(4)]
        gsi = {"i": 0}

        gbanks = [psf[4], (psb[0][0][:, :].bitcast(F32), psb[0][1]), (psb[1][0][:, :].bitcast(F32), psb[1][1])]
        grr = {"i": 0, "ssm": True}

        def nps_gate():
            if not grr["ssm"]:
                return nps()
            grr["i"] = (grr["i"] + 1) % 3
            return gbanks[grr["i"]]

        def gate_chunk(cc, i):
            gw = wload(win_b, B_win, 0, KT, 2560 + i * D + cc * CW, CW)
            for t in range(nt):
                pt, pb = nps_gate()
                mm(pt[:, 0:CW], [(hT[:, k, t * 128:(t + 1) * 128], gw[0][:, k, 0:CW]) for k in range(KT)], [gw[1], B_hT], [pb])
                j = gsi["i"] % 4; gsi["i"] += 1
                P.op("act", lambda e, pt=pt, j=j: e.activation(out=sgst[j], in_=pt[:, 0:CW], func=AF.Sigmoid), [pb], [B_sgst[j]])
                P.dma(sgscr[cc, t, :, i, :], sgst[j], [B_sgst[j]], [B_sgscr], B_sgst[j], par=True, q="act")
        gate_list = [(cc, i) for cc in range(D // CW) for i in range(3)]
        gpos = {"i": 0}

        def gate_some(n):
            for _ in range(n):
                if gpos["i"] < len(gate_list):
                    gate_chunk(*gate_list[gpos["i"]]); gpos["i"] += 1
        for tile_ in range(6):
            for pair in range(2):
                A, fa = gp_stages(tile_, pair * 2, 0)
                Bq, fb = gp_stages(tile_, pair * 2 + 1, 1)
                for i in range(max(len(A), len(Bq))):
                    for lst in (A, Bq):
                        if i < len(lst):
                            for th_ in lst[i]:
                                th_()
                    if i == 0:
                        gate_some(2)
                fa(); fb()
            P.op("act", lambda e, tile_=tile_: e.activation(out=yT[:, tile_, :], in_=psy[0][:, 0:NT], func=AF.Gelu_apprx_tanh), [psy[1]], [B_yT])
        if sample:
            for s in range(nseq):
                from_state(sst_s[s][:, 0, :], B_ssts[s], o_sre[s]); from_state(sst_s[s][:, 1, :], B_ssts[s], o_sim[s])
        elif last:
            from_state(sst[:, 0, :], B_sst, o_pre[:, :]); from_state(sst[:, 1, :], B_sst, o_pim[:, :])
        ck(f"b{tok0 // 512 if not sample else 4}s5")
        rewind(markS)
        wt, wb = wload(wglu_b, B_wglu, 0, 6, 0, CW)
        wt2, wb2 = wload(wglu_b, B_wglu, 0, 6, 256, CW)
        wt3, wb3 = wload(wglu_b, B_wglu, 0, 6, 512, CW)
        sg = carve([128, NT], F32); B_sg = P.buf("sg")
        for ct in range(6):
            w_t, w_b = ((wt, wb), (wt2, wb2), (wt3, wb3))[ct // 2]
            pt, pb = nps()
            mm(pt[:, 0:NT], [(w_t[:, k, (ct % 2) * 128:(ct % 2) * 128 + 128], yT[:, k, 0:NT]) for k in range(6)], [w_b, B_yT], [pb])
            P.op("act", lambda e, pt=pt: e.activation(out=sg[:, 0:NT], in_=pt[:, 0:NT], func=AF.Sigmoid), [pb], [B_sg])
            P.op("dve", lambda e, ct=ct: e.tensor_tensor(out=osT[:, ct, :], in0=yT[:, ct, :], in1=sg[:, 0:NT], op=ALU.mult), [B_sg, B_yT], [B_os])

        ck(f"b{tok0 // 512 if not sample else 4}s6")
        grr["ssm"] = False
        gate_some(len(gate_list))
        sgts = [carve([128, nt, 3, CW], F32) for _ in range(2)]; B_sgts = [P.buf("sgt0"), P.buf("sgt1")]
        mchs = [carve([128, CW], F32) for _ in range(3)]; B_mchs = [P.buf(f"mch{i}") for i in range(3)]; mt2s = [carve([128, CW], F32) for _ in range(3)]
        mi = 0
        ssq = sb_keep["ssq"]; B_ssq = P.buf("ssq")
        P.op("dve", lambda e: e.memset(ssq, 0.0), [], [B_ssq])
        branch = ((oaT, B_oa, 0, 6), (osT, B_os, 6, 6), (omT, B_om, 12, 4))
        for cc in range(D // CW):
            sgt, B_sgt = sgts[cc % 2], B_sgts[cc % 2]
            P.dma(sgt, sgscr[cc, 0:nt].rearrange("t p i c -> p t i c"), [B_sgscr], [B_sgt], B_sgt)
            ow = wload(wout_b, B_wout, 0, KT, cc * CW, CW)
            for t in range(nt):
                pbr = []
                for i, (oT, Bo, k0, nk) in enumerate(branch):
                    pt, pb = nps()
                    mm(pt[:, 0:CW], [(oT[:, k, t * 128:(t + 1) * 128], ow[0][:, k0 + k, 0:CW]) for k in range(nk)], [ow[1], Bo], [pb])
                    pbr.append((pt, pb))
                mch, mt2, B_mch = mchs[mi % 3], mt2s[mi % 3], B_mchs[mi % 3]; mi += 1
                P.op("dve", lambda e, p0=pbr[0][0], t=t, sgt=sgt: e.tensor_tensor(out=mch, in0=p0[:, 0:CW], in1=sgt[:, t, 0, :], op=ALU.mult), [pbr[0][1], B_sgt], [B_mch])
                for i in (1, 2):
                    P.op("dve", lambda e, p=pbr[i][0], i=i, t=t, sgt=sgt: e.tensor_tensor(out=mt2, in0=p[:, 0:CW], in1=sgt[:, t, i, :], op=ALU.mult), [pbr[i][1], B_sgt, B_mch], [B_mch])
                    P.op("dve", lambda e: e.tensor_tensor(out=mch, in0=mch, in1=mt2, op=ALU.add), [B_mch], [B_mch])
                P.op("act", lambda e, t=t, cc=cc: e.activation(out=mt2, in_=mch, func=AF.Square, accum_out=ssq[:, t, cc:cc + 1]), [B_mch, B_ssq], [B_mch, B_ssq])
                P.dma(mixs[t * 128:(t + 1) * 128, cc * CW:(cc + 1) * CW], mch, [B_mch], [B_mixs], B_mch, par=True, q="act")
        ck(f"b{tok0 // 512 if not sample else 4}s7")
        barrier()
        hT2 = carve([128, KT, NT], BF16); B_h2 = P.buf("hT")
        actT = carve([128, FT, NT], BF16); B_act = P.buf("actT")
        ssq2 = sb_keep["ssq2"]; B_ssq2 = P.buf("ssq2")
        ssq_keep = ssq

        def post_norm_residual(t, ssq_t, B_sq, gi, res_src, B_res, out_dst, B_o, then_norm):
            P.op("dve", lambda e: e.tensor_reduce(out=stat[:, 2:3], in_=ssq_t, axis=mybir.AxisListType.X, op=ALU.add), [B_sq], [B_stat])
            rstd_from(2, 3, 128)
            P.dma(mt[:, :], mixs[t * 128:(t + 1) * 128, :], [B_mixs], [B_mt], B_mt)
            P.dma(xt[:, :], res_src, [B_res], [B_xt], B_xt)
            P.op("dve", lambda e: e.tensor_tensor(out=mt[:, :], in0=mt[:, :], in1=gbc[:, gi, :], op=ALU.mult), [B_mt, B_const], [B_mt])
            P.op("dve", lambda e: e.scalar_tensor_tensor(out=xt[:, :], in0=mt[:, :], scalar=stat[:, 3:4], in1=xt[:, :], op0=ALU.mult,
                                                         op1=ALU.add), [B_mt, B_stat, B_xt], [B_xt])
            P.dma(out_dst, xt[:, :], [B_xt], [B_o], P.buf("xst"), q="act")
            if then_norm:
                norm_T(xt, B_xt, t, hT2, B_h2, 1)
        for t in range(nt):
            post_norm_residual(t, ssq_keep[:, t, :], B_ssq, 0, x_src[tok0 + t * 128:tok0 + (t + 1) * 128, :], B_xsrc,
                               y_dst[tok0 + t * 128:tok0 + (t + 1) * 128, :], B_y, True)
        ck(f"b{tok0 // 512 if not sample else 4}s8")
        asb = carve([128, nseq, 2 + NT // nseq], F32); B_asb = P.buf("asb")
        acc = carve([128, NT], F32); B_acc = P.buf("acc")
        gl = carve([128, NT], F32); B_gl = P.buf("gl")
        mchs = [carve([128, CW], F32) for _ in range(4)]; B_mchs = [P.buf(f"mch{i}") for i in range(4)]; mt2s = [carve([128, CW], F32) for _ in range(4)]
        mi = 0
        ns = NT // nseq
        if sample:
            cv_s = carve([128, nseq, FT, 2], F32); B_cvs = P.buf("cvs")
            for s in range(nseq):
                P.dma(nat[0:88, 0:128], st_cv[s].rearrange("i (f p) -> (i f) p", p=128), [], [B_nat], B_nat)
                pt, pb = nps()
                P.op("pe", lambda e, pt=pt: e.transpose(pt[:, 0:88], nat[0:88, 0:128], ident[0:88, 0:88]), [B_nat, B_const], [pb])
                P.op("dve", lambda e, pt=pt, s=s: e.tensor_copy(out=cv_s[:, s, :, :].rearrange("p f i -> p i f"),
                                                               in_=pt[:, 0:88].rearrange("p (i f) -> p i f", i=2)), [pb], [B_cvs])
        for fg in range(FT // 2):
            wa = wload(wup_b, B_wup, 0, KT, fg * CW, CW)
            wb_ = wload(wup_b, B_wup, 0, KT, DFF + fg * CW, CW)
            for j in range(2):
                f = fg * 2 + j
                pa, pab = nps()
                mm(pa[:, 0:NT], [(wa[0][:, k, j * 128:(j + 1) * 128], hT2[:, k, 0:NT]) for k in range(KT)], [wa[1], B_h2], [pab])
                pbv, pbb = nps()
                mm(pbv[:, 0:NT], [(wb_[0][:, k, j * 128:(j + 1) * 128], hT2[:, k, 0:NT]) for k in range(KT)], [wb_[1], B_h2], [pbb])
                hist = cv_s[:, :, f, :] if sample else cvst[:, f, :].unsqueeze(1)
                bh = B_cvs if sample else B_cvst
                P.op("dve", lambda e, hist=hist: e.tensor_copy(out=asb[:, :, 0:2], in_=hist), [bh, B_asb], [B_asb])
                P.op("act", lambda e, pa=pa: e.activation(out=asb[:, :, 2:2 + ns], in_=pa[:, 0:NT].rearrange("p (s n) -> p s n", s=nseq),
                                                          func=AF.Copy), [pab, B_asb], [B_asb])
                P.op("act", lambda e, hist=hist: e.activation(out=hist, in_=asb[:, :, ns:ns + 2], func=AF.Copy), [B_asb], [bh])
                a3 = acc.rearrange("p (s n) -> p s n", s=nseq)
                P.op("dve", lambda e, f=f: e.tensor_scalar(out=a3, in0=asb[:, :, 2:2 + ns], scalar1=cw[:, 2, f:f + 1], scalar2=cw[:, 3, f:f + 1],
                                                           op0=ALU.mult, op1=ALU.add), [B_asb, B_const], [B_acc])
                P.op("dve", lambda e, f=f: e.scalar_tensor_tensor(out=a3, in0=asb[:, :, 1:1 + ns], scalar=cw[:, 1, f:f + 1], in1=a3,
                                                                  op0=ALU.mult, op1=ALU.add), [B_asb, B_const, B_acc], [B_acc])
                P.op("dve", lambda e, f=f: e.scalar_tensor_tensor(out=a3, in0=asb[:, :, 0:ns], scalar=cw[:, 0, f:f + 1], in1=a3,
                                                                  op0=ALU.mult, op1=ALU.add), [B_asb, B_const, B_acc], [B_acc])
                P.op("act", lambda e: e.activation(out=gl, in_=acc, func=AF.Gelu_apprx_tanh), [B_acc], [B_gl])
                P.op("dve", lambda e, pbv=pbv, f=f: e.tensor_tensor(out=actT[:, f, :], in0=pbv[:, 0:NT], in1=gl, op=ALU.mult), [pbb, B_gl], [B_act])
        if sample or last:
            for s in range(nseq):
                src = cv_s[:, s, :, :] if sample else cvst[:, :, :]
                bh = B_cvs if sample else B_cvst
                P.op("dve", lambda e, src=src: e.tensor_copy(out=nat[:, 0:88].rearrange("p (i f) -> p i f", i=2),
                                                             in_=src.rearrange("p f i -> p i f")), [bh, B_nat], [B_nat])
                pt, pb = nps()
                P.op("pe", lambda e, pt=pt: e.transpose(pt[0:88, 0:128], nat[:, 0:88], ident[:, :]), [B_nat, B_const], [pb])
                P.op("act", lambda e, pt=pt: e.activation(out=stg[0:88, 0:128], in_=pt[0:88, 0:128], func=AF.Copy), [pb], [B_stg])
                for i in range(2):
                    dst = (o_scv[s, i:i + 1, :] if sample else o_pcv[i:i + 1, :]).rearrange("o (f p) -> (o f) p", p=128)
                    P.dma(dst, stg[i * 44:(i + 1) * 44, 0:128], [B_stg], [], B_stg)
        ck(f"b{tok0 // 512 if not sample else 4}s9")
        P.op("dve", lambda e: e.memset(ssq2, 0.0), [], [B_ssq2])
        for cc in range(D // CW):
            wds = []
            for k0 in range(0, FT, KT):
                nk = min(KT, FT - k0)
                wds.append((k0, nk, wload(wdn_b, B_wdn, k0, nk, cc * CW, CW)))
            for t0_ in range(0, nt, 2):
                ts_ = list(range(t0_, min(nt, t0_ + 2)))
                pts = {t: nps() for t in ts_}
                for (k0, nk, wd) in wds:
                    for t in ts_:
                        def fn(e, t=t, k0=k0, nk=nk, wd=wd, pt=pts[t][0]):
                            ins = None
                            for k in range(nk):
                                ins = e.matmul(pt[:, 0:CW], lhsT=actT[:, k0 + k, t * 128:(t + 1) * 128], rhs=wd[0][:, k, 0:CW],
                                               start=(k0 + k == 0), stop=(k0 + k == FT - 1))
                            return ins
                        P.op("pe", fn, [wd[1], B_act], [pts[t][1]])
                for t in ts_:
                    mch, mt2, B_mch = mchs[mi % 4], mt2s[mi % 4], B_mchs[mi % 4]; mi += 1
                    P.op("act", lambda e, t=t, pt=pts[t][0]: e.activation(out=mch, in_=pt[:, 0:CW], func=AF.Copy), [pts[t][1]], [B_mch])
                    P.op("act", lambda e, t=t, cc=cc: e.activation(out=mt2, in_=mch, func=AF.Square, accum_out=ssq2[:, t, cc:cc + 1]), [B_mch, B_ssq2], [B_mch, B_ssq2])
                    P.dma(mixs[t * 128:(t + 1) * 128, cc * CW:(cc + 1) * CW], mch, [B_mch], [B_mixs], B_mch, par=True, q="act")
            if next_x is not None and cc % 2 == 0:
                tn = cc // 2
                rms_and_T(next_x[0][next_x[1] + tn * 128:next_x[1] + (tn + 1) * 128, :], B_xsrc, tn, hT2, B_h2, 0)
        ck(f"b{tok0 // 512 if not sample else 4}s10")
        def s11(t):
            rows = y_dst[tok0 + t * 128:tok0 + (t + 1) * 128, :]
            post_norm_residual(t, ssq2[:, t, :], B_ssq2, 1, rows, B_y, rows, B_y, False)
        thunks = [(lambda t=t: s11(t)) for t in range(nt)]
        if defer11:
            return thunks
        for th_ in thunks:
            th_()
        return []

    def to_state_b(dst, src48x64, bdst):
        P.dma(nat[0:48, 0:64], src48x64, [], [B_nat], B_nat)
        P.dma(nat[0:48, 64:128], src48x64, [], [B_nat], B_nat)
        pt, pb = nps()
        P.op("pe", lambda e: e.transpose(pt[:, 0:48], nat[0:48, 0:128], ident[0:48, 0:48]), [B_nat, B_const], [pb])
        pv = pt[:, 0:48].rearrange("p (g two) -> p g two", two=2)
        P.op("dve", lambda e: e.tensor_copy(out=dst[0:64, :], in_=pv[0:64, :, 0]), [pb], [bdst])
        P.op("dve", lambda e: e.tensor_copy(out=dst[64:128, :], in_=pv[64:128, :, 1]), [pb], [bdst])

    def from_state(src, bsrc, dst48x64):
        nv2 = nat[:, 0:48].rearrange("p (g two) -> p g two", two=2)
        P.op("dve", lambda e: e.memset(nat[:, 0:48], 0.0), [B_nat], [B_nat])
        P.op("dve", lambda e: e.tensor_copy(out=nv2[0:64, :, 0], in_=src[0:64, :]), [bsrc, B_nat], [B_nat])
        P.op("dve", lambda e: e.tensor_copy(out=nv2[64:128, :, 1], in_=src[64:128, :]), [bsrc, B_nat], [B_nat])
        pt, pb = nps()
        P.op("pe", lambda e: e.transpose(pt[0:48, 0:128], nat[:, 0:48], ident[:, :]), [B_nat, B_const], [pb])
        P.op("dve", lambda e: e.tensor_copy(out=stg[0:48, 64:192], in_=pt[0:48, 0:128]), [pb], [B_stg])
        P.op("dve", lambda e: e.tensor_tensor(out=stg[0:48, 0:64], in0=stg[0:48, 64:128], in1=stg[0:48, 128:192], op=ALU.add), [B_stg], [B_stg])
        P.dma(dst48x64, stg[0:48, 0:64], [B_stg], [], B_stg)

    sb_keep = {"ssq": sb("ssqk", [128, 4, 8])[:], "ssq2": sb("ssqk2", [128, 4, 8])[:]}
    pm = {"mkT": sb("pmkT", [128, 4, 256], BF16), "mvb": sb("pmvb", [128, 2, 512], BF16), "B": P.buf("pmkv")}

    barrier()

    mem_kv_prompt(pm["mkT"], pm["mvb"], pm["B"])
    ck("memkv")

    pend = []
    for b in range(4):
        pend = block(x_p, B_const, y_p, B_yp, b * 512, 512, [(0, 512)], b == 0, b == 3, False,
                     skip_s1=(b > 0), next_x=((x_p, (b + 1) * 512) if b < 3 else None), pre11=pend, defer11=True)
        pass0["on"] = False
    block(x_s, B_const, y_s, B_ys, 0, 128, [(0, 64), (64, 64)], True, True, True, pre11=pend)

    global _P
    _P = P
    P.emit()
    es.close()
    return nc


_CACHE = {}


def _onehot():
    half, max_exact, nb = 16, 8, 32
    i = np.arange(64)[:, None, None]
    kc = np.arange(3)[None, :, None]
    j = np.arange(64)[None, None, :]
    rel = (kc * 64 + j) - 128 - i
    n = np.abs(rel)
    large = max_exact + (np.log(np.maximum(n, 1).astype(np.float32) / max_exact) / math.log(128 / max_exact) * (half - max_exact)).astype(np.int32)
    large = np.minimum(large, half - 1)
    bucket = np.where(rel > 0, half, 0) + np.where(n < max_exact, n, large)
    oh = (bucket[None] == np.arange(nb)[:, None, None, None]).astype(np.float32)
    return np.ascontiguousarray(oh.reshape(nb, 64 * 3 * 64))


def kernel(**inp):
    f = lambda a: np.ascontiguousarray(np.asarray(a, dtype=np.float32))
    if "nc" not in _CACHE:
        _CACHE["nc"] = build_program()
    nc = _CACHE["nc"]
    oh = _onehot()
    shared = {
        "rbt": f(inp["rel_bias_table"]), "oh": oh,
        "g_pm": f(inp["norm_pre_mix"]), "g_qm": f(inp["norm_post_mix"]), "g_pf": f(inp["norm_pre_ffn"]),
        "g_qf": f(inp["norm_post_ffn"]), "g_mem": f(inp["norm_mem"]),
        "w_in": f(inp["w_in"][0]), "sinks": f(inp["attn_sinks"]),
        "a_re": f(inp["ssm_a_re"][0]), "a_im": f(inp["ssm_a_im"][0]), "log_dt": f(inp["ssm_log_dt"]),
        "b_re": f(inp["ssm_b_re"][0]), "b_im": f(inp["ssm_b_im"][0]),
        "cc_re": f(inp["ssm_c_re"][0]), "cc_im": f(inp["ssm_c_im"][0]),
        "ssm_d": f(inp["ssm_d"][0].reshape(6, 128)), "w_glu": f(inp["w_glu"][0]), "w_mkv": f(inp["w_mem_kv"][0]),
        "w_out": f(inp["w_out"][0]), "w_up": f(inp["w_up"][0]), "conv_w": f(inp["conv_w"][0]),
        "conv_b": f(inp["conv_b"]), "w_down": f(inp["w_down"][0]),
    }
    in_maps = []
    for c in range(8):
        m = dict(shared)
        s = slice(2 * c, 2 * c + 2)
        m.update({
            "x_p": f(inp["x_prompt"][c]), "x_s": f(inp["x_sample"][s].reshape(128, D)),
            "c_ak": f(inp["cache_attn_k"][0, s].reshape(2, 128, 256)), "c_av": f(inp["cache_attn_v"][0, s].reshape(2, 128, 256)),
            "c_mk": f(inp["cache_mem_k"][0, s].reshape(2, 256, 512)), "c_mv": f(inp["cache_mem_v"][0, s].reshape(2, 256, 512)),
            "st_re": f(inp["state_ssm_re"][0, s]), "st_im": f(inp["state_ssm_im"][0, s]),
            "st_cv": f(inp["state_conv"][0, s]), "memp": f(inp["mem_prompt"][c]),
        })
        in_maps.append(m)
    res = run_bass_kernel_spmd(nc, in_maps, core_ids=list(range(8))).results
    g = lambda k: np.stack([np.asarray(r[k], dtype=np.float32) for r in res])
    cat = lambda k: np.concatenate([np.asarray(r[k], dtype=np.float32) for r in res], axis=0)
    return (
        g("y_p").reshape(8, 2048, D), cat("y_s").reshape(16, 64, D),
        g("o_pk").reshape(1, 8, 128, NKV, HD), g("o_pv").reshape(1, 8, 128, NKV, HD),
        g("o_pre").reshape(1, 8, 48, 64), g("o_pim").reshape(1, 8, 48, 64), g("o_pcv").reshape(1, 8, 2, DFF),
        g("o_pmk").reshape(1, 8, 256, 4, 128), g("o_pmv").reshape(1, 8, 256, 4, 128),
        cat("o_sk").reshape(1, 16, 128, NKV, HD), cat("o_sv").reshape(1, 16, 128, NKV, HD),
        cat("o_sre").reshape(1, 16, 48, 64), cat("o_sim").reshape(1, 16, 48, 64), cat("o_scv").reshape(1, 16, 2, DFF),
    )
```

```python
import math
from contextlib import ExitStack
import numpy as np
import concourse.bass as bass
import concourse.mybir as mybir
from concourse.bass_utils import run_bass_kernel_spmd

F32 = mybir.dt.float32
BF16 = mybir.dt.bfloat16
I32 = mybir.dt.int32
AF = mybir.ActivationFunctionType
ALU = mybir.AluOpType

D = 2048
KT = 16
NQ, NKV, HD = 12, 4, 64
DFF = 5632
FT = 44
INC = 8704
L = 64
NGP = 24
QPERM = [0, 3, 1, 4, 2, 5, 6, 9, 7, 10, 8, 11]
EPS = 1e-6
CW = 256


import types


def _snap(fn):
    if getattr(fn, "__closure__", None) is None:
        return fn
    cells = []
    for c in fn.__closure__:
        try:
            cells.append(types.CellType(c.cell_contents))
        except ValueError:
            cells.append(c)
    g = types.FunctionType(fn.__code__, fn.__globals__, fn.__name__, fn.__defaults__, tuple(cells))
    g.__kwdefaults__ = fn.__kwdefaults__
    return g


class Buf:
    def __init__(self, name):
        self.name = name
        self.lastw = None
        self.readers = []
        self.sem = None
        self.cnt = 0
        self.pw = []
        self.nobar = name in ("win", "wglu", "wmkv", "wout", "wup", "wdn", "outs")
        self.excl = name.startswith("ps")


class Prog:
    def __init__(self, nc, es):
        self.nc = nc
        self.es = es
        self.ops = {e: [] for e in ("pe", "act", "dve", "pool", "sp")}
        self.count = {e: 0 for e in ("pe", "act", "dve", "pool", "sp")}
        self.esem = {e: es.enter_context(nc.semaphore("s_" + e)) for e in ("pe", "act", "dve", "pool", "sp")}
        self.dsems = []
        self.bufs = {}
        self.stopped = False

    def buf(self, name):
        if name not in self.bufs:
            self.bufs[name] = Buf(name)
        return self.bufs[name]

    def _deps(self, eng, reads, writes):
        deps = set()
        for b in reads:
            if b.lastw is not None:
                deps.add(b.lastw)
            deps.update(b.pw)
            if b.excl:
                for r in b.readers:
                    if not (r[0] == "eng" and r[1] == eng):
                        deps.add(r)
        for b in writes:
            if b.lastw is not None:
                deps.add(b.lastw)
            deps.update(b.pw)
            for r in b.readers:
                deps.add(r)
        if eng == "pe":
            deps = {d for d in deps if not (d[0] == "eng" and d[1] == "pe")}
        return deps

    def _commit(self, me, reads, writes, par=False):
        for b in reads:
            b.readers.append(me)
        for b in writes:
            if par:
                b.pw.append(me)
            else:
                b.lastw = me
                b.pw = []
            b.readers = []

    def op(self, eng, fn, reads=(), writes=()):
        if self.stopped:
            return
        deps = self._deps(eng, reads, writes)
        self.count[eng] += 1
        me = ("eng", eng, self.count[eng])
        self.ops[eng].append((deps, _snap(fn), None))
        self._commit(me, reads, writes)

    def dma(self, out, in_, reads, writes, sembuf, q="sp", par=False):
        if self.stopped:
            return
        deps = self._deps(q, reads, writes)
        if par:
            drop = set()
            for b in writes:
                drop.update(b.pw)
                if b.lastw is not None and b.lastw[0] == "dma" and b.lastw[1] is sembuf and not b.readers:
                    drop.add(b.lastw)
            deps = {d for d in deps if d not in drop}
        if sembuf.sem is None:
            sembuf.sem = self.es.enter_context(self.nc.semaphore("d_" + sembuf.name))
            self.dsems.append(sembuf)
        sembuf.cnt += 16
        me = ("dma", sembuf, sembuf.cnt)
        self.ops[q].append((deps, (lambda e, o=out, i=in_: e.dma_start(out=o, in_=i)), sembuf))
        self._commit(me, reads, writes, par)

    def barrier(self):
        if self.stopped:
            return
        nop = lambda en: en.nop()
        deps = {("eng", e, self.count[e]) for e in ("pe", "act", "dve", "pool") if self.count[e]}
        deps |= {("dma", b, b.cnt) for b in self.dsems if not b.nobar}
        self.count["sp"] += 1
        self.ops["sp"].append((deps, nop, None))
        me = ("eng", "sp", self.count["sp"])
        for e in ("pe", "act", "dve", "pool"):
            self.count[e] += 1
            self.ops[e].append(({me}, nop, None))

    def emit(self):
        nc = self.nc
        with nc.Block() as block:
            def run(engname, e):
                known = {}
                for deps, fn, sembuf in self.ops[engname]:
                    need = {}
                    for d in deps:
                        key = (d[0], d[1] if d[0] == "eng" else id(d[1]))
                        sem = self.esem[d[1]] if d[0] == "eng" else d[1].sem
                        if known.get(key, 0) >= d[2]:
                            continue
                        if key not in need or need[key][1] < d[2]:
                            need[key] = (sem, d[2])
                    for key, (sem, v) in need.items():
                        e.wait_ge(sem, v)
                        known[key] = v
                    ins = fn(e)
                    if sembuf is not None:
                        ins.then_inc(sembuf.sem, 16)
                    else:
                        ins.then_inc(self.esem[engname], 1)
                if engname == "sp":
                    for en in ("pe", "act", "dve", "pool"):
                        if self.count[en]:
                            e.wait_ge(self.esem[en], self.count[en])
                    for b in self.dsems:
                        e.wait_ge(b.sem, b.cnt)

            @block.tensor
            def _(e):
                run("pe", e)

            @block.scalar
            def _(e):
                run("act", e)

            @block.vector
            def _(e):
                run("dve", e)

            @block.gpsimd
            def _(e):
                run("pool", e)

            @block.sync
            def _(e):
                run("sp", e)


STOP = None


def build_program():
    nc = bass.Bass("TRN2", target_bir_lowering=False)
    es = ExitStack()
    P = Prog(nc, es)

    def ck(name):
        if STOP == name:
            P.stopped = True

    def din(name, shape, dt=F32):
        return nc.dram_tensor(name, list(shape), dt, kind="ExternalInput").ap()

    def dout(name, shape):
        return nc.dram_tensor(name, list(shape), F32, kind="ExternalOutput").ap()

    def dscr(name, shape, dt):
        return nc.dram_tensor(name, list(shape), dt).ap()

    x_p = din("x_p", [2048, D]); x_s = din("x_s", [128, D])
    c_ak = din("c_ak", [2, 128, 256]); c_av = din("c_av", [2, 128, 256])
    c_mk = din("c_mk", [2, 256, 512]); c_mv = din("c_mv", [2, 256, 512])
    st_re = din("st_re", [2, 48, 64]); st_im = din("st_im", [2, 48, 64])
    st_cv = din("st_cv", [2, 2, DFF])
    memp = din("memp", [256, D])
    rbt = din("rbt", [32, 12]); oh = din("oh", [32, 64 * 3 * 64])
    g_pm = din("g_pm", [1, D]); g_qm = din("g_qm", [1, D]); g_pf = din("g_pf", [1, D]); g_qf = din("g_qf", [1, D])
    g_mem = din("g_mem", [1, D])
    w_in = din("w_in", [D, INC]); sinks = din("sinks", [1, 12])
    a_re = din("a_re", [48, 64]); a_im = din("a_im", [48, 64]); log_dt = din("log_dt", [1, 48])
    b_re = din("b_re", [48, 64, 16]); b_im = din("b_im", [48, 64, 16])
    cc_re = din("cc_re", [48, 16, 64]); cc_im = din("cc_im", [48, 16, 64])
    ssm_d = din("ssm_d", [6, 128]); w_glu = din("w_glu", [768, 768]); w_mkv = din("w_mkv", [D, 1024])
    w_out = din("w_out", [D, D]); w_up = din("w_up", [D, 2 * DFF]); conv_w = din("conv_w", [3, DFF])
    conv_b = din("conv_b", [1, DFF]); w_down = din("w_down", [DFF, D])
    y_p = dout("y_p", [2048, D]); y_s = dout("y_s", [128, D])
    o_pk = dout("o_pk", [128, 256]); o_pv = dout("o_pv", [128, 256])
    o_pre = dout("o_pre", [48, 64]); o_pim = dout("o_pim", [48, 64]); o_pcv = dout("o_pcv", [2, DFF])
    o_pmk = dout("o_pmk", [256, 512]); o_pmv = dout("o_pmv", [256, 512])
    o_sk = dout("o_sk", [2, 128, 256]); o_sv = dout("o_sv", [2, 128, 256])
    o_sre = dout("o_sre", [2, 48, 64]); o_sim = dout("o_sim", [2, 48, 64]); o_scv = dout("o_scv", [2, 2, DFF])
    win_b = dscr("win_b", [D, INC], BF16); wglu_b = dscr("wglu_b", [768, 768], BF16)
    wmkv_b = dscr("wmkv_b", [D, 1024], BF16); wout_b = dscr("wout_b", [D, D], BF16)
    wup_b = dscr("wup_b", [D, 2 * DFF], BF16); wdn_b = dscr("wdn_b", [DFF, D], BF16)
    mixs = dscr("mixs", [512, D], F32)
    rot_tab = dscr("rot_tab", [NGP, 2, 128, 512], F32)
    sgscr = dscr("sgscr", [D // CW, 4, 128, 3, CW], F32)
    B_win, B_wglu, B_wmkv, B_wout, B_wup, B_wdn, B_mixs = [P.buf(n) for n in
        ("win", "wglu", "wmkv", "wout", "wup", "wdn", "mixs")]
    origmap = {"win": w_in, "wglu": w_glu, "wmkv": w_mkv, "wout": w_out, "wup": w_up, "wdn": w_down}
    B_yp, B_ys = P.buf("yp"), P.buf("ys")
    B_rot = P.buf("rot"); B_sgscr = P.buf("sgscr")
    B_out = P.buf("outs")

    def sb(name, shape, dt=F32):
        return es.enter_context(nc.sbuf_tensor(name, list(shape), dt))

    def ps(name, shape, dt=F32):
        return es.enter_context(nc.psum_tensor(name, list(shape), dt))

    psf = [(ps(f"psf{i}", [128, 512]), P.buf(f"psf{i}")) for i in range(5)]
    psy = (ps("psy", [128, 512]), P.buf("psy"))
    psb = [(ps(f"psb{i}", [128, 1024], BF16), P.buf(f"psb{i}")) for i in range(2)]
    rr = {"f": 0, "b": 0, "w": 0, "ev": 0}

    def nps():
        rr["f"] = (rr["f"] + 1) % 5
        return psf[rr["f"]]

    def npsb():
        rr["b"] = (rr["b"] + 1) % 2
        return psb[rr["b"]]

    NW = 4
    wbufs = [(sb(f"wb{i}", [128, KT, CW], BF16), P.buf(f"wb{i}")) for i in range(NW)]

    pass0 = {"on": True}

    def wload(src, bsrc, k0, nk, c0, ncol):
        rr["w"] = (rr["w"] + 1) % NW
        t, b = wbufs[rr["w"]]
        v = src.rearrange("(kt p) n -> p kt n", p=128)
        if not pass0["on"]:
            P.dma(t[:, 0:nk, 0:ncol], v[:, k0:k0 + nk, c0:c0 + ncol], [bsrc], [b], b)
            return t, b
        orig = origmap[bsrc.name]
        ov = orig.rearrange("(kt p) n -> p kt n", p=128)
        first = [True]

        bsw = P.buf(f"wbsw{rr['w']}"); bst = P.buf(f"wbst{rr['w']}")

        def ld(dst, srcap):
            P.dma(dst, srcap, [], [b], bsw, q="pool", par=not first[0])
            first[0] = False
        if bsrc.name == "win" and c0 < 768:
            for j in range(ncol // 64):
                h = QPERM[(c0 // 64) + j]
                ld(t[:, 0:nk, j * 64:(j + 1) * 64], ov[:, k0:k0 + nk, h * 64:(h + 1) * 64])
        elif bsrc.name == "wout":
            for k in range(k0, k0 + nk):
                if k < 6:
                    for hf in range(2):
                        h = QPERM[2 * k + hf]
                        ld(t[hf * 64:(hf + 1) * 64, k - k0, 0:ncol], orig[h * 64:(h + 1) * 64, c0:c0 + ncol])
            ka = max(k0, 6)
            if ka < k0 + nk:
                ld(t[:, ka - k0:nk, 0:ncol], ov[:, ka:k0 + nk, c0:c0 + ncol])
        else:
            ld(t[:, 0:nk, 0:ncol], ov[:, k0:k0 + nk, c0:c0 + ncol])
        if bsrc.name != "wmkv":
            P.dma(v[:, k0:k0 + nk, c0:c0 + ncol], t[:, 0:nk, 0:ncol], [b], [bsrc], bst, par=True)
        return t, b

    def evac(out, in_, reads, writes, scale=None):
        rr["ev"] ^= 1
        if rr["ev"]:
            if scale is None:
                P.op("act", lambda e: e.activation(out=out, in_=in_, func=AF.Copy), reads, writes)
            else:
                P.op("act", lambda e: e.activation(out=out, in_=in_, func=AF.Copy, scale=scale), reads, writes)
        else:
            if scale is None:
                P.op("dve", lambda e: e.tensor_copy(out=out, in_=in_), reads, writes)
            else:
                P.op("dve", lambda e: e.tensor_scalar(out=out, in0=in_, scalar1=scale, scalar2=None, op0=ALU.mult), reads, writes)

    def mm(out, pairs, reads, writes):
        def fn(e, out=out, pairs=pairs):
            n = len(pairs)
            ins = None
            for i, (l, r) in enumerate(pairs):
                ins = e.matmul(out, lhsT=l, rhs=r, start=(i == 0), stop=(i == n - 1))
            return ins
        P.op("pe", fn, reads, writes)

    ident = sb("ident", [128, 128]); identb = sb("identb", [128, 128], BF16)
    B_const = P.buf("const")
    ones_b = sb("ones_b", [128, 128], BF16)
    gT = sb("gT", [128, 3, KT])
    gbc = sb("gbc", [128, 2, D], BF16)
    cw = sb("cw", [128, 4, FT])
    dT = sb("dT", [128, 6])
    Ddiag = sb("Ddiag", [128, 6, 128], BF16)
    epsb = sb("epsb", [128, 1])
    biasT = sb("biasT", [64, 3, 64, 12])
    esk = sb("esk", [128, 12])
    rS = sb("rS", [128, NGP])
    tabs = [sb(f"tab{i}", [128, 2, 512]) for i in range(2)]; B_tabs = [P.buf("tab0"), P.buf("tab1")]
    Bre = sb("Bre", [128, NGP, 128], BF16); Bim = sb("Bim", [128, NGP, 128], BF16)
    Cre = sb("Cre", [128, NGP, 128], BF16); Cim = sb("Cim", [128, NGP, 128], BF16)
    sst = sb("sst", [128, 2, NGP])
    cvst = sb("cvst", [128, FT, 2])
    kTt = sb("kTt", [128, 2, 128 + 512], BF16)
    v64 = sb("v64", [64, 10, NKV, 128], BF16)
    B_kT, B_v64, B_sst, B_cvst = P.buf("kT"), P.buf("v64"), P.buf("sst"), P.buf("cvst")
    xt = sb("xt", [128, D]); mt = sb("mt", [128, D]); B_xt, B_mt = P.buf("xt"), P.buf("mt")
    hb = sb("hb", [128, D], BF16); B_hb = P.buf("hb")
    stat = sb("stat", [128, 8]); B_stat = P.buf("stat")
    stg = sb("stg", [128, 512]); B_stg = P.buf("stg")
    ARENA = 82 * 1024
    arena = sb("arena", [128, ARENA // 4])
    B_arena = P.buf("arena")
    off = {"v": 0}

    def carve(shape, dt):
        n = 1
        for s in shape[1:]:
            n *= s
        nbytes = n * (2 if dt == BF16 else 4)
        nbytes = (nbytes + 31) // 32 * 32
        o = off["v"]
        assert o + nbytes <= ARENA, (o, nbytes)
        off["v"] = o + nbytes
        flat = arena[0:shape[0], o // 4:(o + nbytes) // 4]
        if dt != F32:
            flat = flat.bitcast(dt)
        flat = flat[:, 0:n]
        if len(shape) == 2:
            return flat
        names = " ".join(f"d{i}" for i in range(1, len(shape)))
        return flat.rearrange(f"p ({names}) -> p {names}", **{f"d{i}": shape[i] for i in range(2, len(shape))})

    def barrier():
        P.barrier()
        off["v"] = 0

    def rewind(mark):
        P.barrier()
        off["v"] = mark

    def cast(dst, src, bdst, rows=None):
        P.dma(dst, src, [], [bdst], bdst, q="pool")

    def casts_a():
        cast(wmkv_b[:, :], w_mkv[:, :], B_wmkv)
        for s_ in range(12):
            h = QPERM[s_]
            cast(win_b[:, s_ * 64:(s_ + 1) * 64], w_in[:, h * 64:(h + 1) * 64], B_win)
        for r in range(4):
            cast(win_b[r * 512:(r + 1) * 512, 768:INC], w_in[r * 512:(r + 1) * 512, 768:INC], B_win)

    def casts_b():
        cast(wglu_b[:, :], w_glu[:, :], B_wglu)
        for s_ in range(12):
            h = QPERM[s_]
            cast(wout_b[s_ * 64:(s_ + 1) * 64, :], w_out[h * 64:(h + 1) * 64, :], B_wout)
        cast(wout_b[768:D, :], w_out[768:D, :], B_wout)
        for r in range(4):
            cast(wup_b[r * 512:(r + 1) * 512, :], w_up[r * 512:(r + 1) * 512, :], B_wup)
        for r in range(4):
            cast(wdn_b[r * 1408:(r + 1) * 1408, :], w_down[r * 1408:(r + 1) * 1408, :], B_wdn)

    ck("cast")
    P.op("dve", lambda e: e.memset(epsb[:], EPS), [], [B_const])
    idc = carve([128, 128], F32); idr = carve([128, 128], F32); idx = None
    nat = sb("nat_p", [128, 128]); B_nat = P.buf("nat")
    P.op("pool", lambda e: e.iota(idc[:], pattern=[[1, 128]], base=0, channel_multiplier=0,
                                  allow_small_or_imprecise_dtypes=True), [], [B_const])
    P.op("pool", lambda e: e.iota(idr[:], pattern=[[0, 128]], base=0, channel_multiplier=1,
                                  allow_small_or_imprecise_dtypes=True), [], [B_const])
    for i, g in enumerate((g_qm, g_qf)):
        P.dma(gbc[:, i, :], g.broadcast_to([128, D]), [], [B_const], B_const, q="pool")
    P.op("dve", lambda e: e.tensor_tensor(out=ident[:], in0=idc[:], in1=idr[:], op=ALU.is_equal), [B_const], [B_const])
    P.op("dve", lambda e: e.tensor_copy(out=identb[:], in_=ident[:]), [B_const], [B_const])
    P.op("dve", lambda e: e.memset(ones_b[:], 1.0), [], [B_const])
    P.op("dve", lambda e: e.memset(sst[:], 0.0), [], [B_sst])
    P.op("dve", lambda e: e.memset(cvst[:], 0.0), [], [B_cvst])


    def load_T(dst, src_rows_ap, nrows, ncols_in=128):
        P.dma(nat[0:nrows, 0:128], src_rows_ap, [], [B_nat], B_nat)
        pt, pb = nps()
        P.op("pe", lambda e: e.transpose(pt[:, 0:nrows], nat[0:nrows, 0:128], ident[0:nrows, 0:nrows]),
             [B_nat, B_const], [pb])
        P.op("dve", lambda e: e.tensor_copy(out=dst, in_=pt[:, 0:nrows]), [pb], [B_const])

    for i, g in enumerate((g_pm, g_pf, g_mem)):
        load_T(gT[:, i, :], g.rearrange("o (k p) -> (o k) p", p=128), 16)
    for i in range(3):
        load_T(cw[:, i, :], conv_w[i:i + 1, :].rearrange("o (k p) -> (o k) p", p=128), FT)
    load_T(cw[:, 3, :], conv_b.rearrange("o (k p) -> (o k) p", p=128), FT)
    load_T(dT[:, :], ssm_d[:, :], 6)
    for t in range(6):
        P.op("dve", lambda e, t=t: e.tensor_scalar(out=Ddiag[:, t, :], in0=ident[:], scalar1=dT[:, t:t + 1],
                                                   scalar2=None, op0=ALU.mult), [B_const], [B_const])
    P.dma(nat[:, 0:12], sinks.broadcast_to([128, 12]), [], [B_nat], B_nat)
    P.op("act", lambda e: e.activation(out=esk[:], in_=nat[:, 0:12], func=AF.Exp), [B_nat], [B_const])

    oht = carve([32, 64 * 3 * 64], F32); B_oh = P.buf("oh")
    tb = carve([32, 12], F32)
    P.dma(oht[:, :], oh[:, :], [], [B_oh], B_oh)
    P.dma(tb[:, :], rbt[:, :], [], [B_oh], B_oh)
    ohv = oht.rearrange("b (i k j) -> b i k j", i=64, k=3)
    for kc in range(3):
        for ih in range(2):
            pt, pb = nps()
            def fn(e, pt=pt, kc=kc, ih=ih):
                ins = None
                for ii in range(32):
                    i = ih * 32 + ii
                    ins = e.matmul(pt[0:64, ii * 12:(ii + 1) * 12], lhsT=ohv[:, i, kc, :], rhs=tb[:, :],
                                   start=True, stop=True)
                return ins
            P.op("pe", fn, [B_oh], [pb])
            P.op("dve", lambda e, pt=pt, kc=kc, ih=ih: e.tensor_copy(
                out=biasT[:, kc, ih * 32:(ih + 1) * 32, :].rearrange("p i h -> p (i h)"), in_=pt[0:64, 0:384]),
                [pb], [B_const])

    ck("const")
    barrier()

    def to_state(dst, src48x64):
        P.dma(nat[0:48, 0:64], src48x64, [], [B_nat], B_nat)
        P.dma(nat[0:48, 64:128], src48x64, [], [B_nat], B_nat, par=True)
        pt, pb = nps()
        P.op("pe", lambda e: e.transpose(pt[:, 0:48], nat[0:48, 0:128], ident[0:48, 0:48]), [B_nat, B_const], [pb])
        pv = pt[:, 0:48].rearrange("p (g two) -> p g two", two=2)
        P.op("dve", lambda e: e.tensor_copy(out=dst[0:64, :], in_=pv[0:64, :, 0]), [pb], [B_const])
        P.op("dve", lambda e: e.tensor_copy(out=dst[64:128, :], in_=pv[64:128, :, 1]), [pb], [B_const])

    lre = carve([128, NGP], F32); lim = carve([128, NGP], F32); dts = carve([128, NGP], F32)
    th = carve([128, NGP], F32)
    to_state(lre, a_re[:, :]); to_state(lim, a_im[:, :])
    P.dma(nat[:, 0:48], log_dt.broadcast_to([128, 48]), [], [B_nat], B_nat)
    nv = nat[:, 0:48].rearrange("p (g two) -> p g two", two=2)
    P.op("act", lambda e: e.activation(out=dts[0:64, :], in_=nv[0:64, :, 0], func=AF.Exp), [B_nat], [B_const])
    P.op("act", lambda e: e.activation(out=dts[64:128, :], in_=nv[64:128, :, 1], func=AF.Exp), [B_nat], [B_const])
    P.op("dve", lambda e: e.tensor_tensor(out=th[:, :], in0=lim, in1=dts, op=ALU.mult), [B_const], [B_const])
    tmpa = carve([128, NGP], F32)
    P.op("dve", lambda e: e.tensor_tensor(out=tmpa, in0=lre, in1=dts, op=ALU.mult), [B_const], [B_const])
    P.op("act", lambda e: e.activation(out=rS[:, :], in_=tmpa, func=AF.Exp), [B_const], [B_const])
    idx = carve([128, 512], F32)
    P.op("pool", lambda e: e.iota(idx, pattern=[[1, 512]], base=1, channel_multiplier=0,
                                  allow_small_or_imprecise_dtypes=True), [], [B_const])
    LIM = 3.14159
    c0t = carve([128, NGP], F32); s0t = carve([128, NGP], F32)
    tmps = [[carve([128, 512], F32), carve([128, 512], I32), carve([128, 512], F32), carve([128, 512], F32)] for _ in range(3)]
    B_tt = [P.buf("tt0"), P.buf("tt1"), P.buf("tt2")]
    it = 0
    for gp in range(NGP):
        for fn_i, shift in ((0, math.pi / 2), (1, 0.0)):
            ang, ki, kf, sn = tmps[it % 3]; Bt = B_tt[it % 3]; it += 1
            P.op("dve", lambda e, ang=ang, gp=gp, shift=shift: e.tensor_scalar(out=ang, in0=idx, scalar1=th[:, gp:gp + 1], scalar2=shift,
                                                                              op0=ALU.mult, op1=ALU.add), [B_const, Bt], [Bt])
            P.op("dve", lambda e, ang=ang, ki=ki: e.tensor_scalar(out=ki, in0=ang, scalar1=1.0 / (2 * math.pi), scalar2=None, op0=ALU.mult), [Bt], [Bt])
            P.op("dve", lambda e, kf=kf, ki=ki: e.tensor_copy(out=kf, in_=ki), [Bt], [Bt])
            P.op("dve", lambda e, ang=ang, kf=kf: e.scalar_tensor_tensor(out=ang, in0=kf, scalar=-2 * math.pi, in1=ang, op0=ALU.mult, op1=ALU.add), [Bt], [Bt])
            P.op("dve", lambda e, ang=ang: e.tensor_scalar(out=ang, in0=ang, scalar1=-LIM, scalar2=LIM, op0=ALU.max, op1=ALU.min), [Bt], [Bt])
            P.op("act", lambda e, ang=ang, sn=sn: e.activation(out=sn, in_=ang, func=AF.Sin), [Bt], [Bt])
            dstc = c0t if fn_i == 0 else s0t
            P.op("act", lambda e, sn=sn, dstc=dstc, gp=gp: e.activation(out=dstc[:, gp:gp + 1], in_=sn[:, 0:1], func=AF.Copy), [Bt], [B_const])
            P.dma(rot_tab[gp, fn_i], sn, [Bt], [B_rot], Bt)
    nre = carve([128, NGP], F32); nim = carve([128, NGP], F32); den = carve([128, NGP], F32)
    cre = carve([128, NGP], F32); cim = carve([128, NGP], F32); t1 = carve([128, NGP], F32); t2 = carve([128, NGP], F32)
    V = lambda fn, r=(B_const,), w=(B_const,): P.op("dve", fn, list(r), list(w))
    V(lambda e: e.tensor_tensor(out=nre, in0=rS[:, :], in1=c0t, op=ALU.mult))
    V(lambda e: e.tensor_scalar(out=nre, in0=nre, scalar1=-1.0, scalar2=None, op0=ALU.add))
    V(lambda e: e.tensor_tensor(out=nim, in0=rS[:, :], in1=s0t, op=ALU.mult))
    V(lambda e: e.tensor_tensor(out=den, in0=lre, in1=lre, op=ALU.mult))
    V(lambda e: e.tensor_tensor(out=t1, in0=lim, in1=lim, op=ALU.mult))
    V(lambda e: e.tensor_tensor(out=den, in0=den, in1=t1, op=ALU.add))
    V(lambda e: e.reciprocal(out=den, in_=den))
    V(lambda e: e.tensor_tensor(out=t1, in0=nre, in1=lre, op=ALU.mult))
    V(lambda e: e.tensor_tensor(out=t2, in0=nim, in1=lim, op=ALU.mult))
    V(lambda e: e.tensor_tensor(out=t1, in0=t1, in1=t2, op=ALU.add))
    V(lambda e: e.tensor_tensor(out=cre, in0=t1, in1=den, op=ALU.mult))
    V(lambda e: e.tensor_tensor(out=t1, in0=nim, in1=lre, op=ALU.mult))
    V(lambda e: e.tensor_tensor(out=t2, in0=nre, in1=lim, op=ALU.mult))
    V(lambda e: e.tensor_tensor(out=t1, in0=t1, in1=t2, op=ALU.subtract))
    V(lambda e: e.tensor_tensor(out=cim, in0=t1, in1=den, op=ALU.mult))
    Zre = carve([128, NGP, 128], F32); Zim = carve([128, NGP, 128], F32); Zt = carve([128, NGP, 128], F32)
    B_Z = P.buf("Z")
    P.op("dve", lambda e: e.memset(Zre, 0.0), [], [B_Z])
    P.op("dve", lambda e: e.memset(Zim, 0.0), [], [B_Z])
    for (Z, src) in ((Zre, b_re), (Zim, b_im)):
        sv = src.rearrange("(t j two) p c -> two j p t c", j=4, two=2)
        Zv = Z.rearrange("p (t j) m -> p j t m", j=4)
        for gpar in range(2):
            for j in range(4):
                c0 = 32 * j + 16 * gpar
                P.dma(Zv[64 * gpar:64 * gpar + 64, j, :, c0:c0 + 16], sv[gpar, j], [], [B_Z], B_Z, par=True)
    bc = lambda t: t.unsqueeze(2).to_broadcast([128, NGP, 128])
    VZ = lambda fn: P.op("dve", fn, [B_Z, B_const], [B_Z])
    VZ(lambda e: e.tensor_tensor(out=Zt, in0=Zim, in1=bc(cim), op=ALU.mult))
    VZ(lambda e: e.tensor_tensor(out=Zim, in0=Zim, in1=bc(cre), op=ALU.mult))
    Zt2 = carve([128, NGP, 128], F32)
    VZ(lambda e: e.tensor_tensor(out=Zt2, in0=Zre, in1=bc(cim), op=ALU.mult))
    VZ(lambda e: e.tensor_tensor(out=Zim, in0=Zim, in1=Zt2, op=ALU.add))
    VZ(lambda e: e.tensor_tensor(out=Zre, in0=Zre, in1=bc(cre), op=ALU.mult))
    VZ(lambda e: e.tensor_tensor(out=Zre, in0=Zre, in1=Zt, op=ALU.subtract))
    for (Z, dst) in ((Zre, Bre), (Zim, Bim)):
        for gp in range(NGP):
            pt, pb = nps()
            P.op("pe", lambda e, pt=pt, Z=Z, gp=gp: e.transpose(pt[:, 0:128], Z[:, gp, :], ident[:, :]), [B_Z, B_const], [pb])
            evac(dst[:, gp, :], pt[:, 0:128], [pb], [B_const])
    B_W = P.buf("W")
    for (src, dst, sc) in ((cc_re, Cre, None), (cc_im, Cim, -1.0)):
        Wt = Zt if src is cc_re else Zt2
        P.op("dve", lambda e, Wt=Wt: e.memset(Wt, 0.0), [B_Z], [B_W, B_Z])
        sv = src.rearrange("(t j two) c p -> two j c t p", j=4, two=2)
        Wv = Wt.rearrange("p (t j) m -> p j t m", j=4)
        for gpar in range(2):
            for j in range(4):
                r0 = 32 * j + 16 * gpar
                P.dma(Wv[r0:r0 + 16, j, :, 64 * gpar:64 * gpar + 64], sv[gpar, j], [], [B_W], B_W, par=True)
        for gp in range(NGP):
            pt, pb = nps()
            P.op("pe", lambda e, pt=pt, Wt=Wt, gp=gp: e.transpose(pt[:, 0:128], Wt[:, gp, :], ident[:, :]), [B_W, B_const], [pb])
            evac(dst[:, gp, :], pt[:, 0:128], [pb], [B_const], scale=sc)

    ck("ssmsetup")
    def rms_and_T(src_rows, bsrc, ntile, hT, B_hT, gidx, ntok_tile=128, extra=None):
        xb_, Bx_ = (xt, B_xt) if ntile % 2 == 0 else (mt, B_mt)
        P.dma(xb_[0:ntok_tile, :], src_rows, [bsrc], [Bx_], Bx_)
        norm_T(xb_, Bx_, ntile, hT, B_hT, gidx, ntok_tile)

    def rstd_from(col_in, col_out, n):
        P.op("act", lambda e: e.activation(out=stat[0:n, col_out:col_out + 1], in_=stat[0:n, col_in:col_in + 1], func=AF.Ln,
                                           scale=1.0 / D, bias=epsb[0:n, :]), [B_stat, B_const], [B_stat])
        P.op("act", lambda e: e.activation(out=stat[0:n, col_out:col_out + 1], in_=stat[0:n, col_out:col_out + 1], func=AF.Exp,
                                           scale=-0.5), [B_stat], [B_stat])

    ncall = {"n": 0}

    def norm_T(src, bsrc, ntile, hT, B_hT, gidx, n=128):
        ncall["n"] += 1
        ck(f"c{ncall['n']}_n0")
        P.op("dve", lambda e: e.memset(stat[:, 0:2], 0.0), [], [B_stat])
        P.op("act", lambda e: e.activation(out=hb[0:n, :], in_=src[0:n, :], func=AF.Square, accum_out=stat[0:n, 0:1]),
             [bsrc, B_stat], [B_hb, B_stat])
        rstd_from(0, 1, n)
        P.op("dve", lambda e: e.tensor_scalar(out=hb[0:n, :], in0=src[0:n, :], scalar1=stat[0:n, 1:2], scalar2=None,
                                              op0=ALU.mult), [bsrc, B_stat], [B_hb])
        ck(f"c{ncall['n']}_n1")
        for q4 in range(2):
            pt, pb = npsb()
            def fn(e, pt=pt, q4=q4):
                ins = None
                for j in range(8):
                    kt = q4 * 8 + j
                    ins = e.transpose(pt[:, j * 128:j * 128 + n], hb[0:n, kt * 128:(kt + 1) * 128], identb[0:n, 0:n])
                return ins
            P.op("pe", fn, [B_hb, B_const], [pb])
            ck(f"c{ncall['n']}_n2")
            for j in range(8):
                kt = q4 * 8 + j
                evac(hT[:, kt, ntile * 128:ntile * 128 + n], pt[:, j * 128:j * 128 + n], [pb, B_const], [B_hT],
                     scale=gT[:, gidx, kt:kt + 1])
                ck(f"c{ncall['n']}_n3_{q4}_{j}")

    def proj_fm(dst_fn, src, bsrc, c0, ncols, hT, B_hT, ntok, nk=KT, k0=0, tok0=0):
        for cb in range(0, ncols, CW):
            ncb = min(CW, ncols - cb)
            wt, wb = wload(src, bsrc, k0, nk, c0 + cb, ncb)
            for ct in range(ncb // 128):
                pt, pb = nps()
                mm(pt[:, 0:ntok], [(wt[:, k, ct * 128:(ct + 1) * 128], hT[:, k, tok0:tok0 + ntok]) for k in range(nk)],
                   [wb, B_hT], [pb])
                dst, bd = dst_fn((cb // 128) + ct)
                evac(dst, pt[:, 0:ntok], [pb], [bd])

    def mem_kv_prompt(mkT, mvb, B_mkv):
        hmT = carve([128, KT, 256], BF16); B_hm = P.buf("hmT")
        for t in range(2):
            rms_and_T(memp[t * 128:(t + 1) * 128, :], B_const, t, hmT, B_hm, 2)
        ck("mk1")
        for cb in range(0, 1024, CW):
            wt, wb = wload(wmkv_b, B_wmkv, 0, KT, cb, CW)
            ck("mk1a")
            for t in range(2):
                pt, pb = nps()
                mm(pt[:, 0:CW], [(hmT[:, k, t * 128:(t + 1) * 128], wt[:, k, 0:CW]) for k in range(KT)], [wb, B_hm], [pb])
                ck("mk1b")
                P.op("act", lambda e, pt=pt: e.activation(out=stg[:, 0:CW], in_=pt[:, 0:CW], func=AF.Copy), [pb], [B_stg])
                ck("mk1c")
                dst = o_pmk if cb < 512 else o_pmv
                P.dma(dst[t * 128:(t + 1) * 128, (cb % 512):(cb % 512) + CW], stg[:, 0:CW], [B_stg], [], B_stg)
                ck("mk1d")
                if cb >= 512:
                    P.op("dve", lambda e, pt=pt, t=t, cb=cb: e.tensor_copy(out=mvb[:, t, cb - 512:cb - 512 + CW], in_=pt[:, 0:CW]),
                         [pb], [B_mkv])
                ck(f"mk_{cb}_{t}")
        ck("mk2")
        proj_fm(lambda ct: (mkT[:, ct, :], B_mkv), wmkv_b, B_wmkv, 0, 512, hmT, B_hm, 256)

    def mem_kv_sample(mkT, mvb, B_mkv, s):
        f = carve([128, 2, 512], F32); fb = carve([128, 2, 512], BF16); B_f = P.buf("mkf")
        P.dma(f, c_mk[s].rearrange("(t p) n -> p t n", p=128), [], [B_f], B_f)
        P.op("dve", lambda e: e.tensor_copy(out=fb, in_=f), [B_f], [B_f])
        for hh in range(4):
            pt, pb = npsb()
            def fn(e, pt=pt, hh=hh):
                ins = None
                for t in range(2):
                    ins = e.transpose(pt[:, t * 128:(t + 1) * 128], fb[:, t, hh * 128:(hh + 1) * 128], identb[:, :])
                return ins
            P.op("pe", fn, [B_f, B_const], [pb])
            evac(mkT[:, hh, :], pt[:, 0:256], [pb], [B_mkv])
        f2 = carve([128, 2, 512], F32); B_f2 = P.buf("mvf")
        P.dma(f2, c_mv[s].rearrange("(t p) n -> p t n", p=128), [], [B_f2], B_f2)
        P.op("act", lambda e: e.activation(out=mvb, in_=f2, func=AF.Copy), [B_f2], [B_mkv])

    def block(x_src, B_xsrc, y_dst, B_y, tok0, NT, seqs, first, last, sample, skip_s1=False, next_x=None, pre11=None, defer11=False):
        nt = NT // 128
        barrier()
        hT = carve([128, KT, NT], BF16); B_hT = P.buf("hT")
        uT = carve([128, 6, NT], BF16)
        B_q, B_u, B_qm = P.buf("qT"), P.buf("uT"), P.buf("qmT")
        oaT = carve([128, 6, NT], BF16); osT = carve([128, 6, NT], BF16); omT = carve([128, 4, NT], BF16)
        B_oa, B_os, B_om = P.buf("oaT"), P.buf("osT"), P.buf("omT")
        yT = carve([128, 6, NT], BF16); B_yT = P.buf("yT")
        nseq = len(seqs)
        mkTs = [carve([128, 4, 256], BF16) if sample else None for _ in range(nseq)]
        mvbs = [carve([128, 2, 512], BF16) if sample else None for _ in range(nseq)]
        B_mkvs = [P.buf("mkv") for _ in range(nseq)]
        markS = off["v"]
        qT = carve([128, 6, NT], BF16); qmT = carve([128, 4, NT], BF16)
        mark0 = off["v"]
        if sample:
            for s in range(nseq):
                mem_kv_sample(mkTs[s], mvbs[s], B_mkvs[s], s)
        else:
            mkTs[0], mvbs[0], B_mkvs[0] = pm["mkT"], pm["mvb"], pm["B"]
        if not skip_s1:
            for t in range(nt):
                rms_and_T(x_src[tok0 + t * 128:tok0 + (t + 1) * 128, :], B_xsrc, t, hT, B_hT, 0)
        ck(f"b{tok0 // 512 if not sample else 4}s1")
        if sample:
            for s in range(nseq):
                kf32 = carve([128, 256], F32); kb16 = carve([128, 256], BF16); Bk = P.buf("kc")
                P.dma(kf32, c_ak[s], [], [Bk], Bk)
                P.op("dve", lambda e, kb16=kb16, kf32=kf32: e.tensor_copy(out=kb16, in_=kf32), [Bk], [Bk])
                pt, pb = npsb()
                def fn(e, pt=pt, kb16=kb16):
                    ins = None
                    for t in range(2):
                        ins = e.transpose(pt[:, t * 128:(t + 1) * 128], kb16[:, t * 128:(t + 1) * 128], identb[:, :])
                    return ins
                P.op("pe", fn, [Bk, B_const], [pb])
                evac(kTt[:, :, s * 192:s * 192 + 128], pt[:, 0:256].rearrange("p (t n) -> p t n", t=2), [pb], [B_kT])
                vf32 = carve([64, 2, 256], F32); Bv = P.buf("vc")
                P.dma(vf32, c_av[s].rearrange("(c p) n -> p c n", p=64), [], [Bv], Bv)
                vv = vf32.rearrange("p c (h d) -> p c h d", h=NKV)
                for dup in range(2):
                    P.op("dve", lambda e, s=s, dup=dup, vv=vv: e.tensor_copy(out=v64[:, s * 3:s * 3 + 2, :, dup * 64:(dup + 1) * 64], in_=vv),
                         [Bv], [B_v64])
                P.dma(o_sk[s, 0:64, :], c_ak[s, 64:128, :], [], [], B_out)
                P.dma(o_sv[s, 0:64, :], c_av[s, 64:128, :], [], [], B_out)
        if sample:
            kcol = [s * 192 + 128 for s in range(nseq)]
            vch = [s * 3 + 2 for s in range(nseq)]
        else:
            kcol = [128]
            vch = [2]
        def dst_q(ct):
            return qT[:, ct, :], B_q
        pre11 = list(pre11 or [])
        proj_fm(dst_q, win_b, B_win, 0, 768, hT, B_hT, NT)
        if pre11:
            pre11.pop(0)()
        for cb in range(0, 256, CW):
            wt, wb = wload(win_b, B_win, 0, KT, 768 + cb, CW)
            for ct in range(2):
                pt, pb = nps()
                mm(pt[:, 0:NT], [(wt[:, k, ct * 128:(ct + 1) * 128], hT[:, k, 0:NT]) for k in range(KT)], [wb, B_hT], [pb])
                for si, (c0, n) in enumerate(seqs):
                    evac(kTt[:, ct, kcol[si]:kcol[si] + n], pt[:, c0:c0 + n], [pb], [B_kT])
            if last or sample:
                for si, (c0, n) in enumerate(seqs):
                    r0, nr = (c0 + n - 128, 128) if not sample else (c0, 64)
                    pt, pb = nps()
                    mm(pt[0:nr, 0:256], [(hT[:, k, r0:r0 + nr], wt[:, k, 0:256]) for k in range(KT)], [wb, B_hT], [pb])
                    P.op("act", lambda e, pt=pt, nr=nr: e.activation(out=stg[0:nr, 0:256], in_=pt[0:nr, 0:256], func=AF.Copy), [pb], [B_stg])
                    dst = o_sk[si, 64:128, :] if sample else o_pk[:, :]
                    P.dma(dst, stg[0:nr, 0:256], [B_stg], [], B_stg)
        wt, wb = wload(win_b, B_win, 0, KT, 1024, CW)
        for si, (c0, n) in enumerate(seqs):
            for c in range(n // 64):
                pt, pb = nps()
                mm(pt[0:64, 0:256], [(hT[:, k, c0 + c * 64:c0 + (c + 1) * 64], wt[:, k, 0:256]) for k in range(KT)], [wb, B_hT], [pb])
                pv = pt[0:64, 0:256].rearrange("p (h d) -> p h d", h=NKV)
                for dup in range(2):
                    evac(v64[:, vch[si] + c, :, dup * 64:(dup + 1) * 64], pv, [pb], [B_v64])
                is_out = sample or (last and c >= n // 64 - 2)
                if is_out:
                    P.op("act", lambda e, pt=pt: e.activation(out=stg[0:64, 256:512], in_=pt[0:64, 0:256], func=AF.Copy), [pb], [B_stg])
                    if sample:
                        dst = o_sv[si, 64:128, :]
                    else:
                        cc = c - (n // 64 - 2)
                        dst = o_pv[cc * 64:(cc + 1) * 64, :]
                    P.dma(dst, stg[0:64, 256:512], [B_stg], [], B_stg)
        if pre11:
            pre11.pop(0)()
        proj_fm(lambda ct: (uT[:, ct, :], B_u), win_b, B_win, 1280, 768, hT, B_hT, NT)
        if pre11:
            pre11.pop(0)()
        proj_fm(lambda ct: (qmT[:, ct, :], B_qm), win_b, B_win, 2048, 512, hT, B_hT, NT)
        while pre11:
            pre11.pop(0)()

        ck(f"b{tok0 // 512 if not sample else 4}s2")
        if off["v"] != mark0:
            rewind(mark0)
        scs = [carve([64, 3, 192], F32) for _ in range(2)]
        pTs = [carve([64, 3, 192], BF16) for _ in range(2)]
        dns = [carve([128, 192], F32) for _ in range(2)]
        B_scs = [[P.buf(f"sc{a_}{k_}") for k_ in range(3)] for a_ in range(2)]
        B_pTs = [[P.buf(f"pT{a_}{k_}") for k_ in range(3)] for a_ in range(2)]
        B_dns = [P.buf("dn0"), P.buf("dn1")]
        ai = 0
        for si, (c0, n) in enumerate(seqs):
            nch = n // 64
            for c in range(nch):
                kcs = [kc for kc in range(3) if sample or (not first) or (c - 2 + kc) >= 0]
                for h in range(NKV):
                    sc, pT, dn = scs[ai % 2], pTs[ai % 2], dns[ai % 2]
                    B_sc, B_pT, B_dn = B_scs[ai % 2], B_pTs[ai % 2], B_dns[ai % 2]
                    ai += 1
                    half = (h % 2) * 64
                    kt_ = h // 2
                    t0 = (h // 2) * 3
                    rhs = qT[half:half + 64, t0:t0 + 3, c0 + c * 64:c0 + (c + 1) * 64]
                    pss = []
                    for kc in kcs:
                        kk = kcol[si] + (c - 2 + kc) * 64
                        pt, pb = nps()
                        mm(pt[0:64, 0:192], [(kTt[half:half + 64, kt_, kk:kk + 64], rhs)], [B_kT, B_q], [pb])
                        pss.append((kc, pt, pb))
                    for kc, pt, pb in pss:
                        bv = biasT[:, kc, :, 3 * h:3 * h + 3].rearrange("p i g -> p g i")
                        P.op("dve", lambda e, pt=pt, kc=kc, bv=bv, sc=sc: e.scalar_tensor_tensor(
                            out=sc[:, kc, :].rearrange("p (g i) -> p g i", g=3), in0=pt[0:64, 0:192].rearrange("p (g i) -> p g i", g=3),
                            scalar=HD ** -0.5, in1=bv, op0=ALU.mult, op1=ALU.add), [pb, B_const], [B_sc[kc]])
                    for kc, pt, pb in pss:
                        P.op("act", lambda e, kc=kc, sc=sc, pT=pT: e.activation(out=pT[:, kc, :], in_=sc[:, kc, :], func=AF.Exp), [B_sc[kc]], [B_pT[kc]])
                    po, pob = nps()
                    mm(po[:, 0:192], [(v64[:, vch[si] + c - 2 + kc, h, :], pT[:, kc, :]) for kc in kcs], [B_v64] + [B_pT[kc] for kc in kcs], [pob])
                    pd, pdb = nps()
                    mm(pd[:, 0:192], [(ones_b[0:64, :], pT[:, kc, :]) for kc in kcs], [B_const] + [B_pT[kc] for kc in kcs], [pdb])
                    P.op("dve", lambda e, pd=pd, h=h, half=half, dn=dn: e.tensor_tensor(
                        out=dn[half:half + 64, :].rearrange("p (g i) -> p g i", g=3),
                        in0=pd[half:half + 64, 0:192].rearrange("p (g i) -> p g i", g=3),
                        in1=esk[half:half + 64, 3 * h:3 * h + 3].unsqueeze(2).to_broadcast([64, 3, 64]), op=ALU.add),
                        [pdb, B_const], [B_dn])
                    P.op("dve", lambda e, half=half, dn=dn: e.reciprocal(out=dn[half:half + 64, :], in_=dn[half:half + 64, :]), [B_dn], [B_dn])
                    P.op("dve", lambda e, po=po, half=half, t0=t0, c=c, c0=c0, dn=dn: e.tensor_tensor(
                        out=oaT[half:half + 64, t0:t0 + 3, c0 + c * 64:c0 + (c + 1) * 64],
                        in0=po[half:half + 64, 0:192].rearrange("p (g i) -> p g i", g=3),
                        in1=dn[half:half + 64, :].rearrange("p (g i) -> p g i", g=3), op=ALU.mult), [pob, B_dn], [B_oa])
        if not sample and not last:
            P.op("dve", lambda e: e.tensor_copy(out=kTt[:, :, 0:128], in_=kTt[:, :, NT:NT + 128]), [B_kT], [B_kT])
            P.op("dve", lambda e: e.tensor_copy(out=v64[:, 0:2, :, :], in_=v64[:, 8:10, :, :]), [B_v64], [B_v64])

        ck(f"b{tok0 // 512 if not sample else 4}s3")
        pm_ = carve([128, 2, 512], BF16); B_pm = P.buf("pmem")
        rcp = carve([128, 512], F32); B_rcp = P.buf("rcp")
        for si, (c0, n) in enumerate(seqs):
            for hh in range(4):
                for mtile in range(2):
                    pt, pb = nps()
                    mm(pt[:, 0:n], [(mkTs[si][:, hh, mtile * 128:(mtile + 1) * 128], qmT[:, hh, c0:c0 + n])], [B_mkvs[si], B_qm], [pb])
                    P.op("act", lambda e, pt=pt, mtile=mtile, n=n: e.activation(out=pm_[:, mtile, 0:n], in_=pt[:, 0:n], func=AF.Exp,
                                                                           scale=128 ** -0.5), [pb], [B_pm])
                po, pob = nps()
                mm(po[:, 0:n], [(mvbs[si][:, mtile, hh * 128:(hh + 1) * 128], pm_[:, mtile, 0:n]) for mtile in range(2)], [B_mkvs[si], B_pm], [pob])
                pd, pdb = nps()
                mm(pd[:, 0:n], [(ones_b[:, :], pm_[:, mtile, 0:n]) for mtile in range(2)], [B_const, B_pm], [pdb])
                P.op("dve", lambda e, pd=pd, n=n: e.reciprocal(out=rcp[:, 0:n], in_=pd[:, 0:n]), [pdb], [B_rcp])
                P.op("dve", lambda e, po=po, n=n, hh=hh, c0=c0: e.tensor_tensor(out=omT[:, hh, c0:c0 + n], in0=po[:, 0:n], in1=rcp[:, 0:n],
                                                                         op=ALU.mult), [pob, B_rcp], [B_om])

        ck(f"b{tok0 // 512 if not sample else 4}s4")
        rewind(markS)
        tq2 = [[carve([128, NT], F32) for _ in range(2)] for _ in range(2)]
        rin2 = [[carve([128, NT], F32) for _ in range(2)] for _ in range(2)]
        w2 = [[carve([128, NT], F32) for _ in range(2)] for _ in range(2)]
        pq2 = tq2
        sbf2 = [[carve([128, NT], BF16) for _ in range(2)] for _ in range(2)]
        cr2 = [carve([128, 4], F32) for _ in range(2)]
        B_tq2 = [P.buf("tqa"), P.buf("tqb")]; B_rin2 = [P.buf("rina"), P.buf("rinb")]; B_pq2 = B_tq2
        B_w2 = [P.buf("w2a"), P.buf("w2b")]; B_sb2 = [P.buf("sb2a"), P.buf("sb2b")]; B_cr2 = [P.buf("cr2a"), P.buf("cr2b")]
        ns_ = NT // nseq
        if sample:
            sst_s = [carve([128, 2, NGP], F32) for _ in range(nseq)]
            B_ssts = [P.buf("ssts0"), P.buf("ssts1")]
            for s in range(nseq):
                to_state_b(sst_s[s][:, 0, :], st_re[s], B_ssts[s])
                to_state_b(sst_s[s][:, 1, :], st_im[s], B_ssts[s])
        v3 = lambda t: t.rearrange("p (s n) -> p s n", s=nseq)

        def gp_stages(tile_, g4, par):
            gp = tile_ * 4 + g4
            tab, Btab = tabs[par], B_tabs[par]
            w_, Bw = w2[par], B_w2[par]; sbf, Bsb = sbf2[par], B_sb2[par]; cr, Bcr = cr2[par], B_cr2[par]
            rin, B_rin = rin2[par], B_rin2[par]; tq, B_tq = tq2[par], B_tq2[par]
            py, pyb = psy
            cosb = tab[:, 0, 0:ns_].unsqueeze(1).to_broadcast([128, nseq, ns_])
            sinb = tab[:, 1, 0:ns_].unsqueeze(1).to_broadcast([128, nseq, ns_])
            hold = {}
            S = lambda fn, r, w: (lambda: P.op("dve", fn, r, w))
            G = lambda fn, r, w: (lambda: P.op("dve", fn, r, w))

            def st0():
                P.dma(tab[:, :, 0:ns_], rot_tab[gp].rearrange("c p n -> p c n")[:, :, 0:ns_], [B_rot], [Btab], Btab)
                hold["pr"] = psf[2 * par]; hold["pi"] = psf[2 * par + 1]
                mm(hold["pr"][0][:, 0:NT], [(Bre[:, gp, :], uT[:, tile_, 0:NT])], [B_const, B_u], [hold["pr"][1]])
                mm(hold["pi"][0][:, 0:NT], [(Bim[:, gp, :], uT[:, tile_, 0:NT])], [B_const, B_u], [hold["pi"][1]])
            stages = [[st0]]
            PR = lambda: hold["pr"][0][:, 0:NT]
            PI = lambda: hold["pi"][0][:, 0:NT]
            stages.append([lambda: P.op("dve", lambda e, x=PR(): e.tensor_tensor(out=v3(rin[0]), in0=v3(x), in1=cosb, op=ALU.mult), [hold["pr"][1], Btab, B_rin], [B_rin])])
            stages.append([lambda: P.op("dve", lambda e, x=PI(): e.tensor_tensor(out=v3(tq[0]), in0=v3(x), in1=sinb, op=ALU.mult), [hold["pi"][1], Btab, B_tq], [B_tq])])
            stages.append([lambda: P.op("dve", lambda e, x=PI(): e.tensor_tensor(out=v3(rin[1]), in0=v3(x), in1=cosb, op=ALU.mult), [hold["pi"][1], Btab, B_rin], [B_rin])])
            stages.append([lambda: P.op("dve", lambda e, x=PR(): e.tensor_tensor(out=v3(tq[1]), in0=v3(x), in1=sinb, op=ALU.mult), [hold["pr"][1], Btab, B_tq], [B_tq])])
            stages.append([S(lambda e: e.tensor_tensor(out=rin[0], in0=rin[0], in1=tq[0], op=ALU.add), [B_tq, B_rin], [B_rin])])
            stages.append([S(lambda e: e.tensor_tensor(out=rin[1], in0=rin[1], in1=tq[1], op=ALU.subtract), [B_tq, B_rin], [B_rin])])
            for si, (c0, n) in enumerate(seqs):
                stt, bst = (sst_s[si], B_ssts[si]) if sample else (sst, B_sst)
                for ri in range(2):
                    stages.append([S(lambda e, ri=ri, c0=c0, n=n, stt=stt: e.tensor_tensor_scan(
                        out=w_[ri][:, c0:c0 + n], data0=rS[:, gp:gp + 1].to_broadcast([128, n]), data1=rin[ri][:, c0:c0 + n],
                        initial=stt[:, ri, gp:gp + 1], op0=ALU.mult, op1=ALU.add), [B_rin, B_const, bst, Bw], [Bw])])
                cl = tab[:, 0, n - 1:n]; sl = tab[:, 1, n - 1:n]; e1 = c0 + n - 1
                stages.append([S(lambda e, sl=sl, e1=e1: e.tensor_scalar(out=cr[:, 2:3], in0=w_[1][:, e1:e1 + 1], scalar1=sl, scalar2=None, op0=ALU.mult), [Bw, Btab, Bcr], [Bcr])])
                stages.append([S(lambda e, sl=sl, e1=e1: e.tensor_scalar(out=cr[:, 3:4], in0=w_[0][:, e1:e1 + 1], scalar1=sl, scalar2=None, op0=ALU.mult), [Bw, Btab, Bcr], [Bcr])])
                stages.append([S(lambda e, cl=cl, e1=e1, stt=stt: e.scalar_tensor_tensor(out=stt[:, 0, gp:gp + 1], in0=w_[0][:, e1:e1 + 1], scalar=cl, in1=cr[:, 2:3],
                                                                          op0=ALU.mult, op1=ALU.subtract), [Bw, Btab, Bcr], [bst])])
                stages.append([S(lambda e, cl=cl, e1=e1, stt=stt: e.scalar_tensor_tensor(out=stt[:, 1, gp:gp + 1], in0=w_[1][:, e1:e1 + 1], scalar=cl, in1=cr[:, 3:4],
                                                                          op0=ALU.mult, op1=ALU.add), [Bw, Btab, Bcr], [bst])])
            pq, B_pq = pq2[par], B_pq2[par]
            stages.append([G(lambda e: e.tensor_tensor(out=v3(pq[0]), in0=v3(w_[0]), in1=cosb, op=ALU.mult), [Bw, Btab, B_pq], [B_pq])])
            stages.append([G(lambda e: e.tensor_tensor(out=v3(pq[1]), in0=v3(w_[1]), in1=sinb, op=ALU.mult), [Bw, Btab, B_pq], [B_pq])])
            stages.append([G(lambda e: e.tensor_tensor(out=sbf[0], in0=pq[0], in1=pq[1], op=ALU.subtract), [B_pq, Bsb], [Bsb])])
            stages.append([G(lambda e: e.tensor_tensor(out=v3(pq[0]), in0=v3(w_[0]), in1=sinb, op=ALU.mult), [Bw, Btab, B_pq], [B_pq])])
            stages.append([G(lambda e: e.tensor_tensor(out=v3(pq[1]), in0=v3(w_[1]), in1=cosb, op=ALU.mult), [Bw, Btab, B_pq], [B_pq])])
            stages.append([G(lambda e: e.tensor_tensor(out=sbf[1], in0=pq[0], in1=pq[1], op=ALU.add), [B_pq, Bsb], [Bsb])])

            def fy(e, first_mm=(g4 == 0)):
                e.matmul(py[:, 0:NT], lhsT=Cre[:, gp, :], rhs=sbf[0], start=first_mm, stop=False)
                ins = e.matmul(py[:, 0:NT], lhsT=Cim[:, gp, :], rhs=sbf[1], start=False, stop=False)
                if g4 == 3:
                    ins = e.matmul(py[:, 0:NT], lhsT=Ddiag[:, tile_, :], rhs=uT[:, tile_, 0:NT], start=False, stop=True)
                return ins
            final = lambda: P.op("pe", fy, [Bsb, B_const, B_u], [pyb])
            return stages, final

        sgst = [carve([128, CW], F32) for _ in range(4)]; B_sgst = [P.buf(f"sgst{i}") for i in range(4)]
        gsi = {"i": 0}

        gbanks = [psf[4], (psb[0][0][:, :].bitcast(F32), psb[0][1]), (psb[1][0][:, :].bitcast(F32), psb[1][1])]
        grr = {"i": 0, "ssm": True}

        def nps_gate():
            if not grr["ssm"]:
                return nps()
            grr["i"] = (grr["i"] + 1) % 3
            return gbanks[grr["i"]]

        def gate_chunk(cc, i):
            gw = wload(win_b, B_win, 0, KT, 2560 + i * D + cc * CW, CW)
            for t in range(nt):
                pt, pb = nps_gate()
                mm(pt[:, 0:CW], [(hT[:, k, t * 128:(t + 1) * 128], gw[0][:, k, 0:CW]) for k in range(KT)], [gw[1], B_hT], [pb])
                j = gsi["i"] % 4; gsi["i"] += 1
                P.op("act", lambda e, pt=pt, j=j: e.activation(out=sgst[j], in_=pt[:, 0:CW], func=AF.Sigmoid), [pb], [B_sgst[j]])
                P.dma(sgscr[cc, t, :, i, :], sgst[j], [B_sgst[j]], [B_sgscr], B_sgst[j], par=True, q="act")
        gate_list = [(cc, i) for cc in range(D // CW) for i in range(3)]
        gpos = {"i": 0}

        def gate_some(n):
            for _ in range(n):
                if gpos["i"] < len(gate_list):
                    gate_chunk(*gate_list[gpos["i"]]); gpos["i"] += 1
        for tile_ in range(6):
            for pair in range(2):
                A, fa = gp_stages(tile_, pair * 2, 0)
                Bq, fb = gp_stages(tile_, pair * 2 + 1, 1)
                for i in range(max(len(A), len(Bq))):
                    for lst in (A, Bq):
                        if i < len(lst):
                            for th_ in lst[i]:
                                th_()
                    if i == 0:
                        gate_some(2)
                fa(); fb()
            P.op("act", lambda e, tile_=tile_: e.activation(out=yT[:, tile_, :], in_=psy[0][:, 0:NT], func=AF.Gelu_apprx_tanh), [psy[1]], [B_yT])
        if sample:
            for s in range(nseq):
                from_state(sst_s[s][:, 0, :], B_ssts[s], o_sre[s]); from_state(sst_s[s][:, 1, :], B_ssts[s], o_sim[s])
        elif last:
            from_state(sst[:, 0, :], B_sst, o_pre[:, :]); from_state(sst[:, 1, :], B_sst, o_pim[:, :])
        ck(f"b{tok0 // 512 if not sample else 4}s5")
        rewind(markS)
        wt, wb = wload(wglu_b, B_wglu, 0, 6, 0, CW)
        wt2, wb2 = wload(wglu_b, B_wglu, 0, 6, 256, CW)
        wt3, wb3 = wload(wglu_b, B_wglu, 0, 6, 512, CW)
        sg = carve([128, NT], F32); B_sg = P.buf("sg")
        for ct in range(6):
            w_t, w_b = ((wt, wb), (wt2, wb2), (wt3, wb3))[ct // 2]
            pt, pb = nps()
            mm(pt[:, 0:NT], [(w_t[:, k, (ct % 2) * 128:(ct % 2) * 128 + 128], yT[:, k, 0:NT]) for k in range(6)], [w_b, B_yT], [pb])
            P.op("act", lambda e, pt=pt: e.activation(out=sg[:, 0:NT], in_=pt[:, 0:NT], func=AF.Sigmoid), [pb], [B_sg])
            P.op("dve", lambda e, ct=ct: e.tensor_tensor(out=osT[:, ct, :], in0=yT[:, ct, :], in1=sg[:, 0:NT], op=ALU.mult), [B_sg, B_yT], [B_os])

        ck(f"b{tok0 // 512 if not sample else 4}s6")
        grr["ssm"] = False
        gate_some(len(gate_list))
        sgts = [carve([128, nt, 3, CW], F32) for _ in range(2)]; B_sgts = [P.buf("sgt0"), P.buf("sgt1")]
        mchs = [carve([128, CW], F32) for _ in range(3)]; B_mchs = [P.buf(f"mch{i}") for i in range(3)]; mt2s = [carve([128, CW], F32) for _ in range(3)]
        mi = 0
        ssq = sb_keep["ssq"]; B_ssq = P.buf("ssq")
        P.op("dve", lambda e: e.memset(ssq, 0.0), [], [B_ssq])
        branch = ((oaT, B_oa, 0, 6), (osT, B_os, 6, 6), (omT, B_om, 12, 4))
        for cc in range(D // CW):
            sgt, B_sgt = sgts[cc % 2], B_sgts[cc % 2]
            P.dma(sgt, sgscr[cc, 0:nt].rearrange("t p i c -> p t i c"), [B_sgscr], [B_sgt], B_sgt)
            ow = wload(wout_b, B_wout, 0, KT, cc * CW, CW)
            for t in range(nt):
                pbr = []
                for i, (oT, Bo, k0, nk) in enumerate(branch):
                    pt, pb = nps()
                    mm(pt[:, 0:CW], [(oT[:, k, t * 128:(t + 1) * 128], ow[0][:, k0 + k, 0:CW]) for k in range(nk)], [ow[1], Bo], [pb])
                    pbr.append((pt, pb))
                mch, mt2, B_mch = mchs[mi % 3], mt2s[mi % 3], B_mchs[mi % 3]; mi += 1
                P.op("dve", lambda e, p0=pbr[0][0], t=t, sgt=sgt: e.tensor_tensor(out=mch, in0=p0[:, 0:CW], in1=sgt[:, t, 0, :], op=ALU.mult), [pbr[0][1], B_sgt], [B_mch])
                for i in (1, 2):
                    P.op("dve", lambda e, p=pbr[i][0], i=i, t=t, sgt=sgt: e.tensor_tensor(out=mt2, in0=p[:, 0:CW], in1=sgt[:, t, i, :], op=ALU.mult), [pbr[i][1], B_sgt, B_mch], [B_mch])
                    P.op("dve", lambda e: e.tensor_tensor(out=mch, in0=mch, in1=mt2, op=ALU.add), [B_mch], [B_mch])
                P.op("act", lambda e, t=t, cc=cc: e.activation(out=mt2, in_=mch, func=AF.Square, accum_out=ssq[:, t, cc:cc + 1]), [B_mch, B_ssq], [B_mch, B_ssq])
                P.dma(mixs[t * 128:(t + 1) * 128, cc * CW:(cc + 1) * CW], mch, [B_mch], [B_mixs], B_mch, par=True, q="act")
        ck(f"b{tok0 // 512 if not sample else 4}s7")
        barrier()
        hT2 = carve([128, KT, NT], BF16); B_h2 = P.buf("hT")
        actT = carve([128, FT, NT], BF16); B_act = P.buf("actT")
        ssq2 = sb_keep["ssq2"]; B_ssq2 = P.buf("ssq2")
        ssq_keep = ssq

        def post_norm_residual(t, ssq_t, B_sq, gi, res_src, B_res, out_dst, B_o, then_norm):
            P.op("dve", lambda e: e.tensor_reduce(out=stat[:, 2:3], in_=ssq_t, axis=mybir.AxisListType.X, op=ALU.add), [B_sq], [B_stat])
            rstd_from(2, 3, 128)
            P.dma(mt[:, :], mixs[t * 128:(t + 1) * 128, :], [B_mixs], [B_mt], B_mt)
            P.dma(xt[:, :], res_src, [B_res], [B_xt], B_xt)
            P.op("dve", lambda e: e.tensor_tensor(out=mt[:, :], in0=mt[:, :], in1=gbc[:, gi, :], op=ALU.mult), [B_mt, B_const], [B_mt])
            P.op("dve", lambda e: e.scalar_tensor_tensor(out=xt[:, :], in0=mt[:, :], scalar=stat[:, 3:4], in1=xt[:, :], op0=ALU.mult,
                                                         op1=ALU.add), [B_mt, B_stat, B_xt], [B_xt])
            P.dma(out_dst, xt[:, :], [B_xt], [B_o], P.buf("xst"), q="act")
            if then_norm:
                norm_T(xt, B_xt, t, hT2, B_h2, 1)
        for t in range(nt):
            post_norm_residual(t, ssq_keep[:, t, :], B_ssq, 0, x_src[tok0 + t * 128:tok0 + (t + 1) * 128, :], B_xsrc,
                               y_dst[tok0 + t * 128:tok0 + (t + 1) * 128, :], B_y, True)
        ck(f"b{tok0 // 512 if not sample else 4}s8")
        asb = carve([128, nseq, 2 + NT // nseq], F32); B_asb = P.buf("asb")
        acc = carve([128, NT], F32); B_acc = P.buf("acc")
        gl = carve([128, NT], F32); B_gl = P.buf("gl")
        mchs = [carve([128, CW], F32) for _ in range(4)]; B_mchs = [P.buf(f"mch{i}") for i in range(4)]; mt2s = [carve([128, CW], F32) for _ in range(4)]
        mi = 0
        ns = NT // nseq
        if sample:
            cv_s = carve([128, nseq, FT, 2], F32); B_cvs = P.buf("cvs")
            for s in range(nseq):
                P.dma(nat[0:88, 0:128], st_cv[s].rearrange("i (f p) -> (i f) p", p=128), [], [B_nat], B_nat)
                pt, pb = nps()
                P.op("pe", lambda e, pt=pt: e.transpose(pt[:, 0:88], nat[0:88, 0:128], ident[0:88, 0:88]), [B_nat, B_const], [pb])
                P.op("dve", lambda e, pt=pt, s=s: e.tensor_copy(out=cv_s[:, s, :, :].rearrange("p f i -> p i f"),
                                                               in_=pt[:, 0:88].rearrange("p (i f) -> p i f", i=2)), [pb], [B_cvs])
        for fg in range(FT // 2):
            wa = wload(wup_b, B_wup, 0, KT, fg * CW, CW)
            wb_ = wload(wup_b, B_wup, 0, KT, DFF + fg * CW, CW)
            for j in range(2):
                f = fg * 2 + j
                pa, pab = nps()
                mm(pa[:, 0:NT], [(wa[0][:, k, j * 128:(j + 1) * 128], hT2[:, k, 0:NT]) for k in range(KT)], [wa[1], B_h2], [pab])
                pbv, pbb = nps()
                mm(pbv[:, 0:NT], [(wb_[0][:, k, j * 128:(j + 1) * 128], hT2[:, k, 0:NT]) for k in range(KT)], [wb_[1], B_h2], [pbb])
                hist = cv_s[:, :, f, :] if sample else cvst[:, f, :].unsqueeze(1)
                bh = B_cvs if sample else B_cvst
                P.op("dve", lambda e, hist=hist: e.tensor_copy(out=asb[:, :, 0:2], in_=hist), [bh, B_asb], [B_asb])
                P.op("act", lambda e, pa=pa: e.activation(out=asb[:, :, 2:2 + ns], in_=pa[:, 0:NT].rearrange("p (s n) -> p s n", s=nseq),
                                                          func=AF.Copy), [pab, B_asb], [B_asb])
                P.op("act", lambda e, hist=hist: e.activation(out=hist, in_=asb[:, :, ns:ns + 2], func=AF.Copy), [B_asb], [bh])
                a3 = acc.rearrange("p (s n) -> p s n", s=nseq)
                P.op("dve", lambda e, f=f: e.tensor_scalar(out=a3, in0=asb[:, :, 2:2 + ns], scalar1=cw[:, 2, f:f + 1], scalar2=cw[:, 3, f:f + 1],
                                                           op0=ALU.mult, op1=ALU.add), [B_asb, B_const], [B_acc])
                P.op("dve", lambda e, f=f: e.scalar_tensor_tensor(out=a3, in0=asb[:, :, 1:1 + ns], scalar=cw[:, 1, f:f + 1], in1=a3,
                                                                  op0=ALU.mult, op1=ALU.add), [B_asb, B_const, B_acc], [B_acc])
                P.op("dve", lambda e, f=f: e.scalar_tensor_tensor(out=a3, in0=asb[:, :, 0:ns], scalar=cw[:, 0, f:f + 1], in1=a3,
                                                                  op0=ALU.mult, op1=ALU.add), [B_asb, B_const, B_acc], [B_acc])
                P.op("act", lambda e: e.activation(out=gl, in_=acc, func=AF.Gelu_apprx_tanh), [B_acc], [B_gl])
                P.op("dve", lambda e, pbv=pbv, f=f: e.tensor_tensor(out=actT[:, f, :], in0=pbv[:, 0:NT], in1=gl, op=ALU.mult), [pbb, B_gl], [B_act])
        if sample or last:
            for s in range(nseq):
                src = cv_s[:, s, :, :] if sample else cvst[:, :, :]
                bh = B_cvs if sample else B_cvst
                P.op("dve", lambda e, src=src: e.tensor_copy(out=nat[:, 0:88].rearrange("p (i f) -> p i f", i=2),
                                                             in_=src.rearrange("p f i -> p i f")), [bh, B_nat], [B_nat])
                pt, pb = nps()
                P.op("pe", lambda e, pt=pt: e.transpose(pt[0:88, 0:128], nat[:, 0:88], ident[:, :]), [B_nat, B_const], [pb])
                P.op("act", lambda e, pt=pt: e.activation(out=stg[0:88, 0:128], in_=pt[0:88, 0:128], func=AF.Copy), [pb], [B_stg])
                for i in range(2):
                    dst = (o_scv[s, i:i + 1, :] if sample else o_pcv[i:i + 1, :]).rearrange("o (f p) -> (o f) p", p=128)
                    P.dma(dst, stg[i * 44:(i + 1) * 44, 0:128], [B_stg], [], B_stg)
        ck(f"b{tok0 // 512 if not sample else 4}s9")
        P.op("dve", lambda e: e.memset(ssq2, 0.0), [], [B_ssq2])
        for cc in range(D // CW):
            wds = []
            for k0 in range(0, FT, KT):
                nk = min(KT, FT - k0)
                wds.append((k0, nk, wload(wdn_b, B_wdn, k0, nk, cc * CW, CW)))
            for t0_ in range(0, nt, 2):
                ts_ = list(range(t0_, min(nt, t0_ + 2)))
                pts = {t: nps() for t in ts_}
                for (k0, nk, wd) in wds:
                    for t in ts_:
                        def fn(e, t=t, k0=k0, nk=nk, wd=wd, pt=pts[t][0]):
                            ins = None
                            for k in range(nk):
                                ins = e.matmul(pt[:, 0:CW], lhsT=actT[:, k0 + k, t * 128:(t + 1) * 128], rhs=wd[0][:, k, 0:CW],
                                               start=(k0 + k == 0), stop=(k0 + k == FT - 1))
                            return ins
                        P.op("pe", fn, [wd[1], B_act], [pts[t][1]])
                for t in ts_:
                    mch, mt2, B_mch = mchs[mi % 4], mt2s[mi % 4], B_mchs[mi % 4]; mi += 1
                    P.op("act", lambda e, t=t, pt=pts[t][0]: e.activation(out=mch, in_=pt[:, 0:CW], func=AF.Copy), [pts[t][1]], [B_mch])
                    P.op("act", lambda e, t=t, cc=cc: e.activation(out=mt2, in_=mch, func=AF.Square, accum_out=ssq2[:, t, cc:cc + 1]), [B_mch, B_ssq2], [B_mch, B_ssq2])
                    P.dma(mixs[t * 128:(t + 1) * 128, cc * CW:(cc + 1) * CW], mch, [B_mch], [B_mixs], B_mch, par=True, q="act")
            if next_x is not None and cc % 2 == 0 and cc // 2 < next_x[2]:
                tn = cc // 2
                hnext = hT2 if next_x[3] is None else next_x[3]
                rms_and_T(next_x[0][next_x[1] + tn * 128:next_x[1] + (tn + 1) * 128, :], B_xsrc, tn, hnext, B_h2, 0)
        ck(f"b{tok0 // 512 if not sample else 4}s10")
        def s11(t):
            rows = y_dst[tok0 + t * 128:tok0 + (t + 1) * 128, :]
            post_norm_residual(t, ssq2[:, t, :], B_ssq2, 1, rows, B_y, rows, B_y, False)
        thunks = [(lambda t=t: s11(t)) for t in range(nt)]
        if defer11:
            return thunks
        for th_ in thunks:
            th_()
        return []

    def to_state_b(dst, src48x64, bdst):
        P.dma(nat[0:48, 0:64], src48x64, [], [B_nat], B_nat)
        P.dma(nat[0:48, 64:128], src48x64, [], [B_nat], B_nat)
        pt, pb = nps()
        P.op("pe", lambda e: e.transpose(pt[:, 0:48], nat[0:48, 0:128], ident[0:48, 0:48]), [B_nat, B_const], [pb])
        pv = pt[:, 0:48].rearrange("p (g two) -> p g two", two=2)
        P.op("dve", lambda e: e.tensor_copy(out=dst[0:64, :], in_=pv[0:64, :, 0]), [pb], [bdst])
        P.op("dve", lambda e: e.tensor_copy(out=dst[64:128, :], in_=pv[64:128, :, 1]), [pb], [bdst])

    def from_state(src, bsrc, dst48x64):
        nv2 = nat[:, 0:48].rearrange("p (g two) -> p g two", two=2)
        P.op("dve", lambda e: e.memset(nat[:, 0:48], 0.0), [B_nat], [B_nat])
        P.op("dve", lambda e: e.tensor_copy(out=nv2[0:64, :, 0], in_=src[0:64, :]), [bsrc, B_nat], [B_nat])
        P.op("dve", lambda e: e.tensor_copy(out=nv2[64:128, :, 1], in_=src[64:128, :]), [bsrc, B_nat], [B_nat])
        pt, pb = nps()
        P.op("pe", lambda e: e.transpose(pt[0:48, 0:128], nat[:, 0:48], ident[:, :]), [B_nat, B_const], [pb])
        P.op("dve", lambda e: e.tensor_copy(out=stg[0:48, 64:192], in_=pt[0:48, 0:128]), [pb], [B_stg])
        P.op("dve", lambda e: e.tensor_tensor(out=stg[0:48, 0:64], in0=stg[0:48, 64:128], in1=stg[0:48, 128:192], op=ALU.add), [B_stg], [B_stg])
        P.dma(dst48x64, stg[0:48, 0:64], [B_stg], [], B_stg)

    sb_keep = {"ssq": sb("ssqk", [128, 4, 8])[:], "ssq2": sb("ssqk2", [128, 4, 8])[:]}
    pm = {"mkT": sb("pmkT", [128, 4, 256], BF16), "mvb": sb("pmvb", [128, 2, 512], BF16), "B": P.buf("pmkv")}

    barrier()

    mem_kv_prompt(pm["mkT"], pm["mvb"], pm["B"])
    ck("memkv")

    pend = []
    hT_s = arena[:, 0:(KT * 128 * 2) // 4].bitcast(BF16).rearrange("p (k n) -> p k n", n=128)
    for b in range(4):
        nx = (x_p, (b + 1) * 512, 4, None) if b < 3 else (x_s, 0, 1, hT_s)
        pend = block(x_p, B_const, y_p, B_yp, b * 512, 512, [(0, 512)], b == 0, b == 3, False,
                     skip_s1=(b > 0), next_x=nx, pre11=pend, defer11=True)
        pass0["on"] = False
    block(x_s, B_const, y_s, B_ys, 0, 128, [(0, 64), (64, 64)], True, True, True, skip_s1=True, pre11=pend)

    global _P
    _P = P
    P.emit()
    es.close()
    return nc


_CACHE = {}


def _onehot():
    half, max_exact, nb = 16, 8, 32
    i = np.arange(64)[:, None, None]
    kc = np.arange(3)[None, :, None]
    j = np.arange(64)[None, None, :]
    rel = (kc * 64 + j) - 128 - i
    n = np.abs(rel)
    large = max_exact + (np.log(np.maximum(n, 1).astype(np.float32) / max_exact) / math.log(128 / max_exact) * (half - max_exact)).astype(np.int32)
    large = np.minimum(large, half - 1)
    bucket = np.where(rel > 0, half, 0) + np.where(n < max_exact, n, large)
    oh = (bucket[None] == np.arange(nb)[:, None, None, None]).astype(np.float32)
    return np.ascontiguousarray(oh.reshape(nb, 64 * 3 * 64))


def kernel(**inp):
    f = lambda a: np.ascontiguousarray(np.asarray(a, dtype=np.float32))
    if "nc" not in _CACHE:
        _CACHE["nc"] = build_program()
    nc = _CACHE["nc"]
    oh = _onehot()
    shared = {
        "rbt": f(inp["rel_bias_table"]), "oh": oh,
        "g_pm": f(inp["norm_pre_mix"]), "g_qm": f(inp["norm_post_mix"]), "g_pf": f(inp["norm_pre_ffn"]),
        "g_qf": f(inp["norm_post_ffn"]), "g_mem": f(inp["norm_mem"]),
        "w_in": f(inp["w_in"][0]), "sinks": f(inp["attn_sinks"]),
        "a_re": f(inp["ssm_a_re"][0]), "a_im": f(inp["ssm_a_im"][0]), "log_dt": f(inp["ssm_log_dt"]),
        "b_re": f(inp["ssm_b_re"][0]), "b_im": f(inp["ssm_b_im"][0]),
        "cc_re": f(inp["ssm_c_re"][0]), "cc_im": f(inp["ssm_c_im"][0]),
        "ssm_d": f(inp["ssm_d"][0].reshape(6, 128)), "w_glu": f(inp["w_glu"][0]), "w_mkv": f(inp["w_mem_kv"][0]),
        "w_out": f(inp["w_out"][0]), "w_up": f(inp["w_up"][0]), "conv_w": f(inp["conv_w"][0]),
        "conv_b": f(inp["conv_b"]), "w_down": f(inp["w_down"][0]),
    }
    in_maps = []
    for c in range(8):
        m = dict(shared)
        s = slice(2 * c, 2 * c + 2)
        m.update({
            "x_p": f(inp["x_prompt"][c]), "x_s": f(inp["x_sample"][s].reshape(128, D)),
            "c_ak": f(inp["cache_attn_k"][0, s].reshape(2, 128, 256)), "c_av": f(inp["cache_attn_v"][0, s].reshape(2, 128, 256)),
            "c_mk": f(inp["cache_mem_k"][0, s].reshape(2, 256, 512)), "c_mv": f(inp["cache_mem_v"][0, s].reshape(2, 256, 512)),
            "st_re": f(inp["state_ssm_re"][0, s]), "st_im": f(inp["state_ssm_im"][0, s]),
            "st_cv": f(inp["state_conv"][0, s]), "memp": f(inp["mem_prompt"][c]),
        })
        in_maps.append(m)
    res = run_bass_kernel_spmd(nc, in_maps, core_ids=list(range(8))).results
    g = lambda k: np.stack([np.asarray(r[k], dtype=np.float32) for r in res])
    cat = lambda k: np.concatenate([np.asarray(r[k], dtype=np.float32) for r in res], axis=0)
    return (
        g("y_p").reshape(8, 2048, D), cat("y_s").reshape(16, 64, D),
        g("o_pk").reshape(1, 8, 128, NKV, HD), g("o_pv").reshape(1, 8, 128, NKV, HD),
        g("o_pre").reshape(1, 8, 48, 64), g("o_pim").reshape(1, 8, 48, 64), g("o_pcv").reshape(1, 8, 2, DFF),
        g("o_pmk").reshape(1, 8, 256, 4, 128), g("o_pmv").reshape(1, 8, 256, 4, 128),
        cat("o_sk").reshape(1, 16, 128, NKV, HD), cat("o_sv").reshape(1, 16, 128, NKV, HD),
        cat("o_sre").reshape(1, 16, 48, 64), cat("o_sim").reshape(1, 16, 48, 64), cat("o_scv").reshape(1, 16, 2, DFF),
    )
```
